# Optimizing a Trainium2 kernel written in Bass

```python
import jax, jax.numpy as jnp
from jax import lax
import numpy as np

D_MODEL = 2048
BATCH = 1
SEQ = 16384
DEPTH = 1
DEC_BATCH = 16
DEC_SEQ = 64
PAST_LEN = 4096

CHUNK = 64
SB_HEADS = 8
SB_HEAD_DIM = 128
SB_WIDTH = SB_HEADS * SB_HEAD_DIM
SB_BLOCK = 128
HG_HEADS = 8
HG_KEY_DIM = 128
HG_VAL_DIM = 128
HG_KEY_WIDTH = HG_HEADS * HG_KEY_DIM
HG_WIDTH = HG_HEADS * HG_VAL_DIM
HG_BLOCK = 16
EPS = 1e-6
IN_WIDTHS = (SB_WIDTH, SB_WIDTH, SB_WIDTH, SB_WIDTH,
             HG_KEY_WIDTH, HG_WIDTH, HG_KEY_WIDTH, HG_WIDTH,
             D_MODEL, D_MODEL)
IN_TOTAL = sum(IN_WIDTHS)
IN_SPLITS = tuple(int(s) for s in np.cumsum(IN_WIDTHS)[:-1])

kernel_name = "stickbreaking_hgrn2_gated_streaming_step"


def _rms(x):
    xf = x.astype(jnp.float32)
    return xf * lax.rsqrt(jnp.mean(xf * xf, axis=-1, keepdims=True) + EPS)


def _layer_inputs(x, c, norm_gain, w_ada, b_ada, w_in, q_gain, k_gain, lb):
    N, T, _ = x.shape
    dt = x.dtype
    mod = jax.nn.silu(c) @ w_ada + b_ada
    shift, scale, gate = jnp.split(mod.astype(jnp.float32), 3, axis=-1)
    h = (_rms(x) * norm_gain.astype(jnp.float32) * (1.0 + scale[:, None]) + shift[:, None]).astype(dt)
    q_sb, k_sb, v_sb, z_sb, f_h, i_h, q_h, z_h, g_sb, g_h = jnp.split(h @ w_in, IN_SPLITS, axis=-1)
    sb_heads = lambda a: a.reshape(N, T, SB_HEADS, SB_HEAD_DIM)
    q_sb = (_rms(sb_heads(q_sb)) * q_gain.astype(jnp.float32)).astype(dt)
    k_sb = (_rms(sb_heads(k_sb)) * k_gain.astype(jnp.float32)).astype(dt)
    v_sb = sb_heads(v_sb)
    f = lb + (1.0 - lb) * jax.nn.sigmoid(f_h.astype(jnp.float32))
    logf = jnp.log(f).reshape(N, T, HG_HEADS, HG_KEY_DIM)
    k_h = (1.0 - f).reshape(N, T, HG_HEADS, HG_KEY_DIM)
    q_h = jax.nn.silu(q_h.astype(jnp.float32)).reshape(N, T, HG_HEADS, HG_KEY_DIM)
    i_h = i_h.astype(jnp.float32).reshape(N, T, HG_HEADS, HG_VAL_DIM)
    return q_sb, k_sb, v_sb, z_sb, q_h, k_h, i_h, logf, z_h, g_sb, g_h, gate


def _sb_attend(q, q_pos, k, v, k_pos):
    z = jnp.einsum('nqhd,nkhd->nhqk', q, k).astype(jnp.float32) * SB_HEAD_DIM ** -0.5
    causal = k_pos[None, :] < q_pos[:, None]
    log_1m = jnp.where(causal, jax.nn.log_sigmoid(-z), 0.0)
    rev = lax.cumsum(log_1m, axis=3, reverse=True) - log_1m
    w = jnp.where(causal, jnp.exp(jax.nn.log_sigmoid(z) + rev), 0.0)
    return jnp.einsum('nhqk,nkhd->nqhd', w.astype(v.dtype), v)


def _sb_prompt(q, k, v):
    N, T, H, Dh = q.shape
    nb = T // SB_BLOCK
    qb = jnp.moveaxis(q.reshape(N, nb, SB_BLOCK, H, Dh), 1, 0)
    pos = jnp.arange(T).reshape(nb, SB_BLOCK)
    k_pos = jnp.arange(T)
    o = lax.map(lambda a: _sb_attend(a[0], a[1], k, v, k_pos), (qb, pos))
    return jnp.moveaxis(o, 0, 1).reshape(N, T, H, Dh)


def _hgrn2(q, k, v, logf, s0):
    N, T = q.shape[:2]
    pad = (-T) % HG_BLOCK
    padw = ((0, 0), (0, pad), (0, 0), (0, 0))
    q, k, v, logf = (jnp.pad(a, padw) for a in (q, k, v, logf))
    nb = (T + pad) // HG_BLOCK
    blk = lambda a: a.reshape(N, nb, HG_BLOCK, *a.shape[2:])
    q, k, v, logf = blk(q), blk(k), blk(v), blk(logf)
    b = jnp.cumsum(logf, axis=2)
    b_last = b[:, :, -1:]
    q_dec = q * jnp.exp(b)
    k_inv = k * jnp.exp(-b)
    k_end = k * jnp.exp(b_last - b)
    decay = jnp.exp(b_last[:, :, 0])
    tril = jnp.tril(jnp.ones((HG_BLOCK, HG_BLOCK), dtype=bool))
    att = jnp.where(tril, jnp.einsum('ncthd,ncshd->nchts', q_dec, k_inv), 0.0)
    intra = jnp.einsum('nchts,ncshv->ncthv', att, v)

    def step(S, xs):
        qd, ke, vv, dec = xs
        inter = jnp.einsum('nthd,nhdv->nthv', qd, S)
        S = S * dec[..., None] + jnp.einsum('nthd,nthv->nhdv', ke, vv)
        return S, inter

    s_fin, inter = lax.scan(step, s0, (jnp.moveaxis(q_dec, 1, 0), jnp.moveaxis(k_end, 1, 0),
                                       jnp.moveaxis(v, 1, 0), jnp.moveaxis(decay, 1, 0)))
    o = intra + jnp.moveaxis(inter, 0, 1)
    return o.reshape(N, nb * HG_BLOCK, *o.shape[3:])[:, :T], s_fin


def _layer_output(x, o_sb, o_h, z_sb, z_h, g_sb, g_h, gate, onorm_gain, w_br_sb, w_br_hg, w_out):
    N, T, _ = x.shape
    dt = x.dtype
    y_sb = (o_sb.reshape(N, T, SB_WIDTH).astype(jnp.float32) * jax.nn.silu(z_sb.astype(jnp.float32))).astype(dt) @ w_br_sb
    o_h = (_rms(o_h) * onorm_gain.astype(jnp.float32)).reshape(N, T, HG_WIDTH)
    y_h = (o_h * jax.nn.silu(z_h.astype(jnp.float32))).astype(dt) @ w_br_hg
    merged = jax.nn.sigmoid(g_sb) * y_sb + jax.nn.sigmoid(g_h) * y_h
    return (x.astype(jnp.float32) + gate[:, None] * (merged @ w_out).astype(jnp.float32)).astype(dt)


def setup_inputs(seed: int = 0) -> dict:
    key = jax.random.key(seed)
    ks = jax.random.split(key, 20)
    nrm = lambda k, shape: jax.random.normal(k, shape, jnp.float32)
    return {
        "x_prompt": nrm(ks[0], (BATCH, SEQ, D_MODEL)),
        "x_sample": nrm(ks[1], (DEC_BATCH, DEC_SEQ, D_MODEL)),
        "cache_sb_k": nrm(ks[2], (DEPTH, DEC_BATCH, PAST_LEN, SB_HEADS, SB_HEAD_DIM)),
        "cache_sb_v": nrm(ks[3], (DEPTH, DEC_BATCH, PAST_LEN, SB_HEADS, SB_HEAD_DIM)),
        "state_hgrn": 0.5 * nrm(ks[4], (DEPTH, DEC_BATCH, HG_HEADS, HG_KEY_DIM, HG_VAL_DIM)),
        "c_prompt": nrm(ks[5], (BATCH, D_MODEL)),
        "c_sample": nrm(ks[6], (DEC_BATCH, D_MODEL)),
        "norm_gain": 1.0 + 0.02 * nrm(ks[7], (DEPTH, D_MODEL)),
        "w_ada": 0.5 * D_MODEL ** -0.5 * nrm(ks[8], (DEPTH, D_MODEL, 3 * D_MODEL)),
        "b_ada": 0.02 * nrm(ks[9], (DEPTH, 3 * D_MODEL)),
        "w_in": D_MODEL ** -0.5 * nrm(ks[10], (DEPTH, D_MODEL, IN_TOTAL)),
        "q_norm_gain": 1.0 + 0.02 * nrm(ks[11], (DEPTH, SB_HEAD_DIM)),
        "k_norm_gain": 1.0 + 0.02 * nrm(ks[12], (DEPTH, SB_HEAD_DIM)),
        "hgrn_lb_raw": 0.1 * nrm(ks[13], (DEPTH + 1, HG_KEY_WIDTH)),
        "hgrn_onorm_gain": 1.0 + 0.02 * nrm(ks[14], (DEPTH, HG_HEADS, HG_VAL_DIM)),
        "w_branch_sb": SB_WIDTH ** -0.5 * nrm(ks[15], (DEPTH, SB_WIDTH, D_MODEL)),
        "w_branch_hgrn": HG_WIDTH ** -0.5 * nrm(ks[16], (DEPTH, HG_WIDTH, D_MODEL)),
        "w_out": D_MODEL ** -0.5 * nrm(ks[17], (DEPTH, D_MODEL, D_MODEL)),
    }


def reference(x_prompt, x_sample, cache_sb_k, cache_sb_v, state_hgrn, c_prompt, c_sample,
              norm_gain, w_ada, b_ada, w_in, q_norm_gain, k_norm_gain, hgrn_lb_raw,
              hgrn_onorm_gain, w_branch_sb, w_branch_hgrn, w_out):
    lbs = jnp.cumsum(jax.nn.softmax(hgrn_lb_raw.astype(jnp.float32), axis=0), axis=0)
    xp, xs = x_prompt, x_sample
    kp, vp, sp, ksm, vsm, ssm = [], [], [], [], [], []
    n_p, t_p = xp.shape[:2]
    t_s = xs.shape[1]
    past = cache_sb_k.shape[2]
    for l in range(DEPTH):
        w_i = (norm_gain[l], w_ada[l], b_ada[l], w_in[l], q_norm_gain[l], k_norm_gain[l], lbs[l])
        w_o = (hgrn_onorm_gain[l], w_branch_sb[l], w_branch_hgrn[l], w_out[l])
        q_sb, k_sb, v_sb, z_sb, q_h, k_h, i_h, logf, z_h, g_sb, g_h, gate = _layer_inputs(xp, c_prompt, *w_i)
        o_sb = _sb_prompt(q_sb, k_sb, v_sb)
        s0 = jnp.zeros((n_p, HG_HEADS, HG_KEY_DIM, HG_VAL_DIM), jnp.float32)
        o_h, s_p = _hgrn2(q_h, k_h, i_h, logf, s0)
        xp = _layer_output(xp, o_sb, o_h, z_sb, z_h, g_sb, g_h, gate, *w_o)
        kp.append(k_sb)
        vp.append(v_sb)
        sp.append(s_p.astype(x_prompt.dtype))
        q_sb, k_sb, v_sb, z_sb, q_h, k_h, i_h, logf, z_h, g_sb, g_h, gate = _layer_inputs(xs, c_sample, *w_i)
        k_all = jnp.concatenate([cache_sb_k[l].astype(k_sb.dtype), k_sb], axis=1)
        v_all = jnp.concatenate([cache_sb_v[l].astype(v_sb.dtype), v_sb], axis=1)
        q_pos = past + jnp.arange(t_s)
        k_pos = jnp.arange(past + t_s)
        o_sb = _sb_attend(q_sb, q_pos, k_all, v_all, k_pos)
        o_h, s_s = _hgrn2(q_h, k_h, i_h, logf, state_hgrn[l].astype(jnp.float32))
        xs = _layer_output(xs, o_sb, o_h, z_sb, z_h, g_sb, g_h, gate, *w_o)
        ksm.append(k_sb)
        vsm.append(v_sb)
        ssm.append(s_s.astype(state_hgrn.dtype))
    new_k_prompt = jnp.stack(kp)
    new_v_prompt = jnp.stack(vp)
    new_s_prompt = jnp.stack(sp)
    new_k_sample = jnp.stack(ksm)
    new_v_sample = jnp.stack(vsm)
    new_s_sample = jnp.stack(ssm)
    return (xp, xs, new_k_prompt, new_v_prompt, new_s_prompt, new_k_sample, new_v_sample, new_s_sample)
```

```python
import numpy as np
from contextlib import ExitStack
import concourse.bass as bass
import concourse.mybir as mybir
from concourse.bass_utils import run_bass_kernel_spmd

F32 = mybir.dt.float32
BF16 = mybir.dt.bfloat16
AF = mybir.ActivationFunctionType
ALU = mybir.AluOpType

NCORES = 8
D = 2048
KC = D // 128
TP = 16384
NB = 16
TSQ = 64
TS = NB * TSQ
TT = TP + TS
CH = 512
NCH_P = TP // CH
NCH = TT // CH
PAST = 4096
EPS = 1e-6
SEM_CHUNK = 16000
NSLOT = 8
OWN_P = TP // NCORES
OWN_S = TS // NCORES
OWN = OWN_P + OWN_S


class Buf:
    __slots__ = ("name", "w", "r", "psum")

    def __init__(self, name, psum=False):
        self.name = name
        self.w = None
        self.r = {}
        self.psum = psum


class Q:
    def __init__(self, name):
        self.name = name
        self.ops = []
        self.n = 0
        self.waited = {}
        self.maxchunk = {}
        self.dma_k = 0
        self.slot_tot = [0] * NSLOT


class Prog:
    def __init__(self):
        self.q = {n: Q(n) for n in ("pe", "act", "dve", "pool", "sp")}
        self.keys = []
        self.keyset = set()

    def _key(self, key):
        if key not in self.keyset:
            self.keyset.add(key)
            self.keys.append(key)

    def _deps(self, q, reads, writes, extra):
        deps = {}

        def add(t):
            if t is None:
                return
            k, v = t
            if deps.get(k, 0) < v:
                deps[k] = v

        for b in reads:
            add(b.w)
            if b.psum:
                for rk, t in b.r.items():
                    if rk != q.name:
                        add(t)
        for b in writes:
            add(b.w)
            for t in b.r.values():
                add(t)
        for t in extra:
            add(t)
        waits = []
        for key, val in deps.items():
            if q.name == "pe" and key[0] == "pe":
                continue
            if isinstance(key[1], int) and key[0] in ("pe", "act", "dve", "pool", "sp"):
                mc = q.maxchunk.get(key[0], -1)
                if key[1] < mc:
                    continue
                if key[1] > mc:
                    q.maxchunk[key[0]] = key[1]
            if q.waited.get(key, 0) >= val:
                continue
            q.waited[key] = val
            waits.append((key, val))
        return waits

    def _mark(self, tok, reads, writes, rkey):
        for b in writes:
            b.w = tok
            b.r = {}
        for b in reads:
            b.r[rkey] = tok

    def op(self, qn, meth, kw, reads=(), writes=(), extra=()):
        fn = (meth, kw)
        q = self.q[qn]
        waits = self._deps(q, reads, writes, extra)
        idx = q.n
        q.n += 1
        key = (qn, idx // SEM_CHUNK)
        self._key(key)
        tok = (key, idx % SEM_CHUNK + 1)
        q.ops.append((waits, fn, (key, 1)))
        self._mark(tok, reads, writes, qn)
        return tok

    def dma(self, qn, kw, reads=(), writes=(), extra=(), meth="dma_start"):
        fn = (meth, kw)
        q = self.q[qn]
        slot = q.dma_k % NSLOT
        q.dma_k += 1
        key = (qn + "_dma", slot)
        self._key(key)
        prev = q.slot_tot[slot]
        ex = list(extra)
        if prev > 0:
            ex.append((key, prev))
        waits = self._deps(q, reads, writes, ex)
        q.slot_tot[slot] = prev + 16
        tok = (key, prev + 16)
        q.ops.append((waits, fn, (key, 16)))
        self._mark(tok, reads, writes, key)
        return tok

    def barrier(self):
        toks = []
        for q in self.q.values():
            for sl in range(NSLOT):
                if q.slot_tot[sl] > 0:
                    toks.append(((q.name + "_dma", sl), q.slot_tot[sl]))
            if q.n > 0:
                idx = q.n - 1
                toks.append(((q.name, idx // SEM_CHUNK), idx % SEM_CHUNK + 1))
        for q in self.q.values():
            waits = self._deps(q, (), (), toks)
            if waits:
                q.ops.append((waits, None, None))

    def finish(self):
        ex = []
        for q in self.q.values():
            for s in range(NSLOT):
                if q.slot_tot[s] > 0:
                    ex.append(((q.name + "_dma", s), q.slot_tot[s]))
            if q.n > 0 and q.name != "sp":
                idx = q.n - 1
                ex.append(((q.name, idx // SEM_CHUNK), idx % SEM_CHUNK + 1))
        q = self.q["sp"]
        waits = self._deps(q, (), (), ex)
        q.ops.append((waits, None, None))

    def emit(self, nc, es):
        sems = {}
        for key in self.keys:
            sems[key] = es.enter_context(nc.semaphore("s_%s_%s" % (key[0], key[1])))
        with nc.Block() as block:
            self._emit_block(block, sems)

    def _emit_block(self, block, sems):

        def run(q):
            def body(e):
                for waits, fn, inc in q.ops:
                    for key, val in waits:
                        e.wait_ge(sems[key], val)
                    if fn is None:
                        continue
                    ins = getattr(e, fn[0])(**fn[1])
                    ins.then_inc(sems[inc[0]], inc[1])
            return body

        block.tensor(run(self.q["pe"]))
        block.scalar(run(self.q["act"]))
        block.vector(run(self.q["dve"]))
        block.gpsimd(run(self.q["pool"]))
        block.sync(run(self.q["sp"]))


def build(mode="mix", chunks=None, x_rows=TT, stop=None, x_base=0, flags=("att", "hg"), debug=False):
    chunks = list(range(NCH)) if chunks is None else chunks
    if mode == "out":
        chunks = []
    do_mix = mode in ("mix", "fused")
    do_out = mode in ("out", "fused")
    NR = {"mix": 17, "out": 3, "fused": 19}[mode]
    RS0 = NR - 2
    if do_mix:
        debug = True
    nc = bass.Bass("TRN2", target_bir_lowering=False)
    es = ExitStack()
    P = Prog()

    def dram_in(name, shape, dt=F32):
        return nc.dram_tensor(name, list(shape), dt, kind="ExternalInput").ap()

    def dram_out(name, shape, dt=F32):
        return nc.dram_tensor(name, list(shape), dt, kind="ExternalOutput").ap()

    def sb(name, shape, dt):
        return es.enter_context(nc.sbuf_tensor(name, list(shape), dt))

    c_all = dram_in("c_all", [NR, D])
    w_ada = dram_in("w_ada", [D, 3 * D])
    b_adaT = dram_in("b_adaT", [128, 48])
    ngT = dram_in("ngT", [128, KC])
    identf_d = dram_in("identf", [128, 128])
    if do_mix:
        NH = 8 if mode == "fused" else 1
        HD = {"h": 0}
        x_all = dram_in("x_all", [x_rows, D])
        w_head = dram_in("w_head", [NH, D, 1024])
        qkg = dram_in("qkg", [128, 2])
        k_out = dram_out("k_out", [NH, TT, 128])
        v_out = dram_out("v_out", [NH, TT, 128])
        s_out = dram_out("s_out", [NH, 17, 128, 128])
        trin_d = dram_in("trin", [128, 128])
        tric_d = dram_in("tric", [128, 128])
        maskp_d = dram_in("maskp", [128, 4, 512])
        masks_d = dram_in("masks", [128, 2, 64])
        hmask_d = dram_in("hmask", [128, 128])
        resetm_d = dram_in("resetm", [128, 512])
        lbr_d = dram_in("lbr", [NH, 128, 2])
        ogain_d = dram_in("ogain", [NH, 128, 1])
        pmask_d = dram_in("pmask", [128, 2])
        if mode == "fused":
            cache_k = dram_in("cache_k", [NH, 2, PAST, 128])
            cache_v = dram_in("cache_v", [NH, 2, PAST, 128])
            vis_d = dram_in("vis", [128, 512])
            a_own_scr = nc.dram_tensor("a_own_scr", [8, 17, 128, 128], BF16)
        else:
            cache_k = dram_in("cache_k", [NH, NB, PAST, 128])
            cache_v = dram_in("cache_v", [NH, NB, PAST, 128])
        s_in = dram_in("s_in", [NH, NB, 128, 128])
        if mode == "mix":
            dbg_a = dram_out("dbg_a", [128, TT])
            dbg_b = dram_out("dbg_b", [128, TT])
        else:
            ab_scr = nc.dram_tensor("ab_scr", [2 * 8 * 136 * 128, 128], BF16)
            idx_ab_d = nc.dram_tensor("idx_ab", [128, 272], mybir.dt.int32, kind="ExternalInput").ap()
    if do_out:
        x_own = dram_in("x_own", [OWN, D])
        w_gate = dram_in("w_gate", [D, 2 * D])
        w_bsb = dram_in("w_bsb", [1024, D])
        w_bhg = dram_in("w_bhg", [1024, D])
        w_o = dram_in("w_o", [D, D])
        b_gate_bc = dram_in("b_gate_bc", [128, D])
        y_own = dram_out("y_own", [OWN, D])
        if mode == "out":
            ab_own = dram_in("ab_own", [2, 8, 128, OWN])

    es1 = ExitStack()

    def sb1(name, shape, dt):
        return es1.enter_context(nc.sbuf_tensor(name, list(shape), dt))

    ident_f = sb("ident_f", [128, 128], F32)
    ident_b = sb("ident_b", [128, 128], BF16)
    ones_b = sb("ones_b", [128, 128], BF16)
    csT = sb("csT", [128, KC, NR], BF16)
    gm = sb("gm", [128, KC, NR], F32)
    sh = sb("sh", [128, KC, NR], F32)
    badaT = sb("badaT", [128, 48], F32)
    ngTs = sb("ngTs", [128, KC], F32)
    epsb = sb("epsb", [128, 1], F32)
    xbuf = [sb("xbuf%d" % i, [128, D], F32) for i in range(2)]
    xn = sb("xn", [128, D], BF16)
    junk = xn
    stat = [sb("stat%d" % i, [128, 4], F32) for i in range(2)]
    hT = sb("hT", [128, KC, CH], BF16)
    if do_mix:
        wh = sb1("wh", [128, KC, 1024], BF16)
        qkgs = sb1("qkgs", [128, 2], F32)
        qkgs2 = sb1("qkgs2", [128, 2], F32)
        sq_b = sb1("sq_b", [128, CH], BF16)
        rstd_f = sb1("rstd_f", [128, CH], F32)
        kn_f = sb1("kn_f", [128, CH], F32)
        vT_f = sb1("vT_f", [128, CH], F32)
        KT = sb1("KT", [128, TP], BF16)
        Vr = sb1("Vr", [128, TP // 128, 128], BF16)
        QT = sb1("QT", [128, CH], BF16)
    es0 = ExitStack()
    c_sb = es0.enter_context(nc.sbuf_tensor("c_sb", [NR, D], F32))
    cs_sb = es0.enter_context(nc.sbuf_tensor("cs_sb", [NR, D], F32))
    wada = [es0.enter_context(nc.sbuf_tensor("wada%d" % i, [128, KC, 512], BF16)) for i in range(2)]
    modsb = es0.enter_context(nc.sbuf_tensor("modsb", [128, 48, NR], F32))

    ps = [es.enter_context(nc.psum_tensor("ps%d" % i, [128, 512], F32)) for i in range(8)]
    psb = [Buf("ps%d" % i, psum=True) for i in range(8)]

    B = {}

    def bf(name):
        if name not in B:
            B[name] = Buf(name)
        return B[name]

    def done():
        P.finish()
        P.emit(nc, es)
        es.close()
        return nc
    P.dma("sp", dict(out=ident_f[:], in_=identf_d[:, :]), writes=[bf("ident_f")])
    P.dma("sp", dict(out=badaT[:], in_=b_adaT[:, :]), writes=[bf("badaT")])
    P.dma("sp", dict(out=ngTs[:], in_=ngT[:, :]), writes=[bf("ngTs")])
    P.dma("sp", dict(out=c_sb[:], in_=c_all[:, :]), writes=[bf("c_sb")])
    P.op("dve", "tensor_copy", dict(out=ident_b[:], in_=ident_f[:]), reads=[bf("ident_f")], writes=[bf("ident_b")])
    P.op("dve", "memset", dict(ap=ones_b[:], constant=1.0), writes=[bf("ones_b")])
    P.op("dve", "memset", dict(ap=epsb[:], constant=EPS), writes=[bf("epsb")])
    if do_mix:
        P.dma("sp", dict(out=qkgs[:], in_=qkg[:, :]), writes=[bf("qkgs")])
        P.op("dve", "tensor_scalar", dict(out=qkgs2[:, 0:1], in0=qkgs[:, 0:1], scalar1=float(128 ** -0.5),
                                          scalar2=None, op0=ALU.mult), reads=[bf("qkgs")], writes=[bf("qkgs2a")])
        P.op("dve", "tensor_copy", dict(out=qkgs2[:, 1:2], in_=qkgs[:, 1:2]), reads=[bf("qkgs")], writes=[bf("qkgs2b")])

    P.op("act", "activation", dict(out=cs_sb[:], in_=c_sb[:], func=AF.Silu), reads=[bf("c_sb")], writes=[bf("cs_sb")])
    pT = ps[0]
    for k in range(KC):
        P.op("pe", "transpose", dict(out=pT[0:128, k * NR:(k + 1) * NR], in_=cs_sb[:, k * 128:(k + 1) * 128],
                                     identity=ident_f[0:NR, 0:NR]),
             reads=[bf("cs_sb"), bf("ident_f")], writes=[psb[0]])
    P.op("dve", "tensor_copy", dict(out=csT[:].rearrange("p k r -> p (k r)"), in_=pT[:, 0:KC * NR]),
         reads=[psb[0]], writes=[bf("csT")])
    modps = [ps[1], ps[2]]
    for g in range(12):
        wb = wada[g % 2]
        wbuf = bf("wada%d" % (g % 2))
        P.dma("pool", dict(out=wb[:], in_=w_ada[:, g * 512:(g + 1) * 512].rearrange("(k p) c -> p k c", p=128)),
              writes=[wbuf])
        for cc in range(4):
            j = g * 4 + cc
            dst = modps[j // 24]
            dbuf = psb[1 + j // 24]
            jj = j % 24
            for k in range(KC):
                P.op("pe", "matmul", dict(out=dst[:, jj * NR:(jj + 1) * NR], lhsT=wb[:, k, cc * 128:(cc + 1) * 128],
                                          rhs=csT[:, k, :], start=(k == 0), stop=(k == KC - 1)),
                     reads=[wbuf, bf("csT")], writes=[dbuf])
    for half in range(2):
        P.op("dve", "tensor_tensor", dict(
            out=modsb[:, half * 24:(half + 1) * 24, :],
            in0=modps[half][:, 0:24 * NR].rearrange("p (j r) -> p j r", r=NR),
            in1=badaT[:, half * 24:(half + 1) * 24].unsqueeze(2).to_broadcast([128, 24, NR]), op=ALU.add),
            reads=[psb[1 + half], bf("badaT")], writes=[bf("modsb%d" % half)])
    P.op("dve", "tensor_copy", dict(out=sh[:], in_=modsb[:, 0:16, :]), reads=[bf("modsb0")], writes=[bf("sh")])
    P.op("dve", "tensor_scalar", dict(out=gm[:], in0=modsb[:, 16:32, :], scalar1=1.0, scalar2=None, op0=ALU.add),
         reads=[bf("modsb0"), bf("modsb1")], writes=[bf("gm")])
    P.op("dve", "tensor_tensor", dict(out=gm[:], in0=gm[:], in1=ngTs[:].unsqueeze(2).to_broadcast([128, KC, NR]),
                                      op=ALU.mult), reads=[bf("gm"), bf("ngTs")], writes=[bf("gm")])

    if stop == "p0":
        return done()
    P.barrier()
    es0.close()
    if do_mix:
        trin = sb1("trin_s", [128, 128], BF16)
        tric = sb1("tric_s", [128, 128], BF16)
        maskp = sb1("maskp_s", [128, 4, 512], BF16)
        masks = sb1("masks_s", [128, 2, 64], BF16)
        hmask = sb1("hmask_s", [128, 128], F32)
        resetm = sb1("resetm_s", [128, 512], F32)
        lbr = sb1("lbr_s", [128, 4], F32)
        ogain = sb1("ogain_s", [128, 1], F32)
        oneb = sb1("oneb", [128, 1], F32)
        zs_sb = sb1("zs_sb", [128, CH], F32)
        e_sb = [sq_b, sb1("e_sb1", [128, CH], BF16)]
        sp_sb = [sb1("sp_sb%d" % i, [128, CH], BF16) for i in range(2)]
        x_sb = [sb1("x_sb%d" % i, [128, CH], BF16) for i in range(2)]
        w_sb = [sb1("w_sb%d" % i, [128, CH], BF16) for i in range(2)]
        aT = sb1("aT", [128, CH], BF16)
        bT = sb1("bT", [128, CH], BF16)
        ksT = sb1("ksT", [128, CH], BF16)
        vs_bf = sb1("vs_bf", [128, 4, 128], BF16)
        t_f = sb1("t_f", [128, CH], F32)
        t_k = sb1("t_k", [128, CH], F32)
        t_lf = sb1("t_lf", [128, CH], F32)
        t_b = sb1("t_b", [128, CH], F32)
        t_eb = sb1("t_eb", [128, CH], F32)
        t_enb = sb1("t_enb", [128, CH], F32)
        t_qs = sb1("t_qs", [128, CH], F32)
        t_ke = sb1("t_ke", [128, CH], F32)
        kv_o = [t_ke[:].rearrange("p (t d) -> p t d", d=128)]
        t_iT = sb1("t_iT", [128, CH], F32)
        t_zh = sb1("t_zh", [128, CH], F32)
        dec = sb1("dec", [128, 8], F32)
        iv = sb1("iv", [128, 4, 128], F32)
        ket = sb1("ket", [128, 2, 4, 128], F32)
        pmask = sb1("pmask_s", [128, 2], F32)
        if mode == "fused":
            vis = sb1("vis_s", [128, 512], F32)
        attm = sb1("attm", [128, 4, 128], F32)
        Sset = [sb1("Sset%d" % i, [128, 8, 128], F32) for i in range(2)]
        sin = sb1("sin", [128, 8, 128], F32)
        dbg_t = t_lf
        P.dma("pool", dict(out=trin[:], in_=trin_d[:, :]), writes=[bf("trin")])
        P.dma("pool", dict(out=tric[:], in_=tric_d[:, :]), writes=[bf("tric")])
        P.dma("pool", dict(out=maskp[:], in_=maskp_d[:, :, :]), writes=[bf("maskp")])
        P.dma("pool", dict(out=masks[:], in_=masks_d[:, :, :]), writes=[bf("masks")])
        P.dma("sp", dict(out=hmask[:], in_=hmask_d[:, :]), writes=[bf("hmask")])
        P.dma("sp", dict(out=resetm[:], in_=resetm_d[:, :]), writes=[bf("resetm")])
        P.dma("sp", dict(out=pmask[:], in_=pmask_d[:, :]), writes=[bf("pmask")])
        if mode == "fused":
            P.dma("sp", dict(out=vis[:], in_=vis_d[:, :]), writes=[bf("vis")])
        P.op("dve", "memset", dict(ap=oneb[:], constant=1.0), writes=[bf("oneb")])

        def head_setup():
            P.barrier()
            P.dma("pool", dict(out=wh[:], in_=w_head[HD["h"]].rearrange("(k p) c -> p k c", p=128)), writes=[bf("wh")])
            P.dma("sp", dict(out=lbr[:, 0:2], in_=lbr_d[HD["h"]]), writes=[bf("lbr")])
            P.dma("sp", dict(out=ogain[:], in_=ogain_d[HD["h"]]), writes=[bf("ogain")])
            P.op("dve", "memset", dict(ap=Sset[(mixers.n + 1) % 2][:, 7, :], constant=0.0), writes=[bf("S_%d_7" % ((mixers.n + 1) % 2))])
            P.op("dve", "tensor_tensor", dict(out=lbr[:, 2:3], in0=lbr[:, 0:1], in1=lbr[:, 1:2], op=ALU.subtract),
                 reads=[bf("lbr")], writes=[bf("lbr")])
            P.op("act", "activation", dict(out=lbr[:, 2:3], in_=lbr[:, 2:3], func=AF.Sigmoid), reads=[bf("lbr")], writes=[bf("lbr")])
            P.op("dve", "tensor_scalar", dict(out=lbr[:, 3:4], in0=lbr[:, 2:3], scalar1=-1.0, scalar2=1.0, op0=ALU.mult, op1=ALU.add),
                 reads=[bf("lbr")], writes=[bf("lbr")])
            sample_ready["done"] = False
    state = {"tile_i": 0, "tro": 0}

    def proj(j, dst, dbuf, hTb):
        for k in range(KC):
            P.op("pe", "matmul", dict(out=dst[:, :], lhsT=wh[:, k, j * 128:(j + 1) * 128], rhs=hT[:, k, :],
                                      start=(k == 0), stop=(k == KC - 1)),
                 reads=[bf("wh")] + hTb[k], writes=[dbuf])

    def headnorm(src, sbuf_, gain_ap, gbufs, out_ap, outbuf):
        P.op("act", "activation", dict(out=sq_b[:], in_=src[:, :], func=AF.Square), reads=[sbuf_], writes=[bf("e_sb0")])
        P.op("pe", "matmul", dict(out=ps[4][:, :], lhsT=ones_b[:], rhs=sq_b[:], start=True, stop=True),
             reads=[bf("ones_b"), bf("e_sb0")], writes=[psb[4]])
        P.op("act", "activation", dict(out=rstd_f[:], in_=ps[4][:, :], func=AF.Ln, scale=1.0 / 128, bias=epsb[:, 0:1]),
             reads=[psb[4], bf("epsb")], writes=[bf("rstd_f")])
        P.op("act", "activation", dict(out=rstd_f[:], in_=rstd_f[:], func=AF.Exp, scale=-0.5),
             reads=[bf("rstd_f")], writes=[bf("rstd_f")])
        P.op("dve", "scalar_tensor_tensor", dict(out=out_ap, in0=src[:, :], scalar=gain_ap, in1=rstd_f[:],
                                                 op0=ALU.mult, op1=ALU.mult),
             reads=[sbuf_, bf("rstd_f")] + gbufs, writes=[outbuf])

    def attention(nq, q_ap, qbufs, blocks, o_ap, c0):
        n = len(blocks)
        zb = [ps[0], ps[1]]
        A = ps[6]
        for s_ in range(n + 2):
            if 1 <= s_ <= n:
                j = s_ - 1
                P.op("pe", "matmul", dict(out=A[:, 0:nq], lhsT=trin[:], rhs=sp_sb[j % 2][:, 0:nq], start=(j == 0), stop=True, skip_group_check=(j > 0)),
                     reads=[bf("trin"), bf("sp_sb%d" % (j % 2))], writes=[psb[6]])
                P.op("act", "activation", dict(out=x_sb[j % 2][:, 0:nq], in_=A[:, 0:nq], func=AF.Exp),
                     reads=[psb[6]], writes=[bf("x_sb%d" % (j % 2))])
            if s_ < n:
                blk = blocks[s_]
                z = zb[s_ % 2]
                zbuf = psb[s_ % 2]
                has_mask = blk.get("mask") is not None
                P.op("pe", "matmul", dict(out=z[:, 0:nq], lhsT=blk["kT"], rhs=q_ap, start=True, stop=not has_mask),
                     reads=blk["kbufs"] + qbufs, writes=[zbuf])
                if has_mask:
                    P.op("pe", "matmul", dict(out=z[:, 0:nq], lhsT=ident_b[:], rhs=blk["mask"], start=False, stop=True),
                         reads=[bf("ident_b")] + blk["mbufs"], writes=[zbuf])
                if blk.get("bias") is not None:
                    P.op("act", "activation", dict(out=e_sb[s_ % 2][:, 0:nq], in_=z[:, 0:nq], func=AF.Exp, bias=blk["bias"]),
                         reads=[zbuf, bf("vis")], writes=[bf("e_sb%d" % (s_ % 2))])
                else:
                    P.op("act", "activation", dict(out=e_sb[s_ % 2][:, 0:nq], in_=z[:, 0:nq], func=AF.Exp),
                         reads=[zbuf], writes=[bf("e_sb%d" % (s_ % 2))])
                P.op("act", "activation", dict(out=sp_sb[s_ % 2][:, 0:nq], in_=e_sb[s_ % 2][:, 0:nq], func=AF.Ln, bias=oneb[:, 0:1]),
                     reads=[bf("e_sb%d" % (s_ % 2)), bf("oneb")], writes=[bf("sp_sb%d" % (s_ % 2))])
            if s_ >= 2:
                j = s_ - 2
                blk = blocks[j]
                P.op("pe", "matmul", dict(out=o_ap, lhsT=blk["v"], rhs=w_sb[j % 2][:, 0:nq], start=(j == 0), stop=(j == n - 1)),
                     reads=blk["vbufs"] + [bf("w_sb%d" % (j % 2))], writes=[psb[7]])
            if 1 <= s_ <= n:
                j = s_ - 1
                if j < n - 1:
                    P.op("pe", "matmul", dict(out=A[:, 0:nq], lhsT=tric[:], rhs=sp_sb[j % 2][:, 0:nq], start=False, stop=True, skip_group_check=True),
                         reads=[bf("tric"), bf("sp_sb%d" % (j % 2))], writes=[psb[6]])
                P.op("dve", "tensor_tensor", dict(out=w_sb[j % 2][:, 0:nq], in0=e_sb[j % 2][:, 0:nq], in1=x_sb[j % 2][:, 0:nq], op=ALU.mult),
                     reads=[bf("e_sb%d" % (j % 2)), bf("x_sb%d" % (j % 2))], writes=[bf("w_sb%d" % (j % 2))])

    def tr_out(ch, srcT, srcbuf, dram, extra_copy=None):
        pt = ps[5]
        for tt in range(4):
            P.op("pe", "transpose", dict(out=pt[:, tt * 128:(tt + 1) * 128], in_=srcT[:, tt * 128:(tt + 1) * 128],
                                         identity=ident_f[:]),
                 reads=[srcbuf, bf("ident_f")], writes=[psb[5]])
        i = 0
        ko = kv_o[i]
        kob = bf("t_ke")
        P.op("act", "activation", dict(out=ko[:].rearrange("p t d -> p (t d)"), in_=pt[:, :], func=AF.Copy),
             reads=[psb[5]], writes=[kob])
        if extra_copy is not None:
            extra_copy(pt, psb[5])
        P.dma("sp", dict(out=dram[ch * CH:(ch + 1) * CH, :].rearrange("(t p) d -> p t d", p=128), in_=ko[:]),
              reads=[kob])

    sample_ready = {"done": False}

    def emit_ab(ch, which, src_bf, srcbuf):
        if mode != "fused":
            return
        r0 = ((which * 8 + HD["h"]) * 136 + ch * 4) * 128
        P.dma("sp", dict(out=ab_scr.ap()[r0:r0 + 512, :].rearrange("(t p) c -> p t c", p=128),
                         in_=src_bf[:].rearrange("p (t c) -> p t c", c=128)), reads=[srcbuf])

    def load_cache(b, src_k=None, src_v=None):
        i = b % 2
        kc_raw = KT[:, 8192 + i * 4096: 8192 + (i + 1) * 4096].rearrange("p (j d) -> p j d", d=128)
        vc = Vr[:, i * 32:(i + 1) * 32, :]
        kcT = KT[:, i * 4096:(i + 1) * 4096]
        P.dma("pool", dict(out=kc_raw, in_=(cache_k[HD["h"], b] if src_k is None else src_k).rearrange("(j p) d -> p j d", p=128)), writes=[bf("kc_raw%d" % i)])
        P.dma("pool", dict(out=vc, in_=(cache_v[HD["h"], b] if src_v is None else src_v).rearrange("(j p) d -> p j d", p=128)), writes=[bf("vc%d" % i)])
        for g in range(4):
            pb = 2 + (g % 2)
            tp = ps[pb][:, :].bitcast(BF16)
            for jj in range(8):
                j = g * 8 + jj
                P.op("pe", "transpose", dict(out=tp[:, jj * 128:(jj + 1) * 128], in_=kc_raw[:, j, :], identity=ident_b[:]),
                     reads=[bf("kc_raw%d" % i), bf("ident_b")], writes=[psb[pb]])
            if g % 2 == 0:
                P.op("act", "activation", dict(out=kcT[:, g * 1024:(g + 1) * 1024], in_=tp[:, :], func=AF.Copy),
                     reads=[psb[pb]], writes=[bf("kcT%d_%d" % (i, g))])
            else:
                P.op("dve", "tensor_copy", dict(out=kcT[:, g * 1024:(g + 1) * 1024], in_=tp[:, :]),
                     reads=[psb[pb]], writes=[bf("kcT%d_%d" % (i, g))])

    def mixers(ch, hTb):
        n_local = mixers.n
        mixers.n += 1
        is_p = ch < NCH_P
        if mode != "fused":
            proj(3, ps[3], psb[3], hTb)
            P.op("act", "activation", dict(out=zs_sb[:], in_=ps[3][:, :], func=AF.Silu), reads=[psb[3]], writes=[bf("zs_sb")])
        if "att" in flags and mode != "fused":
            if is_p:
                blocks = []
                for j in range(4 * ch + 3, -1, -1):
                    blk = dict(kT=KT[:, j * 128:(j + 1) * 128], kbufs=[bf("KT%d" % (j // 4))],
                               v=Vr[:, j, :], vbufs=[bf("Vr%d" % (j // 4))])
                    if j >= 4 * ch:
                        blk["mask"] = maskp[:, j - 4 * ch, :]
                        blk["mbufs"] = [bf("maskp")]
                    blocks.append(blk)
                attention(CH, QT[:], [bf("QT")], blocks, ps[7][:, :], 0)
            else:
                if not sample_ready["done"]:
                    sample_ready["done"] = True
                    P.barrier()
                    load_cache((ch - NCH_P) * 8)
                for c in range(8):
                    b = (ch - NCH_P) * 8 + c
                    i = b % 2
                    if b + 1 < NB:
                        load_cache(b + 1)
                    tt, par = c // 2, c % 2
                    kcT = KT[:, i * 4096:(i + 1) * 4096]
                    vc = Vr[:, i * 32:(i + 1) * 32, :]
                    blocks = [dict(kT=ksT[:, tt * 128:(tt + 1) * 128], kbufs=[bf("ksT")], v=vs_bf[:, tt, :], vbufs=[bf("vs_bf")],
                                   mask=masks[:, par, :], mbufs=[bf("masks")])]
                    for j in range(31, -1, -1):
                        blocks.append(dict(kT=kcT[:, j * 128:(j + 1) * 128], kbufs=[bf("kcT%d_%d" % (i, j // 8))],
                                           v=vc[:, j, :], vbufs=[bf("vc%d" % i)]))
                    attention(TSQ, QT[:, c * TSQ:(c + 1) * TSQ], [bf("QT")], blocks, ps[7][:, c * TSQ:(c + 1) * TSQ], c * TSQ)
            P.op("dve", "tensor_tensor", dict(out=aT[:], in0=ps[7][:, :], in1=zs_sb[:], op=ALU.mult),
                 reads=[psb[7], bf("zs_sb")], writes=[bf("aT")])
            emit_ab(ch, 0, aT, bf("aT"))
            if mode == "mix":
                P.op("dve", "tensor_copy", dict(out=dbg_t[:], in_=aT[:]), reads=[bf("aT")], writes=[bf("t_lf")])
                P.dma("sp", dict(out=dbg_a[:, ch * CH:(ch + 1) * CH], in_=dbg_t[:]), reads=[bf("t_lf")])
        if "hg" not in flags:
            return
        if not is_p:
            b0_ = (ch - NCH_P) * 8
            P.dma("sp", dict(out=sin[:], in_=s_in[HD["h"], b0_:b0_ + 8].rearrange("b k v -> k b v")), writes=[bf("sin")])
        proj(4, ps[2], psb[2], hTb)
        P.op("act", "activation", dict(out=t_f[:], in_=ps[2][:, :], func=AF.Sigmoid), reads=[psb[2]], writes=[bf("t_f")])
        proj(6, ps[3], psb[3], hTb)
        P.op("act", "activation", dict(out=t_qs[:], in_=ps[3][:, :], func=AF.Silu), reads=[psb[3]], writes=[bf("t_qs")])
        proj(7, ps[2], psb[2], hTb)
        P.op("act", "activation", dict(out=t_zh[:], in_=ps[2][:, :], func=AF.Silu), reads=[psb[2]], writes=[bf("t_zh")])
        proj(5, ps[3], psb[3], hTb)
        P.op("act", "activation", dict(out=t_iT[:], in_=ps[3][:, :], func=AF.Copy), reads=[psb[3]], writes=[bf("t_iT")])
        P.op("dve", "tensor_scalar", dict(out=t_f[:], in0=t_f[:], scalar1=lbr[:, 3:4], scalar2=lbr[:, 2:3], op0=ALU.mult, op1=ALU.add),
             reads=[bf("t_f"), bf("lbr")], writes=[bf("t_f")])
        P.op("act", "activation", dict(out=t_lf[:], in_=t_f[:], func=AF.Ln), reads=[bf("t_f")], writes=[bf("t_lf")])
        P.op("dve", "tensor_scalar", dict(out=t_k[:], in0=t_f[:], scalar1=-1.0, scalar2=1.0, op0=ALU.mult, op1=ALU.add),
             reads=[bf("t_f")], writes=[bf("t_k")])
        P.op("dve", "tensor_tensor_scan", dict(out=t_b[:], data0=resetm[:], data1=t_lf[:], initial=0.0, op0=ALU.mult, op1=ALU.add),
             reads=[bf("resetm"), bf("t_lf")], writes=[bf("t_b")])
        P.op("act", "activation", dict(out=t_eb[:], in_=t_b[:], func=AF.Exp), reads=[bf("t_b")], writes=[bf("t_eb")])
        P.op("act", "activation", dict(out=t_enb[:], in_=t_b[:], func=AF.Exp, scale=-1.0), reads=[bf("t_b")], writes=[bf("t_enb")])
        P.op("act", "activation", dict(out=dec[:].unsqueeze(2), in_=t_b[:].rearrange("p (c t) -> p c t", t=64)[:, :, 63:64], func=AF.Exp),
             reads=[bf("t_b")], writes=[bf("dec")])
        P.op("dve", "tensor_tensor", dict(out=t_eb[:], in0=t_qs[:], in1=t_eb[:], op=ALU.mult),
             reads=[bf("t_qs"), bf("t_eb")], writes=[bf("t_eb")])
        P.op("dve", "tensor_tensor", dict(out=t_enb[:], in0=t_k[:], in1=t_enb[:], op=ALU.mult),
             reads=[bf("t_k"), bf("t_enb")], writes=[bf("t_enb")])
        P.op("dve", "tensor_tensor", dict(out=t_ke[:].rearrange("p (c t) -> p c t", t=64),
                                          in0=t_enb[:].rearrange("p (c t) -> p c t", t=64),
                                          in1=dec[:].unsqueeze(2).to_broadcast([128, 8, 64]), op=ALU.mult),
             reads=[bf("t_enb"), bf("dec")], writes=[bf("t_ke")])
        hgs = [int(f[3:]) for f in flags if f.startswith("hgs")]
        hgs = hgs[0] if hgs else 99
        if hgs <= 1:
            return
        for which in range(2):
            src, sbuf_ = (t_iT, bf("t_iT")) if which == 0 else (t_ke, bf("t_ke"))
            for tt in range(4):
                P.op("pe", "transpose", dict(out=ps[5][:, tt * 128:(tt + 1) * 128], in_=src[:, tt * 128:(tt + 1) * 128], identity=ident_f[:]),
                     reads=[sbuf_, bf("ident_f")], writes=[psb[5]])
            if which == 0:
                P.op("act", "activation", dict(out=iv[:].rearrange("p t d -> p (t d)"), in_=ps[5][:, :], func=AF.Copy),
                     reads=[psb[5]], writes=[bf("iv")])
            else:
                for ab in range(2):
                    P.op("act", "activation", dict(out=ket[:, ab, :, :].rearrange("p t d -> p (t d)"), in_=ps[5][:, :], func=AF.Copy,
                                                   scale=pmask[:, ab:ab + 1]),
                         reads=[psb[5], bf("pmask")], writes=[bf("ket%d" % ab)])
        if hgs <= 2:
            return
        for c in range(8):
            pb = c // 4
            r0 = (c % 2) * 64
            P.op("pe", "matmul", dict(out=ps[pb][:, (c % 4) * 128:(c % 4 + 1) * 128], lhsT=ket[:, c % 2, c // 2, :],
                                      rhs=iv[:, c // 2, :], start=True, stop=True),
                 reads=[bf("ket%d" % (c % 2)), bf("iv")], writes=[psb[pb]])
        if hgs <= 3:
            return
        for p_ in range(4):
            P.op("pe", "matmul", dict(out=ps[4][:, p_ * 128:(p_ + 1) * 128], lhsT=t_enb[:, p_ * 128:(p_ + 1) * 128],
                                      rhs=t_eb[:, p_ * 128:(p_ + 1) * 128], start=True, stop=True),
                 reads=[bf("t_enb"), bf("t_eb")], writes=[psb[4]])
        P.op("dve", "tensor_tensor", dict(out=attm[:], in0=ps[4][:, :].rearrange("p (a t) -> p a t", t=128),
                                          in1=hmask[:].unsqueeze(1).to_broadcast([128, 4, 128]), op=ALU.mult),
             reads=[psb[4], bf("hmask")], writes=[bf("attm")])
        if hgs <= 4:
            return
        cur = Sset[n_local % 2]
        prev_set = Sset[(n_local + 1) % 2]
        Sprev = []
        for c in range(8):
            if is_p:
                if c == 0:
                    sp_ap, sp_buf = prev_set[:, 7, :], bf("S_%d_7" % ((n_local + 1) % 2))
                else:
                    sp_ap, sp_buf = cur[:, c - 1, :], bf("S_%d_%d" % (n_local % 2, c - 1))
            else:
                b = (ch - NCH_P) * 8 + c
                sp_ap, sp_buf = sin[:, c, :], bf("sin")
            Sprev.append((sp_ap, sp_buf))
            pb = c // 4
            P.op("dve", "scalar_tensor_tensor", dict(out=cur[:, c, :], in0=sp_ap, scalar=dec[:, c:c + 1],
                                                     in1=ps[pb][:, (c % 4) * 128:(c % 4 + 1) * 128], op0=ALU.mult, op1=ALU.add),
                 reads=[sp_buf, bf("dec"), psb[pb]], writes=[bf("S_%d_%d" % (n_local % 2, c))])
        mixers.first = False
        if hgs <= 5:
            return
        for p_ in range(4):
            P.op("pe", "matmul", dict(out=ps[6][:, p_ * 128:(p_ + 1) * 128], lhsT=iv[:, p_, :], rhs=attm[:, p_, :], start=True, stop=False),
                 reads=[bf("iv"), bf("attm")], writes=[psb[6]])
            for h2 in range(2):
                c = 2 * p_ + h2
                sp_ap, sp_buf = Sprev[c]
                P.op("pe", "matmul", dict(out=ps[6][:, c * 64:(c + 1) * 64], lhsT=sp_ap, rhs=t_eb[:, c * 64:(c + 1) * 64],
                                          start=False, stop=(h2 == 1)),
                     reads=[sp_buf, bf("t_eb")], writes=[psb[6]])
        if hgs <= 6:
            return
        headnorm(ps[6], psb[6], ogain[:, 0:1], [bf("ogain")], t_ke[:], bf("t_ke"))
        P.op("dve", "tensor_tensor", dict(out=bT[:], in0=t_ke[:], in1=t_zh[:], op=ALU.mult),
             reads=[bf("t_ke"), bf("t_zh")], writes=[bf("bT")])
        emit_ab(ch, 1, bT, bf("bT"))
        if mode == "mix":
            P.op("dve", "tensor_copy", dict(out=dbg_t[:], in_=bT[:]), reads=[bf("bT")], writes=[bf("t_lf")])
            P.dma("sp", dict(out=dbg_b[:, ch * CH:(ch + 1) * CH], in_=dbg_t[:]), reads=[bf("t_lf")])
        if is_p:
            if ch == NCH_P - 1 or (debug and ch == chunks[-1]):
                P.dma("sp", dict(out=s_out[HD["h"], 0], in_=cur[:, 7, :]), reads=[bf("S_%d_7" % (n_local % 2))])
        else:
            b0 = (ch - NCH_P) * 8
            P.dma("sp", dict(out=s_out[HD["h"], 1 + b0:1 + b0 + 8].rearrange("b k v -> k b v"), in_=cur[:]),
                  reads=[bf("S_%d_%d" % (n_local % 2, c)) for c in range(8)])
    mixers.n = 0
    mixers.first = True

    def build_hT(src_dram, row0, ntiles, segs_fn):
        for tt in range(ntiles):
            t0 = row0 + tt * 128
            i = state["tile_i"] % 2
            state["tile_i"] += 1
            xb = xbuf[i]
            xbb = bf("xbuf%d" % i)
            st = stat[i]
            stb = bf("stat%d" % i)
            P.dma("sp", dict(out=xb[:], in_=src_dram[t0:t0 + 128, :]), writes=[xbb])
            P.op("act", "activation", dict(out=junk[:], in_=xb[:], func=AF.Square, accum_out=st[:, 0:1]),
                 reads=[xbb], writes=[bf("xn"), stb])
            P.op("act", "activation", dict(out=st[:, 1:2], in_=st[:, 0:1], func=AF.Ln, scale=1.0 / D, bias=epsb[:, 0:1]),
                 reads=[stb, bf("epsb")], writes=[stb])
            P.op("act", "activation", dict(out=st[:, 2:3], in_=st[:, 1:2], func=AF.Exp, scale=-0.5),
                 reads=[stb], writes=[stb])
            P.op("dve", "tensor_scalar", dict(out=xn[:], in0=xb[:], scalar1=st[:, 2:3], scalar2=None, op0=ALU.mult),
                 reads=[xbb, stb], writes=[bf("xn")])
            for half in range(2):
                tp = ps[half][:, :].bitcast(BF16)
                for kk in range(8):
                    k = half * 8 + kk
                    P.op("pe", "transpose", dict(out=tp[:, kk * 128:(kk + 1) * 128], in_=xn[:, k * 128:(k + 1) * 128],
                                                 identity=ident_b[:]),
                         reads=[bf("xn"), bf("ident_b")], writes=[psb[half]])
                for kk in range(8):
                    k = half * 8 + kk
                    segs = segs_fn(tt)
                    for (c0, c1, r) in segs:
                        o_ap = hT[:, k, tt * 128 + c0:tt * 128 + c1]
                        i_ap = tp[:, kk * 128 + c0:kk * 128 + c1]
                        if len(segs) == 1:
                            wr = [bf("hT_%d_%d_0" % (k, tt)), bf("hT_%d_%d_64" % (k, tt))]
                        else:
                            wr = [bf("hT_%d_%d_%d" % (k, tt, c0))]
                        rd = [psb[half], bf("gm"), bf("sh")]
                        if half == 0:
                            P.op("act", "activation", dict(out=o_ap, in_=i_ap, func=AF.Identity,
                                                           scale=gm[:, k, r:r + 1], bias=sh[:, k, r:r + 1]), reads=rd, writes=wr)
                        else:
                            P.op("dve", "tensor_scalar", dict(out=o_ap, in0=i_ap, scalar1=gm[:, k, r:r + 1],
                                                              scalar2=sh[:, k, r:r + 1], op0=ALU.mult, op1=ALU.add),
                                 reads=rd, writes=wr)
        hTb = {}
        for k in range(KC):
            lst = []
            for tt in range(ntiles):
                lst.append(bf("hT_%d_%d_0" % (k, tt)))
                lst.append(bf("hT_%d_%d_64" % (k, tt)))
            hTb[k] = lst
        return hTb

    def own_slots():
        for s_ in range(5):
            is_p = s_ < 4
            nt = 4 if is_p else 1
            N = nt * 128
            if is_p:
                segs_fn = lambda tt: [(0, 128, 0)]
            else:
                segs_fn = lambda tt: [(0, 64, RS0), (64, 128, RS0 + 1)]
            hTb = build_hT(x_own, s_ * CH, nt, segs_fn)
            proj(0, ps[2], psb[2], hTb)
            headnorm(ps[2], psb[2], qkgs2[:, 0:1], [bf("qkgs2a")], QT[:], bf("QT"))
            proj(1, ps[3], psb[3], hTb)
            headnorm(ps[3], psb[3], qkgs2[:, 1:2], [bf("qkgs2b")], kn_f[:], bf("kn_f"))
            P.op("act", "activation", dict(out=ksT[:], in_=kn_f[:], func=AF.Copy), reads=[bf("kn_f")], writes=[bf("ksT")])
            proj(2, ps[2], psb[2], hTb)
            P.op("dve", "tensor_copy", dict(out=vT_f[:], in_=ps[2][:, :]), reads=[psb[2]], writes=[bf("vT_f")])
            for tt in range(nt):
                P.op("pe", "transpose", dict(out=ps[5][:, tt * 128:(tt + 1) * 128], in_=vT_f[:, tt * 128:(tt + 1) * 128],
                                             identity=ident_f[:]),
                     reads=[bf("vT_f"), bf("ident_f")], writes=[psb[5]])
            P.op("dve", "tensor_copy", dict(out=vs_bf[:, 0:nt, :].rearrange("p t d -> p (t d)"), in_=ps[5][:, 0:N]),
                 reads=[psb[5]], writes=[bf("vs_bf")])
            proj(3, ps[3], psb[3], hTb)
            P.op("act", "activation", dict(out=zs_sb[:], in_=ps[3][:, :], func=AF.Silu), reads=[psb[3]], writes=[bf("zs_sb")])
            if is_p:
                blocks = []
                for i_ in range(3, -1, -1):
                    blocks.append(dict(kT=ksT[:, i_ * 128:(i_ + 1) * 128], kbufs=[bf("ksT")], v=vs_bf[:, i_, :], vbufs=[bf("vs_bf")],
                                       mask=maskp[:, i_, :], mbufs=[bf("maskp")]))
                for j in range(127, -1, -1):
                    blocks.append(dict(kT=KT[:, j * 128:(j + 1) * 128], kbufs=[bf("KT%d" % (j // 4))],
                                       v=Vr[:, j, :], vbufs=[bf("Vr%d" % (j // 4))], bias=vis[:, s_ * 128 + j:s_ * 128 + j + 1]))
                attention(CH, QT[:], [bf("QT")], blocks, ps[7][:, :], 0)
            else:
                P.barrier()
                load_cache(0, cache_k[HD["h"], 0], cache_v[HD["h"], 0])
                for par in range(2):
                    if par == 0:
                        load_cache(1, cache_k[HD["h"], 1], cache_v[HD["h"], 1])
                    i = par
                    kcT = KT[:, i * 4096:(i + 1) * 4096]
                    vc = Vr[:, i * 32:(i + 1) * 32, :]
                    blocks = [dict(kT=ksT[:, 0:128], kbufs=[bf("ksT")], v=vs_bf[:, 0, :], vbufs=[bf("vs_bf")],
                                   mask=masks[:, par, :], mbufs=[bf("masks")])]
                    for j in range(31, -1, -1):
                        blocks.append(dict(kT=kcT[:, j * 128:(j + 1) * 128], kbufs=[bf("kcT%d_%d" % (i, j // 8))],
                                           v=vc[:, j, :], vbufs=[bf("vc%d" % i)]))
                    attention(TSQ, QT[:, par * TSQ:(par + 1) * TSQ], [bf("QT")], blocks, ps[7][:, par * TSQ:(par + 1) * TSQ], 0)
            P.op("dve", "tensor_tensor", dict(out=aT[:, 0:N], in0=ps[7][:, 0:N], in1=zs_sb[:, 0:N], op=ALU.mult),
                 reads=[psb[7], bf("zs_sb")], writes=[bf("aT")])
            P.dma("sp", dict(out=a_own_scr.ap()[HD["h"], s_ * 4:s_ * 4 + nt].rearrange("t p c -> p t c"),
                             in_=aT[:, 0:N].rearrange("p (t c) -> p t c", c=128)), reads=[bf("aT")])

    if mode == "fused" and "zero_scr" in flags:
        P.op("dve", "memset", dict(ap=aT[:], constant=0.0), writes=[bf("aT")])
        P.op("dve", "memset", dict(ap=KT[:], constant=0.0), writes=[bf("KT%d" % i_) for i_ in range(32)])
        P.op("dve", "memset", dict(ap=Vr[:].rearrange("p j d -> p (j d)"), constant=0.0), writes=[bf("Vr%d" % i_) for i_ in range(32)])
        for n_ in range(2 * 8 * 34):
            P.dma("sp", dict(out=ab_scr.ap()[n_ * 512:(n_ + 1) * 512, :].rearrange("(t p) c -> p t c", p=128),
                             in_=aT[:].rearrange("p (t c) -> p t c", c=128)), reads=[bf("aT")])
        P.barrier()
    for hd in range(NH if do_mix else 0):
        HD["h"] = hd
        head_setup()
        for ch in chunks:
            if ch < NCH_P:
                segs_fn = lambda tt: [(0, 128, 0)]
            else:
                segs_fn = (lambda ch: lambda tt: [(0, 64, 1 + ((ch - NCH_P) * 4 + tt) * 2), (64, 128, 2 + ((ch - NCH_P) * 4 + tt) * 2)])(ch)
            hTb = build_hT(x_all, ch * CH - x_base, 4, segs_fn)

            if stop in ("hT", "ev_act", "ev_dve"):
                return done()
            if mode != "fused":
                proj(0, ps[2], psb[2], hTb)
                headnorm(ps[2], psb[2], qkgs2[:, 0:1], [bf("qkgs2a")], QT[:], bf("QT"))
            if stop == "q":
                return done()
            proj(1, ps[3], psb[3], hTb)
            headnorm(ps[3], psb[3], qkgs2[:, 1:2], [bf("qkgs2b")], kn_f[:], bf("kn_f"))
            if ch < NCH_P:
                P.op("act", "activation", dict(out=KT[:, ch * CH:(ch + 1) * CH], in_=kn_f[:], func=AF.Copy),
                     reads=[bf("kn_f")], writes=[bf("KT%d" % ch)])
            else:
                P.op("act", "activation", dict(out=ksT[:], in_=kn_f[:], func=AF.Copy), reads=[bf("kn_f")], writes=[bf("ksT")])
            tr_out(ch, kn_f, bf("kn_f"), k_out[HD["h"]])
            if stop == "k":
                return done()
            proj(2, ps[2], psb[2], hTb)
            P.op("dve", "tensor_copy", dict(out=vT_f[:], in_=ps[2][:, :]), reads=[psb[2]], writes=[bf("vT_f")])

            def vcopy(pt, ptb, ch=ch):
                if ch < NCH_P:
                    P.op("dve", "tensor_copy", dict(out=Vr[:, ch * 4:(ch + 1) * 4, :].rearrange("p t d -> p (t d)"), in_=pt[:, :]),
                         reads=[ptb], writes=[bf("Vr%d" % ch)])
                else:
                    P.op("dve", "tensor_copy", dict(out=vs_bf[:].rearrange("p t d -> p (t d)"), in_=pt[:, :]),
                         reads=[ptb], writes=[bf("vs_bf")])
            tr_out(ch, vT_f, bf("vT_f"), v_out[HD["h"]], extra_copy=vcopy)
            mixers(ch, hTb)
        if mode == "fused" and len(chunks) > 0:
            own_slots()

    if do_out:
        P.barrier()
        es1.close()
        gateP = sb("gateP", [128, D], F32)
        gateS = sb("gateS", [128, D], F32)
        if mode == "fused":
            idx_ab = sb("idx_ab_s", [128, 272], mybir.dt.int32)
            P.dma("sp", dict(out=idx_ab[:], in_=idx_ab_d[:, :]), writes=[bf("idx_ab")])
        es3 = ExitStack()
        csbc = [es3.enter_context(nc.sbuf_tensor("csbc%d" % i, [128, KC, 128], BF16)) for i in range(2)]
        bgate = es3.enter_context(nc.sbuf_tensor("bgate", [128, D], F32))
        wada2 = [es3.enter_context(nc.sbuf_tensor("wada2_%d" % i, [128, KC, 512], BF16)) for i in range(2)]
        P.dma("sp", dict(out=bgate[:], in_=b_gate_bc[:, :]), writes=[bf("bgate")])
        P.op("dve", "tensor_copy", dict(out=csbc[0][:], in_=csT[:, :, 0:1].to_broadcast([128, KC, 128])),
             reads=[bf("csT")], writes=[bf("csbc0")])
        P.op("dve", "tensor_copy", dict(out=csbc[1][:, :, 0:64], in_=csT[:, :, RS0:RS0 + 1].to_broadcast([128, KC, 64])),
             reads=[bf("csT")], writes=[bf("csbc1a")])
        P.op("dve", "tensor_copy", dict(out=csbc[1][:, :, 64:128], in_=csT[:, :, RS0 + 1:RS0 + 2].to_broadcast([128, KC, 64])),
             reads=[bf("csT")], writes=[bf("csbc1b")])
        for g in range(8, 12):
            wb = wada2[g % 2]
            wbuf = bf("wada2_%d" % (g % 2))
            P.dma("pool", dict(out=wb[:], in_=w_ada[:, g * 512:(g + 1) * 512].rearrange("(k p) c -> p k c", p=128)),
                  writes=[wbuf])
            for which in range(2):
                pb = 3 + which
                for k in range(KC):
                    P.op("pe", "matmul", dict(out=ps[pb][:, :], lhsT=csbc[which][:, k, :], rhs=wb[:, k, :],
                                              start=(k == 0), stop=(k == KC - 1)),
                         reads=[wbuf, bf("csbc0"), bf("csbc1a"), bf("csbc1b")], writes=[psb[pb]])
                gt = gateP if which == 0 else gateS
                P.op("dve", "tensor_tensor", dict(out=gt[:, (g - 8) * 512:(g - 7) * 512], in0=ps[pb][:, :],
                                                  in1=bgate[:, (g - 8) * 512:(g - 7) * 512], op=ALU.add),
                     reads=[psb[pb], bf("bgate")], writes=[bf("gate%d_%d" % (which, g - 8))])
        P.barrier()
        es3.close()
        wslot = [sb("wslot%d" % i, [128, 24576], BF16) for i in range(2)]
        abT = sb("abT", [128, 2, 8, CH], BF16)
        mT = sb("mT", [128, KC, CH], BF16)
        sgA = sb("sgA", [128, CH], F32)
        sgB = sb("sgB", [128, CH], F32)
        tmp1 = sb("tmp1", [128, CH], F32)
        tmp2 = sb("tmp2", [128, CH], F32)
        ysl = [sb("ysl%d" % i, [128, CH], F32) for i in range(2)]
        xsl = [sb("xsl%d" % i, [128, CH], F32) for i in range(2)]
        gi = 0
        yi = 0
        for o in range(5):
            N = CH if o < 4 else OWN_S
            NT = N // 128
            if o < 4:
                segs_fn = lambda tt: [(0, 128, 0)]
            else:
                segs_fn = lambda tt: [(0, 64, RS0), (64, 128, RS0 + 1)]
            hTb = build_hT(x_own, o * CH, NT, segs_fn)
            if mode == "out":
                for ab in range(2):
                    P.dma("pool", dict(out=abT[:, ab, :, 0:N], in_=ab_own[ab, :, :, o * CH:o * CH + N].rearrange("h p t -> p h t")),
                          writes=[bf("abT%d" % ab)])
                abbufs = {0: [bf("abT0")], 1: [bf("abT1")]}
            else:
                abbufs = {0: [], 1: []}
                for h in range(8):
                    P.dma("sp", dict(out=abT[:, 0, h, 0:N].rearrange("p (t c) -> p t c", c=128),
                                     in_=a_own_scr.ap()[h, o * 4:o * 4 + NT].rearrange("t p c -> p t c")),
                          writes=[bf("abT_a%d" % h)])
                    abbufs[0].append(bf("abT_a%d" % h))
                for ab in range(1, 2):
                    for h in range(8):
                        for tt in range(NT):
                            col = (ab * 8 + h) * 17 + o * 4 + tt
                            bb = bf("abT_%d_%d_%d" % (ab, h, tt))
                            P.dma("pool", dict(out=abT[:, ab, h, tt * 128:(tt + 1) * 128], out_offset=None, in_=ab_scr.ap(),
                                               in_offset=bass.IndirectOffsetOnAxis(ap=idx_ab[:, col:col + 1], axis=0)),
                                  reads=[bf("idx_ab")], writes=[bb], meth="indirect_dma_start")
                            abbufs[ab].append(bb)
            for g in range(4):
                sl = gi % 2
                gi += 1
                ws = wslot[sl]
                wgA = ws[:, 0:8192].rearrange("p (k c) -> p k c", c=512)
                wgB = ws[:, 8192:16384].rearrange("p (k c) -> p k c", c=512)
                wbA = ws[:, 16384:20480].rearrange("p (h c) -> p h c", c=512)
                wbB = ws[:, 20480:24576].rearrange("p (h c) -> p h c", c=512)
                P.dma("pool", dict(out=wgA, in_=w_gate[:, g * 512:(g + 1) * 512].rearrange("(k p) c -> p k c", p=128)),
                      writes=[bf("ws%d_A" % sl)])
                P.dma("pool", dict(out=wgB, in_=w_gate[:, D + g * 512:D + (g + 1) * 512].rearrange("(k p) c -> p k c", p=128)),
                      writes=[bf("ws%d_B" % sl)])
                P.dma("pool", dict(out=wbA, in_=w_bsb[:, g * 512:(g + 1) * 512].rearrange("(h p) c -> p h c", p=128)),
                      writes=[bf("ws%d_C" % sl)])
                P.dma("pool", dict(out=wbB, in_=w_bhg[:, g * 512:(g + 1) * 512].rearrange("(h p) c -> p h c", p=128)),
                      writes=[bf("ws%d_D" % sl)])
                for cc in range(4):
                    jc = g * 4 + cc
                    for (wg, wgbuf, pb, sg, sgbuf) in ((wgA, bf("ws%d_A" % sl), 2, sgA, bf("sgA")), (wgB, bf("ws%d_B" % sl), 3, sgB, bf("sgB"))):
                        for k in range(KC):
                            P.op("pe", "matmul", dict(out=ps[pb][:, 0:N], lhsT=wg[:, k, cc * 128:(cc + 1) * 128], rhs=hT[:, k, 0:N],
                                                      start=(k == 0), stop=(k == KC - 1)),
                                 reads=[wgbuf] + hTb[k], writes=[psb[pb]])
                        P.op("act", "activation", dict(out=sg[:, 0:N], in_=ps[pb][:, 0:N], func=AF.Sigmoid),
                             reads=[psb[pb]], writes=[sgbuf])
                    for (wbr, wbbuf, pb, ab) in ((wbA, bf("ws%d_C" % sl), 4, 0), (wbB, bf("ws%d_D" % sl), 5, 1)):
                        for h in range(8):
                            P.op("pe", "matmul", dict(out=ps[pb][:, 0:N], lhsT=wbr[:, h, cc * 128:(cc + 1) * 128], rhs=abT[:, ab, h, 0:N],
                                                      start=(h == 0), stop=(h == 7)),
                                 reads=[wbbuf] + abbufs[ab], writes=[psb[pb]])
                    P.op("dve", "tensor_tensor", dict(out=tmp1[:, 0:N], in0=ps[4][:, 0:N], in1=sgA[:, 0:N], op=ALU.mult),
                         reads=[psb[4], bf("sgA")], writes=[bf("tmp1")])
                    P.op("dve", "tensor_tensor", dict(out=tmp2[:, 0:N], in0=ps[5][:, 0:N], in1=sgB[:, 0:N], op=ALU.mult),
                         reads=[psb[5], bf("sgB")], writes=[bf("tmp2")])
                    P.op("pool", "tensor_tensor", dict(out=mT[:, jc, 0:N], in0=tmp1[:, 0:N], in1=tmp2[:, 0:N], op=ALU.add),
                         reads=[bf("tmp1"), bf("tmp2")], writes=[bf("mT%d" % jc)])
            for cg in range(4):
                sl = gi % 2
                gi += 1
                ws = wslot[sl]
                wo = ws[:, 0:8192].rearrange("p (k c) -> p k c", c=512)
                P.dma("pool", dict(out=wo, in_=w_o[:, cg * 512:(cg + 1) * 512].rearrange("(k p) c -> p k c", p=128)),
                      writes=[bf("ws%d_A" % sl)])
                for tt in range(NT):
                    pb = 6 + (yi % 2)
                    ys = ysl[yi % 2]
                    xs = xsl[yi % 2]
                    ysb = bf("ysl%d" % (yi % 2))
                    xsb = bf("xsl%d" % (yi % 2))
                    yi += 1
                    r0 = o * CH + tt * 128
                    P.dma("sp", dict(out=xs[:], in_=x_own[r0:r0 + 128, cg * 512:(cg + 1) * 512]), writes=[xsb])
                    for k in range(KC):
                        P.op("pe", "matmul", dict(out=ps[pb][:, :], lhsT=mT[:, k, tt * 128:(tt + 1) * 128], rhs=wo[:, k, :],
                                                  start=(k == 0), stop=(k == KC - 1)),
                             reads=[bf("ws%d_A" % sl), bf("mT%d" % k)], writes=[psb[pb]])
                    gt = gateP if o < 4 else gateS
                    which = 0 if o < 4 else 1
                    P.op("dve", "tensor_tensor", dict(out=ys[:], in0=ps[pb][:, :], in1=gt[:, cg * 512:(cg + 1) * 512], op=ALU.mult),
                         reads=[psb[pb], bf("gate%d_%d" % (which, cg))], writes=[ysb])
                    P.op("pool", "tensor_tensor", dict(out=ys[:], in0=ys[:], in1=xs[:], op=ALU.add),
                         reads=[ysb, xsb], writes=[ysb])
                    P.dma("sp", dict(out=y_own[r0:r0 + 128, cg * 512:(cg + 1) * 512], in_=ys[:]), reads=[ysb])

    P.finish()
    P.emit(nc, es)
    if not do_out:
        es1.close()
    es.close()
    return nc


def _consts():
    j = np.arange(128)[:, None]
    k = np.arange(128)[None, :]
    trin = np.where(j >= k, -1.0, 0.0).astype(np.float32)
    tric = np.where(j < k, -1.0, 0.0).astype(np.float32)
    q = np.arange(512)[None, None, :]
    kk = np.arange(128)[:, None, None]
    i = np.arange(4)[None, :, None]
    maskp = np.where(q > kk + 128 * i, 0.0, -30000.0).astype(np.float32)
    qs = np.arange(64)[None, :]
    ks = np.arange(128)[:, None]
    m_even = np.where((ks < 64) & (qs > ks), 0.0, -30000.0)
    m_odd = np.where((ks >= 64) & (qs > ks - 64), 0.0, -30000.0)
    masks = np.stack([m_even, m_odd], axis=1).astype(np.float32)
    s_ = np.arange(128)[:, None]
    t_ = np.arange(128)[None, :]
    hmask = ((s_ // 64 == t_ // 64) & (s_ <= t_)).astype(np.float32)
    resetm = np.ones((128, 512), np.float32)
    resetm[:, ::64] = 0.0
    return {"identf": np.eye(128, dtype=np.float32), "trin": trin, "tric": tric, "maskp": maskp, "masks": masks,
            "hmask": hmask, "resetm": resetm,
            "pmask": np.stack([(np.arange(128) < 64), (np.arange(128) >= 64)], axis=1).astype(np.float32)}


def _f32(a):
    return np.ascontiguousarray(np.asarray(a, dtype=np.float32))


def make_in_maps(inp, cores=None, x_rows=TT, x_base=0):
    f32 = _f32
    x_all = f32(np.concatenate([np.asarray(inp["x_prompt"]).reshape(TP, D), np.asarray(inp["x_sample"]).reshape(TS, D)], axis=0))[x_base:x_base + x_rows]
    c_all = f32(np.concatenate([np.asarray(inp["c_prompt"]), np.asarray(inp["c_sample"])], axis=0))
    w_ada0 = f32(np.asarray(inp["w_ada"])[0])
    b_adaT = f32(np.asarray(inp["b_ada"])[0].reshape(48, 128).T)
    ngT = f32(np.asarray(inp["norm_gain"])[0].reshape(KC, 128).T)
    w_in0 = np.asarray(inp["w_in"])[0]
    qkg = f32(np.stack([np.asarray(inp["q_norm_gain"])[0], np.asarray(inp["k_norm_gain"])[0]], axis=1))
    consts = _consts()
    in_maps = []
    for c in (range(NCORES) if cores is None else cores):
        cols = np.concatenate([np.arange(j * 1024 + c * 128, j * 1024 + (c + 1) * 128) for j in range(8)])
        lbr = f32(np.asarray(inp["hgrn_lb_raw"])[:, c * 128:(c + 1) * 128].T)
        m = {"x_all": x_all, "c_all": c_all, "w_ada": w_ada0, "b_adaT": b_adaT, "ngT": ngT,
             "w_head": f32(w_in0[:, cols])[None], "qkg": qkg, "lbr": lbr[None],
             "ogain": f32(np.asarray(inp["hgrn_onorm_gain"])[0, c, :].reshape(1, 128, 1)),
             "cache_k": f32(np.asarray(inp["cache_sb_k"])[0, :, :, c, :])[None],
             "cache_v": f32(np.asarray(inp["cache_sb_v"])[0, :, :, c, :])[None],
             "s_in": f32(np.asarray(inp["state_hgrn"])[0, :, c])[None]}
        m.update(consts)
        in_maps.append(m)
    return in_maps


def make_out_maps(inp, a_all, b_all, cores=None):
    f32 = _f32
    xp = np.asarray(inp["x_prompt"]).reshape(TP, D)
    xs = np.asarray(inp["x_sample"]).reshape(TS, D)
    cp = np.asarray(inp["c_prompt"])
    cs = np.asarray(inp["c_sample"])
    w_ada0 = f32(np.asarray(inp["w_ada"])[0])
    b_ada0 = np.asarray(inp["b_ada"])[0]
    b_adaT = f32(b_ada0.reshape(48, 128).T)
    ngT = f32(np.asarray(inp["norm_gain"])[0].reshape(KC, 128).T)
    w_gate = f32(np.asarray(inp["w_in"])[0][:, 8192:])
    w_bsb = f32(np.asarray(inp["w_branch_sb"])[0])
    w_bhg = f32(np.asarray(inp["w_branch_hgrn"])[0])
    w_o = f32(np.asarray(inp["w_out"])[0])
    b_gate_bc = f32(np.broadcast_to(b_ada0[2 * D:][None, :], (128, D)))
    maps = []
    for c in (range(NCORES) if cores is None else cores):
        tok = np.concatenate([np.arange(c * OWN_P, (c + 1) * OWN_P), TP + np.arange(c * OWN_S, (c + 1) * OWN_S)])
        m = {"c_all": f32(np.concatenate([cp, cs[2 * c:2 * c + 2]], axis=0)), "w_ada": w_ada0, "b_adaT": b_adaT, "ngT": ngT,
             "identf": np.eye(128, dtype=np.float32),
             "x_own": f32(np.concatenate([xp[c * OWN_P:(c + 1) * OWN_P], xs[c * OWN_S:(c + 1) * OWN_S]], axis=0)),
             "w_gate": w_gate, "w_bsb": w_bsb, "w_bhg": w_bhg, "w_o": w_o, "b_gate_bc": b_gate_bc,
             }
        if a_all is not None:
            m["ab_own"] = f32(np.stack([a_all[:, :, tok], b_all[:, :, tok]], axis=0))
        maps.append(m)
    return maps


def make_fused_maps(inp, cores=None, x_rows=TT, x_base=0):
    f32 = _f32
    base = make_out_maps(inp, None, None, cores=cores)
    x_all = f32(np.concatenate([np.asarray(inp["x_prompt"]).reshape(TP, D), np.asarray(inp["x_sample"]).reshape(TS, D)], axis=0))[x_base:x_base + x_rows]
    cp = np.asarray(inp["c_prompt"])
    cs = np.asarray(inp["c_sample"])
    w_in0 = np.asarray(inp["w_in"])[0]
    w_head = f32(np.stack([w_in0[:, np.concatenate([np.arange(j * 1024 + h * 128, j * 1024 + (h + 1) * 128) for j in range(8)])]
                           for h in range(8)], axis=0))
    qkg = f32(np.stack([np.asarray(inp["q_norm_gain"])[0], np.asarray(inp["k_norm_gain"])[0]], axis=1))
    lbr = f32(np.asarray(inp["hgrn_lb_raw"]).reshape(2, 8, 128).transpose(1, 2, 0))
    ogain = f32(np.asarray(inp["hgrn_onorm_gain"])[0].reshape(8, 128, 1))
    ck = np.asarray(inp["cache_sb_k"])[0]
    cv = np.asarray(inp["cache_sb_v"])[0]
    s_in = f32(np.asarray(inp["state_hgrn"])[0].transpose(1, 0, 2, 3))
    consts = _consts()
    maps = []
    for n, c in enumerate(range(NCORES) if cores is None else cores):
        m = dict(base[n])
        m.pop("ab_own", None)
        m["c_all"] = f32(np.concatenate([cp, cs, cs[2 * c:2 * c + 2]], axis=0))
        idx = np.zeros((128, 272), np.int32)
        for ab in range(2):
            for h in range(8):
                for j in range(17):
                    tile = 16 * c + j if j < 16 else 128 + c
                    idx[:, (ab * 8 + h) * 17 + j] = ((ab * 8 + h) * 136 + tile) * 128 + np.arange(128)
        vis = np.full((128, 512), -30000.0, np.float32)
        for s_ in range(4):
            vis[:, s_ * 128:s_ * 128 + 4 * (4 * c + s_)] = 0.0
        m.update({"x_all": x_all, "w_head": w_head, "qkg": qkg, "lbr": lbr, "ogain": ogain,
                  "cache_k": f32(ck[2 * c:2 * c + 2].transpose(2, 0, 1, 3)), "cache_v": f32(cv[2 * c:2 * c + 2].transpose(2, 0, 1, 3)),
                  "s_in": s_in, "idx_ab": idx, "vis": vis})
        m.update(consts)
        maps.append(m)
    return maps


def kernel(**inp):
    nc = build(mode="fused")
    res = run_bass_kernel_spmd(nc, make_fused_maps(inp), core_ids=list(range(NCORES)))
    r = res.results
    kall = r[0]["k_out"].reshape(8, TT, 128).transpose(1, 0, 2)
    vall = r[0]["v_out"].reshape(8, TT, 128).transpose(1, 0, 2)
    sall = r[0]["s_out"].reshape(8, 17, 128, 128).transpose(1, 0, 2, 3)
    y = [r[c]["y_own"] for c in range(NCORES)]
    y_prompt = np.concatenate([yy[:OWN_P] for yy in y], axis=0).reshape(1, TP, D)
    y_sample = np.concatenate([yy[OWN_P:] for yy in y], axis=0).reshape(NB, TSQ, D)
    new_k_prompt = kall[:TP].reshape(1, 1, TP, 8, 128)
    new_v_prompt = vall[:TP].reshape(1, 1, TP, 8, 128)
    new_k_sample = kall[TP:].reshape(1, NB, TSQ, 8, 128)
    new_v_sample = vall[TP:].reshape(1, NB, TSQ, 8, 128)
    new_s_prompt = sall[0].reshape(1, 1, 8, 128, 128)
    new_s_sample = sall[1:].reshape(1, NB, 8, 128, 128)
    c32 = lambda a: np.ascontiguousarray(a, dtype=np.float32)
    return (c32(y_prompt), c32(y_sample), c32(new_k_prompt), c32(new_v_prompt), c32(new_s_prompt),
            c32(new_k_sample), c32(new_v_sample), c32(new_s_sample))
```

```python
import numpy as np
from contextlib import ExitStack
import concourse.bass as bass
import concourse.mybir as mybir
from concourse.bass_utils import run_bass_kernel_spmd

F32 = mybir.dt.float32
BF16 = mybir.dt.bfloat16
AF = mybir.ActivationFunctionType
ALU = mybir.AluOpType

NCORES = 8
D = 2048
KC = D // 128
TP = 16384
NB = 16
TSQ = 64
TS = NB * TSQ
TT = TP + TS
CH = 512
NCH_P = TP // CH
NCH = TT // CH
PAST = 4096
EPS = 1e-6
SEM_CHUNK = 16000
NSLOT = 8
OWN_P = TP // NCORES
OWN_S = TS // NCORES
OWN = OWN_P + OWN_S


def own_chunk(c, s):
    return [c, 15 - c, 16 + c, 31 - c][s]


class Buf:
    __slots__ = ("name", "w", "r", "psum")

    def __init__(self, name, psum=False):
        self.name = name
        self.w = None
        self.r = {}
        self.psum = psum


class Q:
    def __init__(self, name):
        self.name = name
        self.ops = []
        self.n = 0
        self.waited = {}
        self.maxchunk = {}
        self.dma_k = 0
        self.slot_tot = [0] * NSLOT


class Prog:
    def __init__(self):
        self.q = {n: Q(n) for n in ("pe", "act", "dve", "pool", "sp")}
        self.keys = []
        self.keyset = set()

    def _key(self, key):
        if key not in self.keyset:
            self.keyset.add(key)
            self.keys.append(key)

    def _deps(self, q, reads, writes, extra):
        deps = {}

        def add(t):
            if t is None:
                return
            k, v = t
            if deps.get(k, 0) < v:
                deps[k] = v

        for b in reads:
            add(b.w)
            if b.psum:
                for rk, t in b.r.items():
                    if rk != q.name:
                        add(t)
        for b in writes:
            add(b.w)
            for t in b.r.values():
                add(t)
        for t in extra:
            add(t)
        waits = []
        for key, val in deps.items():
            if q.name == "pe" and key[0] == "pe":
                continue
            if isinstance(key[1], int) and key[0] in ("pe", "act", "dve", "pool", "sp"):
                mc = q.maxchunk.get(key[0], -1)
                if key[1] < mc:
                    continue
                if key[1] > mc:
                    q.maxchunk[key[0]] = key[1]
            if q.waited.get(key, 0) >= val:
                continue
            q.waited[key] = val
            waits.append((key, val))
        return waits

    def _mark(self, tok, reads, writes, rkey):
        for b in writes:
            b.w = tok
            b.r = {}
        for b in reads:
            b.r[rkey] = tok

    def op(self, qn, meth, kw, reads=(), writes=(), extra=()):
        fn = (meth, kw)
        q = self.q[qn]
        waits = self._deps(q, reads, writes, extra)
        idx = q.n
        q.n += 1
        key = (qn, idx // SEM_CHUNK)
        self._key(key)
        tok = (key, idx % SEM_CHUNK + 1)
        q.ops.append((waits, fn, (key, 1)))
        self._mark(tok, reads, writes, qn)
        return tok

    def dma(self, qn, kw, reads=(), writes=(), extra=(), meth="dma_start"):
        fn = (meth, kw)
        q = self.q[qn]
        slot = q.dma_k % NSLOT
        q.dma_k += 1
        key = (qn + "_dma", slot)
        self._key(key)
        prev = q.slot_tot[slot]
        ex = list(extra)
        if prev > 0:
            ex.append((key, prev))
        waits = self._deps(q, reads, writes, ex)
        q.slot_tot[slot] = prev + 16
        tok = (key, prev + 16)
        q.ops.append((waits, fn, (key, 16)))
        self._mark(tok, reads, writes, key)
        return tok

    def barrier(self):
        toks = []
        for q in self.q.values():
            for sl in range(NSLOT):
                if q.slot_tot[sl] > 0:
                    toks.append(((q.name + "_dma", sl), q.slot_tot[sl]))
            if q.n > 0:
                idx = q.n - 1
                toks.append(((q.name, idx // SEM_CHUNK), idx % SEM_CHUNK + 1))
        for q in self.q.values():
            waits = self._deps(q, (), (), toks)
            if waits:
                q.ops.append((waits, None, None))

    def finish(self):
        ex = []
        for q in self.q.values():
            for s in range(NSLOT):
                if q.slot_tot[s] > 0:
                    ex.append(((q.name + "_dma", s), q.slot_tot[s]))
            if q.n > 0 and q.name != "sp":
                idx = q.n - 1
                ex.append(((q.name, idx // SEM_CHUNK), idx % SEM_CHUNK + 1))
        q = self.q["sp"]
        waits = self._deps(q, (), (), ex)
        q.ops.append((waits, None, None))

    def emit(self, nc, es):
        sems = {}
        for key in self.keys:
            sems[key] = es.enter_context(nc.semaphore("s_%s_%s" % (key[0], key[1])))
        with nc.Block() as block:
            self._emit_block(block, sems)

    def _emit_block(self, block, sems):

        def run(q):
            def body(e):
                for waits, fn, inc in q.ops:
                    for key, val in waits:
                        e.wait_ge(sems[key], val)
                    if fn is None:
                        continue
                    ins = getattr(e, fn[0])(**fn[1])
                    ins.then_inc(sems[inc[0]], inc[1])
            return body

        block.tensor(run(self.q["pe"]))
        block.scalar(run(self.q["act"]))
        block.vector(run(self.q["dve"]))
        block.gpsimd(run(self.q["pool"]))
        block.sync(run(self.q["sp"]))


def build(mode="mix", chunks=None, x_rows=TT, stop=None, x_base=0, flags=("att", "hg"), debug=False):
    chunks = list(range(NCH)) if chunks is None else chunks
    if mode == "out":
        chunks = []
    do_mix = mode in ("mix", "fused")
    do_out = mode in ("out", "fused")
    NR = {"mix": 17, "out": 3, "fused": 19}[mode]
    RS0 = NR - 2
    if do_mix:
        debug = True
    nc = bass.Bass("TRN2", target_bir_lowering=False)
    es = ExitStack()
    P = Prog()

    def dram_in(name, shape, dt=F32):
        return nc.dram_tensor(name, list(shape), dt, kind="ExternalInput").ap()

    def dram_out(name, shape, dt=F32):
        return nc.dram_tensor(name, list(shape), dt, kind="ExternalOutput").ap()

    def sb(name, shape, dt):
        return es.enter_context(nc.sbuf_tensor(name, list(shape), dt))

    c_all = dram_in("c_all", [NR, D])
    w_ada = dram_in("w_ada", [D, 3 * D])
    b_adaT = dram_in("b_adaT", [128, 48])
    ngT = dram_in("ngT", [128, KC])
    identf_d = dram_in("identf", [128, 128])
    if do_mix:
        NH = 8 if mode == "fused" else 1
        HD = {"h": 0}
        x_all = dram_in("x_all", [x_rows, D])
        w_head = dram_in("w_head", [NH, D, 1024])
        qkg = dram_in("qkg", [128, 2])
        k_out = dram_out("k_out", [NH, TT, 128])
        v_out = dram_out("v_out", [NH, TT, 128])
        s_out = dram_out("s_out", [NH, 17, 128, 128])
        trin_d = dram_in("trin", [128, 128])
        tric_d = dram_in("tric", [128, 128])
        maskp_d = dram_in("maskp", [128, 4, 512])
        masks_d = dram_in("masks", [128, 2, 64])
        hmask_d = dram_in("hmask", [128, 128])
        resetm_d = dram_in("resetm", [128, 512])
        lbr_d = dram_in("lbr", [NH, 128, 2])
        ogain_d = dram_in("ogain", [NH, 128, 1])
        pmask_d = dram_in("pmask", [128, 2])
        if mode == "fused":
            cache_k = dram_in("cache_k", [NH, 2, PAST, 128])
            cache_v = dram_in("cache_v", [NH, 2, PAST, 128])
            vis_d = dram_in("vis", [128, 512])
            a_own_scr = nc.dram_tensor("a_own_scr", [8, 17, 128, 128], BF16)
        else:
            cache_k = dram_in("cache_k", [NH, NB, PAST, 128])
            cache_v = dram_in("cache_v", [NH, NB, PAST, 128])
        s_in = dram_in("s_in", [NH, NB, 128, 128])
        if mode == "mix":
            dbg_a = dram_out("dbg_a", [128, TT])
            dbg_b = dram_out("dbg_b", [128, TT])
        else:
            ab_scr = nc.dram_tensor("ab_scr", [2 * 8 * 136 * 128, 128], BF16)
            idx_ab_d = nc.dram_tensor("idx_ab", [128, 272], mybir.dt.int32, kind="ExternalInput").ap()
    if do_out:
        x_own = dram_in("x_own", [OWN, D])
        w_gate = dram_in("w_gate", [D, 2 * D])
        w_bsb = dram_in("w_bsb", [1024, D])
        w_bhg = dram_in("w_bhg", [1024, D])
        w_o = dram_in("w_o", [D, D])
        b_gate_bc = dram_in("b_gate_bc", [128, D])
        y_own = dram_out("y_own", [OWN, D])
        if mode == "out":
            ab_own = dram_in("ab_own", [2, 8, 128, OWN])

    es1 = ExitStack()

    def sb1(name, shape, dt):
        return es1.enter_context(nc.sbuf_tensor(name, list(shape), dt))

    ident_f = sb("ident_f", [128, 128], F32)
    ident_b = sb("ident_b", [128, 128], BF16)
    ones_b = sb("ones_b", [128, 128], BF16)
    csT = sb("csT", [128, KC, NR], BF16)
    gm = sb("gm", [128, KC, NR], F32)
    sh = sb("sh", [128, KC, NR], F32)
    badaT = sb("badaT", [128, 48], F32)
    ngTs = sb("ngTs", [128, KC], F32)
    epsb = sb("epsb", [128, 1], F32)
    xbuf = [sb("xbuf%d" % i, [128, D], F32) for i in range(2)]
    xn = sb("xn", [128, D], BF16)
    junk = xn
    stat = [sb("stat%d" % i, [128, 4], F32) for i in range(2)]
    hT = sb("hT", [128, KC, CH], BF16)
    if do_mix:
        wh = sb1("wh", [128, KC, 1024], BF16)
        qkgs = sb1("qkgs", [128, 2], F32)
        qkgs2 = sb1("qkgs2", [128, 2], F32)
        sq_b = sb1("sq_b", [128, CH], BF16)
        rstd_f = sb1("rstd_f", [128, CH], F32)
        kn_f = sb1("kn_f", [128, CH], F32)
        vT_f = sb1("vT_f", [128, CH], F32)
        KT = sb1("KT", [128, TP], BF16)
        Vr = sb1("Vr", [128, TP // 128, 128], BF16)
        QT = sb1("QT", [128, CH], BF16)
    es0 = ExitStack()
    c_sb = es0.enter_context(nc.sbuf_tensor("c_sb", [NR, D], F32))
    cs_sb = es0.enter_context(nc.sbuf_tensor("cs_sb", [NR, D], F32))
    wada = [es0.enter_context(nc.sbuf_tensor("wada%d" % i, [128, KC, 512], BF16)) for i in range(2)]
    modsb = es0.enter_context(nc.sbuf_tensor("modsb", [128, 48, NR], F32))

    ps = [es.enter_context(nc.psum_tensor("ps%d" % i, [128, 512], F32)) for i in range(8)]
    psb = [Buf("ps%d" % i, psum=True) for i in range(8)]

    B = {}

    def bf(name):
        if name not in B:
            B[name] = Buf(name)
        return B[name]

    def done():
        P.finish()
        P.emit(nc, es)
        es.close()
        return nc
    P.dma("sp", dict(out=ident_f[:], in_=identf_d[:, :]), writes=[bf("ident_f")])
    P.dma("sp", dict(out=badaT[:], in_=b_adaT[:, :]), writes=[bf("badaT")])
    P.dma("sp", dict(out=ngTs[:], in_=ngT[:, :]), writes=[bf("ngTs")])
    P.dma("sp", dict(out=c_sb[:], in_=c_all[:, :]), writes=[bf("c_sb")])
    P.op("dve", "tensor_copy", dict(out=ident_b[:], in_=ident_f[:]), reads=[bf("ident_f")], writes=[bf("ident_b")])
    P.op("dve", "memset", dict(ap=ones_b[:], constant=1.0), writes=[bf("ones_b")])
    P.op("dve", "memset", dict(ap=epsb[:], constant=EPS), writes=[bf("epsb")])
    if do_mix:
        P.dma("sp", dict(out=qkgs[:], in_=qkg[:, :]), writes=[bf("qkgs")])
        P.op("dve", "tensor_scalar", dict(out=qkgs2[:, 0:1], in0=qkgs[:, 0:1], scalar1=float(128 ** -0.5),
                                          scalar2=None, op0=ALU.mult), reads=[bf("qkgs")], writes=[bf("qkgs2a")])
        P.op("dve", "tensor_copy", dict(out=qkgs2[:, 1:2], in_=qkgs[:, 1:2]), reads=[bf("qkgs")], writes=[bf("qkgs2b")])

    P.op("act", "activation", dict(out=cs_sb[:], in_=c_sb[:], func=AF.Silu), reads=[bf("c_sb")], writes=[bf("cs_sb")])
    pT = ps[0]
    for k in range(KC):
        P.op("pe", "transpose", dict(out=pT[0:128, k * NR:(k + 1) * NR], in_=cs_sb[:, k * 128:(k + 1) * 128],
                                     identity=ident_f[0:NR, 0:NR]),
             reads=[bf("cs_sb"), bf("ident_f")], writes=[psb[0]])
    P.op("dve", "tensor_copy", dict(out=csT[:].rearrange("p k r -> p (k r)"), in_=pT[:, 0:KC * NR]),
         reads=[psb[0]], writes=[bf("csT")])
    modps = [ps[1], ps[2]]
    for g in range(12):
        wb = wada[g % 2]
        wbuf = bf("wada%d" % (g % 2))
        P.dma("pool", dict(out=wb[:], in_=w_ada[:, g * 512:(g + 1) * 512].rearrange("(k p) c -> p k c", p=128)),
              writes=[wbuf])
        for cc in range(4):
            j = g * 4 + cc
            dst = modps[j // 24]
            dbuf = psb[1 + j // 24]
            jj = j % 24
            for k in range(KC):
                P.op("pe", "matmul", dict(out=dst[:, jj * NR:(jj + 1) * NR], lhsT=wb[:, k, cc * 128:(cc + 1) * 128],
                                          rhs=csT[:, k, :], start=(k == 0), stop=(k == KC - 1)),
                     reads=[wbuf, bf("csT")], writes=[dbuf])
    for half in range(2):
        P.op("dve", "tensor_tensor", dict(
            out=modsb[:, half * 24:(half + 1) * 24, :],
            in0=modps[half][:, 0:24 * NR].rearrange("p (j r) -> p j r", r=NR),
            in1=badaT[:, half * 24:(half + 1) * 24].unsqueeze(2).to_broadcast([128, 24, NR]), op=ALU.add),
            reads=[psb[1 + half], bf("badaT")], writes=[bf("modsb%d" % half)])
    P.op("dve", "tensor_copy", dict(out=sh[:], in_=modsb[:, 0:16, :]), reads=[bf("modsb0")], writes=[bf("sh")])
    P.op("dve", "tensor_scalar", dict(out=gm[:], in0=modsb[:, 16:32, :], scalar1=1.0, scalar2=None, op0=ALU.add),
         reads=[bf("modsb0"), bf("modsb1")], writes=[bf("gm")])
    P.op("dve", "tensor_tensor", dict(out=gm[:], in0=gm[:], in1=ngTs[:].unsqueeze(2).to_broadcast([128, KC, NR]),
                                      op=ALU.mult), reads=[bf("gm"), bf("ngTs")], writes=[bf("gm")])

    if stop == "p0":
        return done()
    P.barrier()
    es0.close()
    if do_mix:
        trin = sb1("trin_s", [128, 128], BF16)
        tric = sb1("tric_s", [128, 128], BF16)
        maskp = sb1("maskp_s", [128, 4, 512], BF16)
        masks = sb1("masks_s", [128, 2, 64], BF16)
        hmask = sb1("hmask_s", [128, 128], F32)
        resetm = sb1("resetm_s", [128, 512], F32)
        lbr = sb1("lbr_s", [128, 4], F32)
        ogain = sb1("ogain_s", [128, 1], F32)
        oneb = sb1("oneb", [128, 1], F32)
        zs_sb = sb1("zs_sb", [128, CH], F32)
        e_sb = [sq_b, sb1("e_sb1", [128, CH], BF16), sb1("e_sb2", [128, CH], BF16)]
        sp_sb = [sb1("sp_sb%d" % i, [128, CH], BF16) for i in range(3)]
        x_sb = [sb1("x_sb%d" % i, [128, CH], BF16) for i in range(2)]
        w_sb = [sb1("w_sb%d" % i, [128, CH], BF16) for i in range(2)]
        bT = sb1("bT", [128, CH], BF16)
        aT = bT
        ksT = sb1("ksT", [128, CH], BF16)
        vs_bf = sb1("vs_bf", [128, 4, 128], BF16)
        t_f = sb1("t_f", [128, CH], F32)
        t_k = sb1("t_k", [128, CH], F32)
        t_lf = sb1("t_lf", [128, CH], F32)
        t_b = sb1("t_b", [128, CH], F32)
        t_eb = sb1("t_eb", [128, CH], F32)
        t_enb = sb1("t_enb", [128, CH], F32)
        t_qs = sb1("t_qs", [128, CH], F32)
        t_ke = sb1("t_ke", [128, CH], F32)
        kv_o = [t_ke[:].rearrange("p (t d) -> p t d", d=128)]
        t_iT = sb1("t_iT", [128, CH], F32)
        t_zh = sb1("t_zh", [128, CH], F32)
        dec = sb1("dec", [128, 8], F32)
        iv = sb1("iv", [128, 4, 128], F32)
        ket = sb1("ket", [128, 2, 4, 128], F32)
        pmask = sb1("pmask_s", [128, 2], F32)
        if mode == "fused":
            vis = sb1("vis_s", [128, 512], F32)
        attm = sb1("attm", [128, 4, 128], F32)
        Sset = [sb1("Sset%d" % i, [128, 8, 128], F32) for i in range(2)]
        sin = sb1("sin", [128, 8, 128], F32)
        dbg_t = t_lf
        P.dma("pool", dict(out=trin[:], in_=trin_d[:, :]), writes=[bf("trin")])
        P.dma("pool", dict(out=tric[:], in_=tric_d[:, :]), writes=[bf("tric")])
        P.dma("pool", dict(out=maskp[:], in_=maskp_d[:, :, :]), writes=[bf("maskp")])
        P.dma("pool", dict(out=masks[:], in_=masks_d[:, :, :]), writes=[bf("masks")])
        P.dma("sp", dict(out=hmask[:], in_=hmask_d[:, :]), writes=[bf("hmask")])
        P.dma("sp", dict(out=resetm[:], in_=resetm_d[:, :]), writes=[bf("resetm")])
        P.dma("sp", dict(out=pmask[:], in_=pmask_d[:, :]), writes=[bf("pmask")])
        if mode == "fused":
            P.dma("sp", dict(out=vis[:], in_=vis_d[:, :]), writes=[bf("vis")])
        P.op("dve", "memset", dict(ap=oneb[:], constant=1.0), writes=[bf("oneb")])

        def head_setup():
            P.barrier()
            P.dma("pool", dict(out=wh[:], in_=w_head[HD["h"]].rearrange("(k p) c -> p k c", p=128)), writes=[bf("wh")])
            P.dma("sp", dict(out=lbr[:, 0:2], in_=lbr_d[HD["h"]]), writes=[bf("lbr")])
            P.dma("sp", dict(out=ogain[:], in_=ogain_d[HD["h"]]), writes=[bf("ogain")])
            P.op("dve", "memset", dict(ap=Sset[(mixers.n + 1) % 2][:, 7, :], constant=0.0), writes=[bf("S_%d_7" % ((mixers.n + 1) % 2))])
            P.op("dve", "tensor_tensor", dict(out=lbr[:, 2:3], in0=lbr[:, 0:1], in1=lbr[:, 1:2], op=ALU.subtract),
                 reads=[bf("lbr")], writes=[bf("lbr")])
            P.op("act", "activation", dict(out=lbr[:, 2:3], in_=lbr[:, 2:3], func=AF.Sigmoid), reads=[bf("lbr")], writes=[bf("lbr")])
            P.op("dve", "tensor_scalar", dict(out=lbr[:, 3:4], in0=lbr[:, 2:3], scalar1=-1.0, scalar2=1.0, op0=ALU.mult, op1=ALU.add),
                 reads=[bf("lbr")], writes=[bf("lbr")])
            sample_ready["done"] = False
    state = {"tile_i": 0, "tro": 0}

    def proj(j, dst, dbuf, hTb):
        for k in range(KC):
            P.op("pe", "matmul", dict(out=dst[:, :], lhsT=wh[:, k, j * 128:(j + 1) * 128], rhs=hT[:, k, :],
                                      start=(k == 0), stop=(k == KC - 1)),
                 reads=[bf("wh")] + hTb[k], writes=[dbuf])

    def headnorm(src, sbuf_, gain_ap, gbufs, out_ap, outbuf):
        P.op("act", "activation", dict(out=sq_b[:], in_=src[:, :], func=AF.Square), reads=[sbuf_], writes=[bf("e_sb0")])
        P.op("pe", "matmul", dict(out=ps[4][:, :], lhsT=ones_b[:], rhs=sq_b[:], start=True, stop=True),
             reads=[bf("ones_b"), bf("e_sb0")], writes=[psb[4]])
        P.op("act", "activation", dict(out=rstd_f[:], in_=ps[4][:, :], func=AF.Ln, scale=1.0 / 128, bias=epsb[:, 0:1]),
             reads=[psb[4], bf("epsb")], writes=[bf("rstd_f")])
        P.op("act", "activation", dict(out=rstd_f[:], in_=rstd_f[:], func=AF.Exp, scale=-0.5),
             reads=[bf("rstd_f")], writes=[bf("rstd_f")])
        P.op("dve", "scalar_tensor_tensor", dict(out=out_ap, in0=src[:, :], scalar=gain_ap, in1=rstd_f[:],
                                                 op0=ALU.mult, op1=ALU.mult),
             reads=[sbuf_, bf("rstd_f")] + gbufs, writes=[outbuf])

    def attention(nq, q_ap, qbufs, blocks, o_ap, c0):
        n = len(blocks)
        zb = [ps[0], ps[1]]
        A = ps[6]

        def stage1(j):
            blk = blocks[j]
            z = zb[j % 2]
            zbuf = psb[j % 2]
            eb_, ebuf = e_sb[j % 3], bf("e_sb%d" % (j % 3))
            sb_, sbuf_ = sp_sb[j % 3], bf("sp_sb%d" % (j % 3))
            has_mask = blk.get("mask") is not None
            P.op("pe", "matmul", dict(out=z[:, 0:nq], lhsT=blk["kT"], rhs=q_ap, start=True, stop=not has_mask),
                 reads=blk["kbufs"] + qbufs, writes=[zbuf])
            if has_mask:
                P.op("pe", "matmul", dict(out=z[:, 0:nq], lhsT=ident_b[:], rhs=blk["mask"], start=False, stop=True),
                     reads=[bf("ident_b")] + blk["mbufs"], writes=[zbuf])
            if blk.get("bias") is not None:
                P.op("act", "activation", dict(out=eb_[:, 0:nq], in_=z[:, 0:nq], func=AF.Exp, bias=blk["bias"]),
                     reads=[zbuf, bf("vis")], writes=[ebuf])
            else:
                P.op("act", "activation", dict(out=eb_[:, 0:nq], in_=z[:, 0:nq], func=AF.Exp), reads=[zbuf], writes=[ebuf])
            P.op("act", "activation", dict(out=sb_[:, 0:nq], in_=eb_[:, 0:nq], func=AF.Ln, bias=oneb[:, 0:1]),
                 reads=[ebuf, bf("oneb")], writes=[sbuf_])

        def o_mm(j):
            blk = blocks[j]
            P.op("pe", "matmul", dict(out=o_ap, lhsT=blk["v"], rhs=w_sb[j % 2][:, 0:nq], start=(j == 0), stop=(j == n - 1)),
                 reads=blk["vbufs"] + [bf("w_sb%d" % (j % 2))], writes=[psb[7]])

        stage1(0)
        if n > 1:
            stage1(1)
        for j in range(n):
            spj, spbuf = sp_sb[j % 3], bf("sp_sb%d" % (j % 3))
            P.op("pe", "matmul", dict(out=A[:, 0:nq], lhsT=trin[:], rhs=spj[:, 0:nq], start=(j == 0), stop=True, skip_group_check=(j > 0)),
                 reads=[bf("trin"), spbuf], writes=[psb[6]])
            P.op("act", "activation", dict(out=x_sb[j % 2][:, 0:nq], in_=A[:, 0:nq], func=AF.Exp),
                 reads=[psb[6]], writes=[bf("x_sb%d" % (j % 2))])
            if j + 2 < n:
                stage1(j + 2)
            if j >= 1:
                o_mm(j - 1)
            if j < n - 1:
                P.op("pe", "matmul", dict(out=A[:, 0:nq], lhsT=tric[:], rhs=spj[:, 0:nq], start=False, stop=True, skip_group_check=True),
                     reads=[bf("tric"), spbuf], writes=[psb[6]])
            P.op("dve", "tensor_tensor", dict(out=w_sb[j % 2][:, 0:nq], in0=e_sb[j % 3][:, 0:nq], in1=x_sb[j % 2][:, 0:nq], op=ALU.mult),
                 reads=[bf("e_sb%d" % (j % 3)), bf("x_sb%d" % (j % 2))], writes=[bf("w_sb%d" % (j % 2))])
        o_mm(n - 1)

    def tr_out(ch, srcT, srcbuf, dram, extra_copy=None):
        pt = ps[5]
        for tt in range(4):
            P.op("pe", "transpose", dict(out=pt[:, tt * 128:(tt + 1) * 128], in_=srcT[:, tt * 128:(tt + 1) * 128],
                                         identity=ident_f[:]),
                 reads=[srcbuf, bf("ident_f")], writes=[psb[5]])
        i = 0
        ko = kv_o[i]
        kob = bf("t_ke")
        P.op("act", "activation", dict(out=ko[:].rearrange("p t d -> p (t d)"), in_=pt[:, :], func=AF.Copy),
             reads=[psb[5]], writes=[kob])
        if extra_copy is not None:
            extra_copy(pt, psb[5])
        P.dma("sp", dict(out=dram[ch * CH:(ch + 1) * CH, :].rearrange("(t p) d -> p t d", p=128), in_=ko[:]),
              reads=[kob])

    sample_ready = {"done": False}

    def emit_ab(ch, which, src_bf, srcbuf):
        if mode != "fused":
            return
        r0 = ((which * 8 + HD["h"]) * 136 + ch * 4) * 128
        P.dma("sp", dict(out=ab_scr.ap()[r0:r0 + 512, :].rearrange("(t p) c -> p t c", p=128),
                         in_=src_bf[:].rearrange("p (t c) -> p t c", c=128)), reads=[srcbuf])

    def load_cache(b, src_k=None, src_v=None):
        i = b % 2
        kc_raw = KT[:, 8192 + i * 4096: 8192 + (i + 1) * 4096].rearrange("p (j d) -> p j d", d=128)
        vc = Vr[:, i * 32:(i + 1) * 32, :]
        kcT = KT[:, i * 4096:(i + 1) * 4096]
        P.dma("pool", dict(out=kc_raw, in_=(cache_k[HD["h"], b] if src_k is None else src_k).rearrange("(j p) d -> p j d", p=128)), writes=[bf("kc_raw%d" % i)])
        P.dma("pool", dict(out=vc, in_=(cache_v[HD["h"], b] if src_v is None else src_v).rearrange("(j p) d -> p j d", p=128)), writes=[bf("vc%d" % i)])
        for g in range(4):
            pb = 2 + (g % 2)
            tp = ps[pb][:, :].bitcast(BF16)
            for jj in range(8):
                j = g * 8 + jj
                P.op("pe", "transpose", dict(out=tp[:, jj * 128:(jj + 1) * 128], in_=kc_raw[:, j, :], identity=ident_b[:]),
                     reads=[bf("kc_raw%d" % i), bf("ident_b")], writes=[psb[pb]])
            if g % 2 == 0:
                P.op("act", "activation", dict(out=kcT[:, g * 1024:(g + 1) * 1024], in_=tp[:, :], func=AF.Copy),
                     reads=[psb[pb]], writes=[bf("kcT%d_%d" % (i, g))])
            else:
                P.op("dve", "tensor_copy", dict(out=kcT[:, g * 1024:(g + 1) * 1024], in_=tp[:, :]),
                     reads=[psb[pb]], writes=[bf("kcT%d_%d" % (i, g))])

    def mixers(ch, hTb):
        n_local = mixers.n
        mixers.n += 1
        is_p = ch < NCH_P
        if mode != "fused":
            proj(3, ps[3], psb[3], hTb)
            P.op("act", "activation", dict(out=zs_sb[:], in_=ps[3][:, :], func=AF.Silu), reads=[psb[3]], writes=[bf("zs_sb")])
        if "att" in flags and mode != "fused":
            if is_p:
                blocks = []
                for j in range(4 * ch + 3, -1, -1):
                    blk = dict(kT=KT[:, j * 128:(j + 1) * 128], kbufs=[bf("KT%d" % (j // 4))],
                               v=Vr[:, j, :], vbufs=[bf("Vr%d" % (j // 4))])
                    if j >= 4 * ch:
                        blk["mask"] = maskp[:, j - 4 * ch, :]
                        blk["mbufs"] = [bf("maskp")]
                    blocks.append(blk)
                attention(CH, QT[:], [bf("QT")], blocks, ps[7][:, :], 0)
            else:
                if not sample_ready["done"]:
                    sample_ready["done"] = True
                    P.barrier()
                    load_cache((ch - NCH_P) * 8)
                for c in range(8):
                    b = (ch - NCH_P) * 8 + c
                    i = b % 2
                    if b + 1 < NB:
                        load_cache(b + 1)
                    tt, par = c // 2, c % 2
                    kcT = KT[:, i * 4096:(i + 1) * 4096]
                    vc = Vr[:, i * 32:(i + 1) * 32, :]
                    blocks = [dict(kT=ksT[:, tt * 128:(tt + 1) * 128], kbufs=[bf("ksT")], v=vs_bf[:, tt, :], vbufs=[bf("vs_bf")],
                                   mask=masks[:, par, :], mbufs=[bf("masks")])]
                    for j in range(31, -1, -1):
                        blocks.append(dict(kT=kcT[:, j * 128:(j + 1) * 128], kbufs=[bf("kcT%d_%d" % (i, j // 8))],
                                           v=vc[:, j, :], vbufs=[bf("vc%d" % i)]))
                    attention(TSQ, QT[:, c * TSQ:(c + 1) * TSQ], [bf("QT")], blocks, ps[7][:, c * TSQ:(c + 1) * TSQ], c * TSQ)
            P.op("dve", "tensor_tensor", dict(out=aT[:], in0=ps[7][:, :], in1=zs_sb[:], op=ALU.mult),
                 reads=[psb[7], bf("zs_sb")], writes=[bf("bT")])
            emit_ab(ch, 0, aT, bf("bT"))
            if mode == "mix":
                P.op("dve", "tensor_copy", dict(out=dbg_t[:], in_=aT[:]), reads=[bf("bT")], writes=[bf("t_lf")])
                P.dma("sp", dict(out=dbg_a[:, ch * CH:(ch + 1) * CH], in_=dbg_t[:]), reads=[bf("t_lf")])
        if "hg" not in flags:
            return
        if not is_p:
            b0_ = (ch - NCH_P) * 8
            P.dma("sp", dict(out=sin[:], in_=s_in[HD["h"], b0_:b0_ + 8].rearrange("b k v -> k b v")), writes=[bf("sin")])
        proj(4, ps[2], psb[2], hTb)
        P.op("act", "activation", dict(out=t_f[:], in_=ps[2][:, :], func=AF.Sigmoid), reads=[psb[2]], writes=[bf("t_f")])
        proj(6, ps[3], psb[3], hTb)
        P.op("act", "activation", dict(out=t_qs[:], in_=ps[3][:, :], func=AF.Silu), reads=[psb[3]], writes=[bf("t_qs")])
        proj(7, ps[2], psb[2], hTb)
        P.op("act", "activation", dict(out=t_zh[:], in_=ps[2][:, :], func=AF.Silu), reads=[psb[2]], writes=[bf("t_zh")])
        proj(5, ps[3], psb[3], hTb)
        P.op("act", "activation", dict(out=t_iT[:], in_=ps[3][:, :], func=AF.Copy), reads=[psb[3]], writes=[bf("t_iT")])
        P.op("dve", "tensor_scalar", dict(out=t_f[:], in0=t_f[:], scalar1=lbr[:, 3:4], scalar2=lbr[:, 2:3], op0=ALU.mult, op1=ALU.add),
             reads=[bf("t_f"), bf("lbr")], writes=[bf("t_f")])
        P.op("act", "activation", dict(out=t_lf[:], in_=t_f[:], func=AF.Ln), reads=[bf("t_f")], writes=[bf("t_lf")])
        P.op("dve", "tensor_scalar", dict(out=t_k[:], in0=t_f[:], scalar1=-1.0, scalar2=1.0, op0=ALU.mult, op1=ALU.add),
             reads=[bf("t_f")], writes=[bf("t_k")])
        P.op("dve", "tensor_tensor_scan", dict(out=t_b[:], data0=resetm[:], data1=t_lf[:], initial=0.0, op0=ALU.mult, op1=ALU.add),
             reads=[bf("resetm"), bf("t_lf")], writes=[bf("t_b")])
        P.op("act", "activation", dict(out=t_eb[:], in_=t_b[:], func=AF.Exp), reads=[bf("t_b")], writes=[bf("t_eb")])
        P.op("act", "activation", dict(out=t_enb[:], in_=t_b[:], func=AF.Exp, scale=-1.0), reads=[bf("t_b")], writes=[bf("t_enb")])
        P.op("act", "activation", dict(out=dec[:].unsqueeze(2), in_=t_b[:].rearrange("p (c t) -> p c t", t=64)[:, :, 63:64], func=AF.Exp),
             reads=[bf("t_b")], writes=[bf("dec")])
        P.op("dve", "tensor_tensor", dict(out=t_eb[:], in0=t_qs[:], in1=t_eb[:], op=ALU.mult),
             reads=[bf("t_qs"), bf("t_eb")], writes=[bf("t_eb")])
        P.op("dve", "tensor_tensor", dict(out=t_enb[:], in0=t_k[:], in1=t_enb[:], op=ALU.mult),
             reads=[bf("t_k"), bf("t_enb")], writes=[bf("t_enb")])
        P.op("dve", "tensor_tensor", dict(out=t_ke[:].rearrange("p (c t) -> p c t", t=64),
                                          in0=t_enb[:].rearrange("p (c t) -> p c t", t=64),
                                          in1=dec[:].unsqueeze(2).to_broadcast([128, 8, 64]), op=ALU.mult),
             reads=[bf("t_enb"), bf("dec")], writes=[bf("t_ke")])
        hgs = [int(f[3:]) for f in flags if f.startswith("hgs")]
        hgs = hgs[0] if hgs else 99
        if hgs <= 1:
            return
        for which in range(2):
            src, sbuf_ = (t_iT, bf("t_iT")) if which == 0 else (t_ke, bf("t_ke"))
            for tt in range(4):
                P.op("pe", "transpose", dict(out=ps[5][:, tt * 128:(tt + 1) * 128], in_=src[:, tt * 128:(tt + 1) * 128], identity=ident_f[:]),
                     reads=[sbuf_, bf("ident_f")], writes=[psb[5]])
            if which == 0:
                P.op("act", "activation", dict(out=iv[:].rearrange("p t d -> p (t d)"), in_=ps[5][:, :], func=AF.Copy),
                     reads=[psb[5]], writes=[bf("iv")])
            else:
                for ab in range(2):
                    P.op("act", "activation", dict(out=ket[:, ab, :, :].rearrange("p t d -> p (t d)"), in_=ps[5][:, :], func=AF.Copy,
                                                   scale=pmask[:, ab:ab + 1]),
                         reads=[psb[5], bf("pmask")], writes=[bf("ket%d" % ab)])
        if hgs <= 2:
            return
        for c in range(8):
            pb = c // 4
            r0 = (c % 2) * 64
            P.op("pe", "matmul", dict(out=ps[pb][:, (c % 4) * 128:(c % 4 + 1) * 128], lhsT=ket[:, c % 2, c // 2, :],
                                      rhs=iv[:, c // 2, :], start=True, stop=True),
                 reads=[bf("ket%d" % (c % 2)), bf("iv")], writes=[psb[pb]])
        if hgs <= 3:
            return
        for p_ in range(4):
            P.op("pe", "matmul", dict(out=ps[4][:, p_ * 128:(p_ + 1) * 128], lhsT=t_enb[:, p_ * 128:(p_ + 1) * 128],
                                      rhs=t_eb[:, p_ * 128:(p_ + 1) * 128], start=True, stop=True),
                 reads=[bf("t_enb"), bf("t_eb")], writes=[psb[4]])
        P.op("dve", "tensor_tensor", dict(out=attm[:], in0=ps[4][:, :].rearrange("p (a t) -> p a t", t=128),
                                          in1=hmask[:].unsqueeze(1).to_broadcast([128, 4, 128]), op=ALU.mult),
             reads=[psb[4], bf("hmask")], writes=[bf("attm")])
        if hgs <= 4:
            return
        cur = Sset[n_local % 2]
        prev_set = Sset[(n_local + 1) % 2]
        Sprev = []
        for c in range(8):
            if is_p:
                if c == 0:
                    sp_ap, sp_buf = prev_set[:, 7, :], bf("S_%d_7" % ((n_local + 1) % 2))
                else:
                    sp_ap, sp_buf = cur[:, c - 1, :], bf("S_%d_%d" % (n_local % 2, c - 1))
            else:
                b = (ch - NCH_P) * 8 + c
                sp_ap, sp_buf = sin[:, c, :], bf("sin")
            Sprev.append((sp_ap, sp_buf))
            pb = c // 4
            P.op("dve", "scalar_tensor_tensor", dict(out=cur[:, c, :], in0=sp_ap, scalar=dec[:, c:c + 1],
                                                     in1=ps[pb][:, (c % 4) * 128:(c % 4 + 1) * 128], op0=ALU.mult, op1=ALU.add),
                 reads=[sp_buf, bf("dec"), psb[pb]], writes=[bf("S_%d_%d" % (n_local % 2, c))])
        mixers.first = False
        if hgs <= 5:
            return
        for p_ in range(4):
            P.op("pe", "matmul", dict(out=ps[6][:, p_ * 128:(p_ + 1) * 128], lhsT=iv[:, p_, :], rhs=attm[:, p_, :], start=True, stop=False),
                 reads=[bf("iv"), bf("attm")], writes=[psb[6]])
            for h2 in range(2):
                c = 2 * p_ + h2
                sp_ap, sp_buf = Sprev[c]
                P.op("pe", "matmul", dict(out=ps[6][:, c * 64:(c + 1) * 64], lhsT=sp_ap, rhs=t_eb[:, c * 64:(c + 1) * 64],
                                          start=False, stop=(h2 == 1)),
                     reads=[sp_buf, bf("t_eb")], writes=[psb[6]])
        if hgs <= 6:
            return
        headnorm(ps[6], psb[6], ogain[:, 0:1], [bf("ogain")], t_ke[:], bf("t_ke"))
        P.op("dve", "tensor_tensor", dict(out=bT[:], in0=t_ke[:], in1=t_zh[:], op=ALU.mult),
             reads=[bf("t_ke"), bf("t_zh")], writes=[bf("bT")])
        emit_ab(ch, 1, bT, bf("bT"))
        if mode == "mix":
            P.op("dve", "tensor_copy", dict(out=dbg_t[:], in_=bT[:]), reads=[bf("bT")], writes=[bf("t_lf")])
            P.dma("sp", dict(out=dbg_b[:, ch * CH:(ch + 1) * CH], in_=dbg_t[:]), reads=[bf("t_lf")])
        if is_p:
            if ch == NCH_P - 1 or (debug and ch == chunks[-1]):
                P.dma("sp", dict(out=s_out[HD["h"], 0], in_=cur[:, 7, :]), reads=[bf("S_%d_7" % (n_local % 2))])
        else:
            b0 = (ch - NCH_P) * 8
            P.dma("sp", dict(out=s_out[HD["h"], 1 + b0:1 + b0 + 8].rearrange("b k v -> k b v"), in_=cur[:]),
                  reads=[bf("S_%d_%d" % (n_local % 2, c)) for c in range(8)])
    mixers.n = 0
    mixers.first = True

    def build_hT(src_dram, row0, ntiles, segs_fn):
        for tt in range(ntiles):
            t0 = row0 + tt * 128
            i = state["tile_i"] % 2
            state["tile_i"] += 1
            xb = xbuf[i]
            xbb = bf("xbuf%d" % i)
            st = stat[i]
            stb = bf("stat%d" % i)
            P.dma("sp", dict(out=xb[:], in_=src_dram[t0:t0 + 128, :]), writes=[xbb])
            P.op("act", "activation", dict(out=junk[:], in_=xb[:], func=AF.Square, accum_out=st[:, 0:1]),
                 reads=[xbb], writes=[bf("xn"), stb])
            P.op("act", "activation", dict(out=st[:, 1:2], in_=st[:, 0:1], func=AF.Ln, scale=1.0 / D, bias=epsb[:, 0:1]),
                 reads=[stb, bf("epsb")], writes=[stb])
            P.op("act", "activation", dict(out=st[:, 2:3], in_=st[:, 1:2], func=AF.Exp, scale=-0.5),
                 reads=[stb], writes=[stb])
            P.op("dve", "tensor_scalar", dict(out=xn[:], in0=xb[:], scalar1=st[:, 2:3], scalar2=None, op0=ALU.mult),
                 reads=[xbb, stb], writes=[bf("xn")])
            for half in range(2):
                tp = ps[half][:, :].bitcast(BF16)
                for kk in range(8):
                    k = half * 8 + kk
                    P.op("pe", "transpose", dict(out=tp[:, kk * 128:(kk + 1) * 128], in_=xn[:, k * 128:(k + 1) * 128],
                                                 identity=ident_b[:]),
                         reads=[bf("xn"), bf("ident_b")], writes=[psb[half]])
                for kk in range(8):
                    k = half * 8 + kk
                    segs = segs_fn(tt)
                    for (c0, c1, r) in segs:
                        o_ap = hT[:, k, tt * 128 + c0:tt * 128 + c1]
                        i_ap = tp[:, kk * 128 + c0:kk * 128 + c1]
                        if len(segs) == 1:
                            wr = [bf("hT_%d_%d_0" % (k, tt)), bf("hT_%d_%d_64" % (k, tt))]
                        else:
                            wr = [bf("hT_%d_%d_%d" % (k, tt, c0))]
                        rd = [psb[half], bf("gm"), bf("sh")]
                        if half == 0:
                            P.op("act", "activation", dict(out=o_ap, in_=i_ap, func=AF.Identity,
                                                           scale=gm[:, k, r:r + 1], bias=sh[:, k, r:r + 1]), reads=rd, writes=wr)
                        else:
                            P.op("dve", "tensor_scalar", dict(out=o_ap, in0=i_ap, scalar1=gm[:, k, r:r + 1],
                                                              scalar2=sh[:, k, r:r + 1], op0=ALU.mult, op1=ALU.add),
                                 reads=rd, writes=wr)
        hTb = {}
        for k in range(KC):
            lst = []
            for tt in range(ntiles):
                lst.append(bf("hT_%d_%d_0" % (k, tt)))
                lst.append(bf("hT_%d_%d_64" % (k, tt)))
            hTb[k] = lst
        return hTb

    def own_slots():
        for s_ in range(5):
            is_p = s_ < 4
            nt = 4 if is_p else 1
            N = nt * 128
            if is_p:
                segs_fn = lambda tt: [(0, 128, 0)]
            else:
                segs_fn = lambda tt: [(0, 64, RS0), (64, 128, RS0 + 1)]
            hTb = build_hT(x_own, s_ * CH, nt, segs_fn)
            proj(0, ps[2], psb[2], hTb)
            headnorm(ps[2], psb[2], qkgs2[:, 0:1], [bf("qkgs2a")], QT[:], bf("QT"))
            proj(1, ps[3], psb[3], hTb)
            headnorm(ps[3], psb[3], qkgs2[:, 1:2], [bf("qkgs2b")], kn_f[:], bf("kn_f"))
            P.op("act", "activation", dict(out=ksT[:], in_=kn_f[:], func=AF.Copy), reads=[bf("kn_f")], writes=[bf("ksT")])
            proj(2, ps[2], psb[2], hTb)
            P.op("dve", "tensor_copy", dict(out=vT_f[:], in_=ps[2][:, :]), reads=[psb[2]], writes=[bf("vT_f")])
            for tt in range(nt):
                P.op("pe", "transpose", dict(out=ps[5][:, tt * 128:(tt + 1) * 128], in_=vT_f[:, tt * 128:(tt + 1) * 128],
                                             identity=ident_f[:]),
                     reads=[bf("vT_f"), bf("ident_f")], writes=[psb[5]])
            P.op("dve", "tensor_copy", dict(out=vs_bf[:, 0:nt, :].rearrange("p t d -> p (t d)"), in_=ps[5][:, 0:N]),
                 reads=[psb[5]], writes=[bf("vs_bf")])
            proj(3, ps[3], psb[3], hTb)
            P.op("act", "activation", dict(out=zs_sb[:], in_=ps[3][:, :], func=AF.Silu), reads=[psb[3]], writes=[bf("zs_sb")])
            if is_p:
                blocks = []
                for i_ in range(3, -1, -1):
                    blocks.append(dict(kT=ksT[:, i_ * 128:(i_ + 1) * 128], kbufs=[bf("ksT")], v=vs_bf[:, i_, :], vbufs=[bf("vs_bf")],
                                       mask=maskp[:, i_, :], mbufs=[bf("maskp")]))
                for j in range(32 * (s_ + 1) - 1, -1, -1):
                    blocks.append(dict(kT=KT[:, j * 128:(j + 1) * 128], kbufs=[bf("KT%d" % (j // 4))],
                                       v=Vr[:, j, :], vbufs=[bf("Vr%d" % (j // 4))], bias=vis[:, s_ * 128 + j:s_ * 128 + j + 1]))
                attention(CH, QT[:], [bf("QT")], blocks, ps[7][:, :], 0)
            else:
                P.barrier()
                load_cache(0, cache_k[HD["h"], 0], cache_v[HD["h"], 0])
                for par in range(2):
                    if par == 0:
                        load_cache(1, cache_k[HD["h"], 1], cache_v[HD["h"], 1])
                    i = par
                    kcT = KT[:, i * 4096:(i + 1) * 4096]
                    vc = Vr[:, i * 32:(i + 1) * 32, :]
                    blocks = [dict(kT=ksT[:, 0:128], kbufs=[bf("ksT")], v=vs_bf[:, 0, :], vbufs=[bf("vs_bf")],
                                   mask=masks[:, par, :], mbufs=[bf("masks")])]
                    for j in range(31, -1, -1):
                        blocks.append(dict(kT=kcT[:, j * 128:(j + 1) * 128], kbufs=[bf("kcT%d_%d" % (i, j // 8))],
                                           v=vc[:, j, :], vbufs=[bf("vc%d" % i)]))
                    attention(TSQ, QT[:, par * TSQ:(par + 1) * TSQ], [bf("QT")], blocks, ps[7][:, par * TSQ:(par + 1) * TSQ], 0)
            P.op("dve", "tensor_tensor", dict(out=aT[:, 0:N], in0=ps[7][:, 0:N], in1=zs_sb[:, 0:N], op=ALU.mult),
                 reads=[psb[7], bf("zs_sb")], writes=[bf("bT")])
            P.dma("sp", dict(out=a_own_scr.ap()[HD["h"], s_ * 4:s_ * 4 + nt].rearrange("t p c -> p t c"),
                             in_=aT[:, 0:N].rearrange("p (t c) -> p t c", c=128)), reads=[bf("bT")])

    if mode == "fused" and "zero_scr" in flags:
        P.op("dve", "memset", dict(ap=aT[:], constant=0.0), writes=[bf("bT")])
        P.op("dve", "memset", dict(ap=KT[:], constant=0.0), writes=[bf("KT%d" % i_) for i_ in range(32)])
        P.op("dve", "memset", dict(ap=Vr[:].rearrange("p j d -> p (j d)"), constant=0.0), writes=[bf("Vr%d" % i_) for i_ in range(32)])
        for n_ in range(2 * 8 * 34):
            P.dma("sp", dict(out=ab_scr.ap()[n_ * 512:(n_ + 1) * 512, :].rearrange("(t p) c -> p t c", p=128),
                             in_=aT[:].rearrange("p (t c) -> p t c", c=128)), reads=[bf("bT")])
        P.barrier()
    for hd in range(NH if do_mix else 0):
        HD["h"] = hd
        head_setup()
        for ch in chunks:
            if ch < NCH_P:
                segs_fn = lambda tt: [(0, 128, 0)]
            else:
                segs_fn = (lambda ch: lambda tt: [(0, 64, 1 + ((ch - NCH_P) * 4 + tt) * 2), (64, 128, 2 + ((ch - NCH_P) * 4 + tt) * 2)])(ch)
            hTb = build_hT(x_all, ch * CH - x_base, 4, segs_fn)

            if stop in ("hT", "ev_act", "ev_dve"):
                return done()
            if mode != "fused":
                proj(0, ps[2], psb[2], hTb)
                headnorm(ps[2], psb[2], qkgs2[:, 0:1], [bf("qkgs2a")], QT[:], bf("QT"))
            if stop == "q":
                return done()
            proj(1, ps[3], psb[3], hTb)
            headnorm(ps[3], psb[3], qkgs2[:, 1:2], [bf("qkgs2b")], kn_f[:], bf("kn_f"))
            if ch < NCH_P:
                P.op("act", "activation", dict(out=KT[:, ch * CH:(ch + 1) * CH], in_=kn_f[:], func=AF.Copy),
                     reads=[bf("kn_f")], writes=[bf("KT%d" % ch)])
            else:
                P.op("act", "activation", dict(out=ksT[:], in_=kn_f[:], func=AF.Copy), reads=[bf("kn_f")], writes=[bf("ksT")])
            tr_out(ch, kn_f, bf("kn_f"), k_out[HD["h"]])
            if stop == "k":
                return done()
            proj(2, ps[2], psb[2], hTb)
            P.op("dve", "tensor_copy", dict(out=vT_f[:], in_=ps[2][:, :]), reads=[psb[2]], writes=[bf("vT_f")])

            def vcopy(pt, ptb, ch=ch):
                if ch < NCH_P:
                    P.op("dve", "tensor_copy", dict(out=Vr[:, ch * 4:(ch + 1) * 4, :].rearrange("p t d -> p (t d)"), in_=pt[:, :]),
                         reads=[ptb], writes=[bf("Vr%d" % ch)])
                else:
                    P.op("dve", "tensor_copy", dict(out=vs_bf[:].rearrange("p t d -> p (t d)"), in_=pt[:, :]),
                         reads=[ptb], writes=[bf("vs_bf")])
            tr_out(ch, vT_f, bf("vT_f"), v_out[HD["h"]], extra_copy=vcopy)
            mixers(ch, hTb)
        if mode == "fused" and len(chunks) > 0:
            own_slots()

    if do_out:
        P.barrier()
        es1.close()
        gateP = sb("gateP", [128, D], F32)
        gateS = sb("gateS", [128, D], F32)
        if mode == "fused":
            idx_ab = sb("idx_ab_s", [128, 272], mybir.dt.int32)
            P.dma("sp", dict(out=idx_ab[:], in_=idx_ab_d[:, :]), writes=[bf("idx_ab")])
        es3 = ExitStack()
        csbc = [es3.enter_context(nc.sbuf_tensor("csbc%d" % i, [128, KC, 128], BF16)) for i in range(2)]
        bgate = es3.enter_context(nc.sbuf_tensor("bgate", [128, D], F32))
        wada2 = [es3.enter_context(nc.sbuf_tensor("wada2_%d" % i, [128, KC, 512], BF16)) for i in range(2)]
        P.dma("sp", dict(out=bgate[:], in_=b_gate_bc[:, :]), writes=[bf("bgate")])
        P.op("dve", "tensor_copy", dict(out=csbc[0][:], in_=csT[:, :, 0:1].to_broadcast([128, KC, 128])),
             reads=[bf("csT")], writes=[bf("csbc0")])
        P.op("dve", "tensor_copy", dict(out=csbc[1][:, :, 0:64], in_=csT[:, :, RS0:RS0 + 1].to_broadcast([128, KC, 64])),
             reads=[bf("csT")], writes=[bf("csbc1a")])
        P.op("dve", "tensor_copy", dict(out=csbc[1][:, :, 64:128], in_=csT[:, :, RS0 + 1:RS0 + 2].to_broadcast([128, KC, 64])),
             reads=[bf("csT")], writes=[bf("csbc1b")])
        for g in range(8, 12):
            wb = wada2[g % 2]
            wbuf = bf("wada2_%d" % (g % 2))
            P.dma("pool", dict(out=wb[:], in_=w_ada[:, g * 512:(g + 1) * 512].rearrange("(k p) c -> p k c", p=128)),
                  writes=[wbuf])
            for which in range(2):
                pb = 3 + which
                for k in range(KC):
                    P.op("pe", "matmul", dict(out=ps[pb][:, :], lhsT=csbc[which][:, k, :], rhs=wb[:, k, :],
                                              start=(k == 0), stop=(k == KC - 1)),
                         reads=[wbuf, bf("csbc0"), bf("csbc1a"), bf("csbc1b")], writes=[psb[pb]])
                gt = gateP if which == 0 else gateS
                P.op("dve", "tensor_tensor", dict(out=gt[:, (g - 8) * 512:(g - 7) * 512], in0=ps[pb][:, :],
                                                  in1=bgate[:, (g - 8) * 512:(g - 7) * 512], op=ALU.add),
                     reads=[psb[pb], bf("bgate")], writes=[bf("gate%d_%d" % (which, g - 8))])
        P.barrier()
        es3.close()
        wslot = [sb("wslot%d" % i, [128, 24576], BF16) for i in range(2)]
        abT = sb("abT", [128, 2, 8, CH], BF16)
        mT = sb("mT", [128, KC, CH], BF16)
        sgA = sb("sgA", [128, CH], F32)
        sgB = sb("sgB", [128, CH], F32)
        tmp1 = sb("tmp1", [128, CH], F32)
        tmp2 = sb("tmp2", [128, CH], F32)
        ysl = [sb("ysl%d" % i, [128, CH], F32) for i in range(2)]
        xsl = [sb("xsl%d" % i, [128, CH], F32) for i in range(2)]
        gi = 0
        yi = 0
        for o in range(5):
            N = CH if o < 4 else OWN_S
            NT = N // 128
            if o < 4:
                segs_fn = lambda tt: [(0, 128, 0)]
            else:
                segs_fn = lambda tt: [(0, 64, RS0), (64, 128, RS0 + 1)]
            hTb = build_hT(x_own, o * CH, NT, segs_fn)
            if mode == "out":
                for ab in range(2):
                    P.dma("pool", dict(out=abT[:, ab, :, 0:N], in_=ab_own[ab, :, :, o * CH:o * CH + N].rearrange("h p t -> p h t")),
                          writes=[bf("abT%d" % ab)])
                abbufs = {0: [bf("abT0")], 1: [bf("abT1")]}
            else:
                abbufs = {0: [], 1: []}
                for h in range(8):
                    P.dma("sp", dict(out=abT[:, 0, h, 0:N].rearrange("p (t c) -> p t c", c=128),
                                     in_=a_own_scr.ap()[h, o * 4:o * 4 + NT].rearrange("t p c -> p t c")),
                          writes=[bf("abT_a%d" % h)])
                    abbufs[0].append(bf("abT_a%d" % h))
                for ab in range(1, 2):
                    for h in range(8):
                        for tt in range(NT):
                            col = (ab * 8 + h) * 17 + o * 4 + tt
                            bb = bf("abT_%d_%d_%d" % (ab, h, tt))
                            P.dma("pool", dict(out=abT[:, ab, h, tt * 128:(tt + 1) * 128], out_offset=None, in_=ab_scr.ap(),
                                               in_offset=bass.IndirectOffsetOnAxis(ap=idx_ab[:, col:col + 1], axis=0)),
                                  reads=[bf("idx_ab")], writes=[bb], meth="indirect_dma_start")
                            abbufs[ab].append(bb)
            for g in range(4):
                sl = gi % 2
                gi += 1
                ws = wslot[sl]
                wgA = ws[:, 0:8192].rearrange("p (k c) -> p k c", c=512)
                wgB = ws[:, 8192:16384].rearrange("p (k c) -> p k c", c=512)
                wbA = ws[:, 16384:20480].rearrange("p (h c) -> p h c", c=512)
                wbB = ws[:, 20480:24576].rearrange("p (h c) -> p h c", c=512)
                P.dma("pool", dict(out=wgA, in_=w_gate[:, g * 512:(g + 1) * 512].rearrange("(k p) c -> p k c", p=128)),
                      writes=[bf("ws%d_A" % sl)])
                P.dma("pool", dict(out=wgB, in_=w_gate[:, D + g * 512:D + (g + 1) * 512].rearrange("(k p) c -> p k c", p=128)),
                      writes=[bf("ws%d_B" % sl)])
                P.dma("pool", dict(out=wbA, in_=w_bsb[:, g * 512:(g + 1) * 512].rearrange("(h p) c -> p h c", p=128)),
                      writes=[bf("ws%d_C" % sl)])
                P.dma("pool", dict(out=wbB, in_=w_bhg[:, g * 512:(g + 1) * 512].rearrange("(h p) c -> p h c", p=128)),
                      writes=[bf("ws%d_D" % sl)])
                for cc in range(4):
                    jc = g * 4 + cc
                    for (wg, wgbuf, pb, sg, sgbuf) in ((wgA, bf("ws%d_A" % sl), 2, sgA, bf("sgA")), (wgB, bf("ws%d_B" % sl), 3, sgB, bf("sgB"))):
                        for k in range(KC):
                            P.op("pe", "matmul", dict(out=ps[pb][:, 0:N], lhsT=wg[:, k, cc * 128:(cc + 1) * 128], rhs=hT[:, k, 0:N],
                                                      start=(k == 0), stop=(k == KC - 1)),
                                 reads=[wgbuf] + hTb[k], writes=[psb[pb]])
                        P.op("act", "activation", dict(out=sg[:, 0:N], in_=ps[pb][:, 0:N], func=AF.Sigmoid),
                             reads=[psb[pb]], writes=[sgbuf])
                    for (wbr, wbbuf, pb, ab) in ((wbA, bf("ws%d_C" % sl), 4, 0), (wbB, bf("ws%d_D" % sl), 5, 1)):
                        for h in range(8):
                            P.op("pe", "matmul", dict(out=ps[pb][:, 0:N], lhsT=wbr[:, h, cc * 128:(cc + 1) * 128], rhs=abT[:, ab, h, 0:N],
                                                      start=(h == 0), stop=(h == 7)),
                                 reads=[wbbuf] + abbufs[ab], writes=[psb[pb]])
                    P.op("dve", "tensor_tensor", dict(out=tmp1[:, 0:N], in0=ps[4][:, 0:N], in1=sgA[:, 0:N], op=ALU.mult),
                         reads=[psb[4], bf("sgA")], writes=[bf("tmp1")])
                    P.op("dve", "tensor_tensor", dict(out=tmp2[:, 0:N], in0=ps[5][:, 0:N], in1=sgB[:, 0:N], op=ALU.mult),
                         reads=[psb[5], bf("sgB")], writes=[bf("tmp2")])
                    P.op("pool", "tensor_tensor", dict(out=mT[:, jc, 0:N], in0=tmp1[:, 0:N], in1=tmp2[:, 0:N], op=ALU.add),
                         reads=[bf("tmp1"), bf("tmp2")], writes=[bf("mT%d" % jc)])
            for cg in range(4):
                sl = gi % 2
                gi += 1
                ws = wslot[sl]
                wo = ws[:, 0:8192].rearrange("p (k c) -> p k c", c=512)
                P.dma("pool", dict(out=wo, in_=w_o[:, cg * 512:(cg + 1) * 512].rearrange("(k p) c -> p k c", p=128)),
                      writes=[bf("ws%d_A" % sl)])
                for tt in range(NT):
                    pb = 6 + (yi % 2)
                    ys = ysl[yi % 2]
                    xs = xsl[yi % 2]
                    ysb = bf("ysl%d" % (yi % 2))
                    xsb = bf("xsl%d" % (yi % 2))
                    yi += 1
                    r0 = o * CH + tt * 128
                    P.dma("sp", dict(out=xs[:], in_=x_own[r0:r0 + 128, cg * 512:(cg + 1) * 512]), writes=[xsb])
                    for k in range(KC):
                        P.op("pe", "matmul", dict(out=ps[pb][:, :], lhsT=mT[:, k, tt * 128:(tt + 1) * 128], rhs=wo[:, k, :],
                                                  start=(k == 0), stop=(k == KC - 1)),
                             reads=[bf("ws%d_A" % sl), bf("mT%d" % k)], writes=[psb[pb]])
                    gt = gateP if o < 4 else gateS
                    which = 0 if o < 4 else 1
                    P.op("dve", "tensor_tensor", dict(out=ys[:], in0=ps[pb][:, :], in1=gt[:, cg * 512:(cg + 1) * 512], op=ALU.mult),
                         reads=[psb[pb], bf("gate%d_%d" % (which, cg))], writes=[ysb])
                    P.op("pool", "tensor_tensor", dict(out=ys[:], in0=ys[:], in1=xs[:], op=ALU.add),
                         reads=[ysb, xsb], writes=[ysb])
                    P.dma("sp", dict(out=y_own[r0:r0 + 128, cg * 512:(cg + 1) * 512], in_=ys[:]), reads=[ysb])

    P.finish()
    P.emit(nc, es)
    if not do_out:
        es1.close()
    es.close()
    return nc


def _consts():
    j = np.arange(128)[:, None]
    k = np.arange(128)[None, :]
    trin = np.where(j >= k, -1.0, 0.0).astype(np.float32)
    tric = np.where(j < k, -1.0, 0.0).astype(np.float32)
    q = np.arange(512)[None, None, :]
    kk = np.arange(128)[:, None, None]
    i = np.arange(4)[None, :, None]
    maskp = np.where(q > kk + 128 * i, 0.0, -30000.0).astype(np.float32)
    qs = np.arange(64)[None, :]
    ks = np.arange(128)[:, None]
    m_even = np.where((ks < 64) & (qs > ks), 0.0, -30000.0)
    m_odd = np.where((ks >= 64) & (qs > ks - 64), 0.0, -30000.0)
    masks = np.stack([m_even, m_odd], axis=1).astype(np.float32)
    s_ = np.arange(128)[:, None]
    t_ = np.arange(128)[None, :]
    hmask = ((s_ // 64 == t_ // 64) & (s_ <= t_)).astype(np.float32)
    resetm = np.ones((128, 512), np.float32)
    resetm[:, ::64] = 0.0
    return {"identf": np.eye(128, dtype=np.float32), "trin": trin, "tric": tric, "maskp": maskp, "masks": masks,
            "hmask": hmask, "resetm": resetm,
            "pmask": np.stack([(np.arange(128) < 64), (np.arange(128) >= 64)], axis=1).astype(np.float32)}


def _f32(a):
    return np.ascontiguousarray(np.asarray(a, dtype=np.float32))


def make_in_maps(inp, cores=None, x_rows=TT, x_base=0):
    f32 = _f32
    x_all = f32(np.concatenate([np.asarray(inp["x_prompt"]).reshape(TP, D), np.asarray(inp["x_sample"]).reshape(TS, D)], axis=0))[x_base:x_base + x_rows]
    c_all = f32(np.concatenate([np.asarray(inp["c_prompt"]), np.asarray(inp["c_sample"])], axis=0))
    w_ada0 = f32(np.asarray(inp["w_ada"])[0])
    b_adaT = f32(np.asarray(inp["b_ada"])[0].reshape(48, 128).T)
    ngT = f32(np.asarray(inp["norm_gain"])[0].reshape(KC, 128).T)
    w_in0 = np.asarray(inp["w_in"])[0]
    qkg = f32(np.stack([np.asarray(inp["q_norm_gain"])[0], np.asarray(inp["k_norm_gain"])[0]], axis=1))
    consts = _consts()
    in_maps = []
    for c in (range(NCORES) if cores is None else cores):
        cols = np.concatenate([np.arange(j * 1024 + c * 128, j * 1024 + (c + 1) * 128) for j in range(8)])
        lbr = f32(np.asarray(inp["hgrn_lb_raw"])[:, c * 128:(c + 1) * 128].T)
        m = {"x_all": x_all, "c_all": c_all, "w_ada": w_ada0, "b_adaT": b_adaT, "ngT": ngT,
             "w_head": f32(w_in0[:, cols])[None], "qkg": qkg, "lbr": lbr[None],
             "ogain": f32(np.asarray(inp["hgrn_onorm_gain"])[0, c, :].reshape(1, 128, 1)),
             "cache_k": f32(np.asarray(inp["cache_sb_k"])[0, :, :, c, :])[None],
             "cache_v": f32(np.asarray(inp["cache_sb_v"])[0, :, :, c, :])[None],
             "s_in": f32(np.asarray(inp["state_hgrn"])[0, :, c])[None]}
        m.update(consts)
        in_maps.append(m)
    return in_maps


def make_out_maps(inp, a_all, b_all, cores=None):
    f32 = _f32
    xp = np.asarray(inp["x_prompt"]).reshape(TP, D)
    xs = np.asarray(inp["x_sample"]).reshape(TS, D)
    cp = np.asarray(inp["c_prompt"])
    cs = np.asarray(inp["c_sample"])
    w_ada0 = f32(np.asarray(inp["w_ada"])[0])
    b_ada0 = np.asarray(inp["b_ada"])[0]
    b_adaT = f32(b_ada0.reshape(48, 128).T)
    ngT = f32(np.asarray(inp["norm_gain"])[0].reshape(KC, 128).T)
    w_gate = f32(np.asarray(inp["w_in"])[0][:, 8192:])
    w_bsb = f32(np.asarray(inp["w_branch_sb"])[0])
    w_bhg = f32(np.asarray(inp["w_branch_hgrn"])[0])
    w_o = f32(np.asarray(inp["w_out"])[0])
    b_gate_bc = f32(np.broadcast_to(b_ada0[2 * D:][None, :], (128, D)))
    maps = []
    for c in (range(NCORES) if cores is None else cores):
        ptok = np.concatenate([own_chunk(c, s_) * CH + np.arange(CH) for s_ in range(4)])
        tok = np.concatenate([ptok, TP + np.arange(c * OWN_S, (c + 1) * OWN_S)])
        m = {"c_all": f32(np.concatenate([cp, cs[2 * c:2 * c + 2]], axis=0)), "w_ada": w_ada0, "b_adaT": b_adaT, "ngT": ngT,
             "identf": np.eye(128, dtype=np.float32),
             "x_own": f32(np.concatenate([xp[ptok], xs[c * OWN_S:(c + 1) * OWN_S]], axis=0)),
             "w_gate": w_gate, "w_bsb": w_bsb, "w_bhg": w_bhg, "w_o": w_o, "b_gate_bc": b_gate_bc,
             }
        if a_all is not None:
            m["ab_own"] = f32(np.stack([a_all[:, :, tok], b_all[:, :, tok]], axis=0))
        maps.append(m)
    return maps


def make_fused_maps(inp, cores=None, x_rows=TT, x_base=0):
    f32 = _f32
    base = make_out_maps(inp, None, None, cores=cores)
    x_all = f32(np.concatenate([np.asarray(inp["x_prompt"]).reshape(TP, D), np.asarray(inp["x_sample"]).reshape(TS, D)], axis=0))[x_base:x_base + x_rows]
    cp = np.asarray(inp["c_prompt"])
    cs = np.asarray(inp["c_sample"])
    w_in0 = np.asarray(inp["w_in"])[0]
    w_head = f32(np.stack([w_in0[:, np.concatenate([np.arange(j * 1024 + h * 128, j * 1024 + (h + 1) * 128) for j in range(8)])]
                           for h in range(8)], axis=0))
    qkg = f32(np.stack([np.asarray(inp["q_norm_gain"])[0], np.asarray(inp["k_norm_gain"])[0]], axis=1))
    lbr = f32(np.asarray(inp["hgrn_lb_raw"]).reshape(2, 8, 128).transpose(1, 2, 0))
    ogain = f32(np.asarray(inp["hgrn_onorm_gain"])[0].reshape(8, 128, 1))
    ck = np.asarray(inp["cache_sb_k"])[0]
    cv = np.asarray(inp["cache_sb_v"])[0]
    s_in = f32(np.asarray(inp["state_hgrn"])[0].transpose(1, 0, 2, 3))
    consts = _consts()
    maps = []
    for n, c in enumerate(range(NCORES) if cores is None else cores):
        m = dict(base[n])
        m.pop("ab_own", None)
        m["c_all"] = f32(np.concatenate([cp, cs, cs[2 * c:2 * c + 2]], axis=0))
        idx = np.zeros((128, 272), np.int32)
        for ab in range(2):
            for h in range(8):
                for j in range(17):
                    tile = own_chunk(c, j // 4) * 4 + j % 4 if j < 16 else 128 + c
                    idx[:, (ab * 8 + h) * 17 + j] = ((ab * 8 + h) * 136 + tile) * 128 + np.arange(128)
        vis = np.full((128, 512), -30000.0, np.float32)
        for s_ in range(4):
            vis[:, s_ * 128:s_ * 128 + 4 * own_chunk(c, s_)] = 0.0
        m.update({"x_all": x_all, "w_head": w_head, "qkg": qkg, "lbr": lbr, "ogain": ogain,
                  "cache_k": f32(ck[2 * c:2 * c + 2].transpose(2, 0, 1, 3)), "cache_v": f32(cv[2 * c:2 * c + 2].transpose(2, 0, 1, 3)),
                  "s_in": s_in, "idx_ab": idx, "vis": vis})
        m.update(consts)
        maps.append(m)
    return maps


def kernel(**inp):
    nc = build(mode="fused")
    res = run_bass_kernel_spmd(nc, make_fused_maps(inp), core_ids=list(range(NCORES)))
    r = res.results
    kall = r[0]["k_out"].reshape(8, TT, 128).transpose(1, 0, 2)
    vall = r[0]["v_out"].reshape(8, TT, 128).transpose(1, 0, 2)
    sall = r[0]["s_out"].reshape(8, 17, 128, 128).transpose(1, 0, 2, 3)
    y = [r[c]["y_own"] for c in range(NCORES)]
    y_prompt = np.zeros((TP, D), np.float32)
    for c in range(NCORES):
        for s_ in range(4):
            y_prompt[own_chunk(c, s_) * CH:(own_chunk(c, s_) + 1) * CH] = y[c][s_ * CH:(s_ + 1) * CH]
    y_prompt = y_prompt.reshape(1, TP, D)
    y_sample = np.concatenate([yy[OWN_P:] for yy in y], axis=0).reshape(NB, TSQ, D)
    new_k_prompt = kall[:TP].reshape(1, 1, TP, 8, 128)
    new_v_prompt = vall[:TP].reshape(1, 1, TP, 8, 128)
    new_k_sample = kall[TP:].reshape(1, NB, TSQ, 8, 128)
    new_v_sample = vall[TP:].reshape(1, NB, TSQ, 8, 128)
    new_s_prompt = sall[0].reshape(1, 1, 8, 128, 128)
    new_s_sample = sall[1:].reshape(1, NB, 8, 128, 128)
    c32 = lambda a: np.ascontiguousarray(a, dtype=np.float32)
    return (c32(y_prompt), c32(y_sample), c32(new_k_prompt), c32(new_v_prompt), c32(new_s_prompt),
            c32(new_k_sample), c32(new_v_sample), c32(new_s_sample))
```

```python
import numpy as np
from contextlib import ExitStack
import concourse.bass as bass
import concourse.mybir as mybir
from concourse.bass_utils import run_bass_kernel_spmd

F32 = mybir.dt.float32
BF16 = mybir.dt.bfloat16
AF = mybir.ActivationFunctionType
ALU = mybir.AluOpType

NCORES = 8
D = 2048
KC = D // 128
TP = 16384
NB = 16
TSQ = 64
TS = NB * TSQ
TT = TP + TS
CH = 512
NCH_P = TP // CH
NCH = TT // CH
PAST = 4096
EPS = 1e-6
SEM_CHUNK = 16000
NSLOT = 8
OWN_P = TP // NCORES
OWN_S = TS // NCORES
OWN = OWN_P + OWN_S


import os
_CONTIG = bool(os.environ.get("OWN_CONTIG"))


def own_chunk(c, s):
    if _CONTIG:
        return 4 * c + s
    return [c, 15 - c, 16 + c, 31 - c][s]


class Buf:
    __slots__ = ("name", "w", "r", "psum")

    def __init__(self, name, psum=False):
        self.name = name
        self.w = None
        self.r = {}
        self.psum = psum


class Q:
    def __init__(self, name):
        self.name = name
        self.ops = []
        self.n = 0
        self.waited = {}
        self.maxchunk = {}
        self.dma_k = 0
        self.slot_tot = [0] * NSLOT


class Prog:
    def __init__(self):
        self.q = {n: Q(n) for n in ("pe", "act", "dve", "pool", "sp")}
        self.keys = []
        self.keyset = set()

    def _key(self, key):
        if key not in self.keyset:
            self.keyset.add(key)
            self.keys.append(key)

    def _deps(self, q, reads, writes, extra):
        deps = {}

        def add(t):
            if t is None:
                return
            k, v = t
            if deps.get(k, 0) < v:
                deps[k] = v

        for b in reads:
            add(b.w)
            if b.psum:
                for rk, t in b.r.items():
                    if rk != q.name:
                        add(t)
        for b in writes:
            add(b.w)
            for t in b.r.values():
                add(t)
        for t in extra:
            add(t)
        waits = []
        for key, val in deps.items():
            if q.name == "pe" and key[0] == "pe":
                continue
            if isinstance(key[1], int) and key[0] in ("pe", "act", "dve", "pool", "sp"):
                mc = q.maxchunk.get(key[0], -1)
                if key[1] < mc:
                    continue
                if key[1] > mc:
                    q.maxchunk[key[0]] = key[1]
            if q.waited.get(key, 0) >= val:
                continue
            q.waited[key] = val
            waits.append((key, val))
        return waits

    def _mark(self, tok, reads, writes, rkey):
        for b in writes:
            b.w = tok
            b.r = {}
        for b in reads:
            b.r[rkey] = tok

    def op(self, qn, meth, kw, reads=(), writes=(), extra=()):
        fn = (meth, kw)
        q = self.q[qn]
        waits = self._deps(q, reads, writes, extra)
        idx = q.n
        q.n += 1
        key = (qn, idx // SEM_CHUNK)
        self._key(key)
        tok = (key, idx % SEM_CHUNK + 1)
        q.ops.append((waits, fn, (key, 1)))
        self._mark(tok, reads, writes, qn)
        return tok

    def dma(self, qn, kw, reads=(), writes=(), extra=(), meth="dma_start"):
        fn = (meth, kw)
        q = self.q[qn]
        slot = q.dma_k % NSLOT
        q.dma_k += 1
        key = (qn + "_dma", slot)
        self._key(key)
        prev = q.slot_tot[slot]
        ex = list(extra)
        if prev > 0:
            ex.append((key, prev))
        waits = self._deps(q, reads, writes, ex)
        q.slot_tot[slot] = prev + 16
        tok = (key, prev + 16)
        q.ops.append((waits, fn, (key, 16)))
        self._mark(tok, reads, writes, key)
        return tok

    def barrier(self):
        toks = []
        for q in self.q.values():
            for sl in range(NSLOT):
                if q.slot_tot[sl] > 0:
                    toks.append(((q.name + "_dma", sl), q.slot_tot[sl]))
            if q.n > 0:
                idx = q.n - 1
                toks.append(((q.name, idx // SEM_CHUNK), idx % SEM_CHUNK + 1))
        for q in self.q.values():
            waits = self._deps(q, (), (), toks)
            if waits:
                q.ops.append((waits, None, None))

    def finish(self):
        ex = []
        for q in self.q.values():
            for s in range(NSLOT):
                if q.slot_tot[s] > 0:
                    ex.append(((q.name + "_dma", s), q.slot_tot[s]))
            if q.n > 0 and q.name != "sp":
                idx = q.n - 1
                ex.append(((q.name, idx // SEM_CHUNK), idx % SEM_CHUNK + 1))
        q = self.q["sp"]
        waits = self._deps(q, (), (), ex)
        q.ops.append((waits, None, None))

    def emit(self, nc, es):
        sems = {}
        for key in self.keys:
            sems[key] = es.enter_context(nc.semaphore("s_%s_%s" % (key[0], key[1])))
        with nc.Block() as block:
            self._emit_block(block, sems)

    def _emit_block(self, block, sems):

        def run(q):
            def body(e):
                for waits, fn, inc in q.ops:
                    for key, val in waits:
                        e.wait_ge(sems[key], val)
                    if fn is None:
                        continue
                    ins = getattr(e, fn[0])(**fn[1])
                    ins.then_inc(sems[inc[0]], inc[1])
            return body

        block.tensor(run(self.q["pe"]))
        block.scalar(run(self.q["act"]))
        block.vector(run(self.q["dve"]))
        block.gpsimd(run(self.q["pool"]))
        block.sync(run(self.q["sp"]))


def build(mode="mix", chunks=None, x_rows=TT, stop=None, x_base=0, flags=("att", "hg"), debug=False):
    chunks = list(range(NCH)) if chunks is None else chunks
    if mode == "out":
        chunks = []
    do_mix = mode in ("mix", "fused")
    do_out = mode in ("out", "fused")
    NR = {"mix": 17, "out": 3, "fused": 19}[mode]
    RS0 = NR - 2
    if do_mix:
        debug = True
    nc = bass.Bass("TRN2", target_bir_lowering=False)
    es = ExitStack()
    P = Prog()

    def dram_in(name, shape, dt=F32):
        return nc.dram_tensor(name, list(shape), dt, kind="ExternalInput").ap()

    def dram_out(name, shape, dt=F32):
        return nc.dram_tensor(name, list(shape), dt, kind="ExternalOutput").ap()

    def sb(name, shape, dt):
        return es.enter_context(nc.sbuf_tensor(name, list(shape), dt))

    c_all = dram_in("c_all", [NR, D])
    w_ada = dram_in("w_ada", [D, 3 * D])
    b_adaT = dram_in("b_adaT", [128, 48])
    ngT = dram_in("ngT", [128, KC])
    identf_d = dram_in("identf", [128, 128])
    if do_mix:
        NH = 8 if mode == "fused" else 1
        HD = {"h": 0}
        x_all = dram_in("x_all", [x_rows, D])
        w_head = dram_in("w_head", [NH, D, 1024])
        qkg = dram_in("qkg", [128, 2])
        k_out = dram_out("k_out", [NH, TT, 128])
        v_out = dram_out("v_out", [NH, TT, 128])
        s_out = dram_out("s_out", [NH, 17, 128, 128])
        trin_d = dram_in("trin", [128, 128])
        tric_d = dram_in("tric", [128, 128])
        maskp_d = dram_in("maskp", [128, 4, 512])
        masks_d = dram_in("masks", [128, 2, 64])
        hmask_d = dram_in("hmask", [128, 128])
        resetm_d = dram_in("resetm", [128, 512])
        lbr_d = dram_in("lbr", [NH, 128, 2])
        ogain_d = dram_in("ogain", [NH, 128, 1])
        pmask_d = dram_in("pmask", [128, 2])
        if mode == "fused":
            cache_k = dram_in("cache_k", [NH, 2, PAST, 128])
            cache_v = dram_in("cache_v", [NH, 2, PAST, 128])
            vis_d = dram_in("vis", [128, 512])
            a_own_scr = nc.dram_tensor("a_own_scr", [8, 17, 128, 128], BF16)
            b_own_scr = nc.dram_tensor("b_own_scr", [8, 17, 128, 128], BF16)
            S_scr = nc.dram_tensor("S_scr", [33 * 128, 128], F32)
            s_in_own = dram_in("s_in_own", [NH, 2, 128, 128])
            idx_s_d = nc.dram_tensor("idx_s", [128, 4], mybir.dt.int32, kind="ExternalInput").ap()
        else:
            cache_k = dram_in("cache_k", [NH, NB, PAST, 128])
            cache_v = dram_in("cache_v", [NH, NB, PAST, 128])
        s_in = dram_in("s_in", [NH, NB, 128, 128])
        if mode == "mix":
            dbg_a = dram_out("dbg_a", [128, TT])
            dbg_b = dram_out("dbg_b", [128, TT])
    if do_out:
        x_own = dram_in("x_own", [OWN, D])
        w_gate = dram_in("w_gate", [D, 2 * D])
        w_bsb = dram_in("w_bsb", [1024, D])
        w_bhg = dram_in("w_bhg", [1024, D])
        w_o = dram_in("w_o", [D, D])
        b_gate_bc = dram_in("b_gate_bc", [128, D])
        y_own = dram_out("y_own", [OWN, D])
        if mode == "out":
            ab_own = dram_in("ab_own", [2, 8, 128, OWN])

    es1 = ExitStack()

    def sb1(name, shape, dt):
        return es1.enter_context(nc.sbuf_tensor(name, list(shape), dt))

    ident_f = sb("ident_f", [128, 128], F32)
    ident_b = sb("ident_b", [128, 128], BF16)
    ones_b = sb("ones_b", [128, 128], BF16)
    csT = sb("csT", [128, KC, NR], BF16)
    gm = sb("gm", [128, KC, NR], F32)
    sh = sb("sh", [128, KC, NR], F32)
    badaT = sb("badaT", [128, 48], F32)
    ngTs = sb("ngTs", [128, KC], F32)
    epsb = sb("epsb", [128, 1], F32)
    xbuf = [sb("xbuf%d" % i, [128, D], F32) for i in range(2)]
    xn = sb("xn", [128, D], BF16)
    junk = xn
    stat = [sb("stat%d" % i, [128, 4], F32) for i in range(2)]
    hT = sb("hT", [128, KC, CH], BF16)
    if do_mix:
        wh = sb1("wh", [128, KC, 1024], BF16)
        qkgs = sb1("qkgs", [128, 2], F32)
        qkgs2 = sb1("qkgs2", [128, 2], F32)
        sq_b = sb1("sq_b", [128, CH], BF16)
        rstd_f = sb1("rstd_f", [128, CH], F32)
        kn_f = sb1("kn_f", [128, CH], F32)
        vT_f = sb1("vT_f", [128, CH], F32)
        KT = sb1("KT", [128, TP], BF16)
        Vr = sb1("Vr", [128, TP // 128, 128], BF16)
        QT = sb1("QT", [128, CH], BF16)
    es0 = ExitStack()
    c_sb = es0.enter_context(nc.sbuf_tensor("c_sb", [NR, D], F32))
    cs_sb = es0.enter_context(nc.sbuf_tensor("cs_sb", [NR, D], F32))
    wada = [es0.enter_context(nc.sbuf_tensor("wada%d" % i, [128, KC, 512], BF16)) for i in range(2)]
    modsb = es0.enter_context(nc.sbuf_tensor("modsb", [128, 48, NR], F32))

    ps = [es.enter_context(nc.psum_tensor("ps%d" % i, [128, 512], F32)) for i in range(8)]
    psb = [Buf("ps%d" % i, psum=True) for i in range(8)]

    B = {}

    def bf(name):
        if name not in B:
            B[name] = Buf(name)
        return B[name]

    def done():
        P.finish()
        P.emit(nc, es)
        es.close()
        return nc
    P.dma("sp", dict(out=ident_f[:], in_=identf_d[:, :]), writes=[bf("ident_f")])
    P.dma("sp", dict(out=badaT[:], in_=b_adaT[:, :]), writes=[bf("badaT")])
    P.dma("sp", dict(out=ngTs[:], in_=ngT[:, :]), writes=[bf("ngTs")])
    P.dma("sp", dict(out=c_sb[:], in_=c_all[:, :]), writes=[bf("c_sb")])
    P.op("dve", "tensor_copy", dict(out=ident_b[:], in_=ident_f[:]), reads=[bf("ident_f")], writes=[bf("ident_b")])
    P.op("dve", "memset", dict(ap=ones_b[:], constant=1.0), writes=[bf("ones_b")])
    P.op("dve", "memset", dict(ap=epsb[:], constant=EPS), writes=[bf("epsb")])
    if do_mix:
        P.dma("sp", dict(out=qkgs[:], in_=qkg[:, :]), writes=[bf("qkgs")])
        P.op("dve", "tensor_scalar", dict(out=qkgs2[:, 0:1], in0=qkgs[:, 0:1], scalar1=float(128 ** -0.5),
                                          scalar2=None, op0=ALU.mult), reads=[bf("qkgs")], writes=[bf("qkgs2a")])
        P.op("dve", "tensor_copy", dict(out=qkgs2[:, 1:2], in_=qkgs[:, 1:2]), reads=[bf("qkgs")], writes=[bf("qkgs2b")])

    P.op("act", "activation", dict(out=cs_sb[:], in_=c_sb[:], func=AF.Silu), reads=[bf("c_sb")], writes=[bf("cs_sb")])
    pT = ps[0]
    for k in range(KC):
        P.op("pe", "transpose", dict(out=pT[0:128, k * NR:(k + 1) * NR], in_=cs_sb[:, k * 128:(k + 1) * 128],
                                     identity=ident_f[0:NR, 0:NR]),
             reads=[bf("cs_sb"), bf("ident_f")], writes=[psb[0]])
    P.op("dve", "tensor_copy", dict(out=csT[:].rearrange("p k r -> p (k r)"), in_=pT[:, 0:KC * NR]),
         reads=[psb[0]], writes=[bf("csT")])
    modps = [ps[1], ps[2]]
    for g in range(12):
        wb = wada[g % 2]
        wbuf = bf("wada%d" % (g % 2))
        P.dma("pool", dict(out=wb[:], in_=w_ada[:, g * 512:(g + 1) * 512].rearrange("(k p) c -> p k c", p=128)),
              writes=[wbuf])
        for cc in range(4):
            j = g * 4 + cc
            dst = modps[j // 24]
            dbuf = psb[1 + j // 24]
            jj = j % 24
            for k in range(KC):
                P.op("pe", "matmul", dict(out=dst[:, jj * NR:(jj + 1) * NR], lhsT=wb[:, k, cc * 128:(cc + 1) * 128],
                                          rhs=csT[:, k, :], start=(k == 0), stop=(k == KC - 1)),
                     reads=[wbuf, bf("csT")], writes=[dbuf])
    for half in range(2):
        P.op("dve", "tensor_tensor", dict(
            out=modsb[:, half * 24:(half + 1) * 24, :],
            in0=modps[half][:, 0:24 * NR].rearrange("p (j r) -> p j r", r=NR),
            in1=badaT[:, half * 24:(half + 1) * 24].unsqueeze(2).to_broadcast([128, 24, NR]), op=ALU.add),
            reads=[psb[1 + half], bf("badaT")], writes=[bf("modsb%d" % half)])
    P.op("dve", "tensor_copy", dict(out=sh[:], in_=modsb[:, 0:16, :]), reads=[bf("modsb0")], writes=[bf("sh")])
    P.op("dve", "tensor_scalar", dict(out=gm[:], in0=modsb[:, 16:32, :], scalar1=1.0, scalar2=None, op0=ALU.add),
         reads=[bf("modsb0"), bf("modsb1")], writes=[bf("gm")])
    P.op("dve", "tensor_tensor", dict(out=gm[:], in0=gm[:], in1=ngTs[:].unsqueeze(2).to_broadcast([128, KC, NR]),
                                      op=ALU.mult), reads=[bf("gm"), bf("ngTs")], writes=[bf("gm")])

    if stop == "p0":
        return done()
    P.barrier()
    es0.close()
    if do_mix:
        trin = sb1("trin_s", [128, 128], BF16)
        tric = sb1("tric_s", [128, 128], BF16)
        maskp = sb1("maskp_s", [128, 4, 512], BF16)
        masks = sb1("masks_s", [128, 2, 64], BF16)
        hmask = sb1("hmask_s", [128, 128], F32)
        resetm = sb1("resetm_s", [128, 512], F32)
        lbr = sb1("lbr_s", [128, 4], F32)
        ogain = sb1("ogain_s", [128, 1], F32)
        oneb = sb1("oneb", [128, 1], F32)
        zs_sb = sb1("zs_sb", [128, CH], F32)
        e_sb = [sq_b, sb1("e_sb1", [128, CH], BF16), sb1("e_sb2", [128, CH], BF16)]
        sp_sb = [sb1("sp_sb%d" % i, [128, CH], BF16) for i in range(3)]
        x_sb = [sb1("x_sb%d" % i, [128, CH], BF16) for i in range(2)]
        w_sb = [sb1("w_sb%d" % i, [128, CH], BF16) for i in range(2)]
        bT = sb1("bT", [128, CH], BF16)
        aT = bT
        ksT = sb1("ksT", [128, CH], BF16)
        vs_bf = sb1("vs_bf", [128, 4, 128], BF16)
        t_f = sb1("t_f", [128, CH], F32)
        t_k = sb1("t_k", [128, CH], F32)
        t_lf = sb1("t_lf", [128, CH], F32)
        t_b = sb1("t_b", [128, CH], F32)
        t_eb = sb1("t_eb", [128, CH], F32)
        t_enb = sb1("t_enb", [128, CH], F32)
        t_qs = sb1("t_qs", [128, CH], F32)
        t_ke = sb1("t_ke", [128, CH], F32)
        kv_o = [t_ke[:].rearrange("p (t d) -> p t d", d=128)]
        t_iT = sb1("t_iT", [128, CH], F32)
        t_zh = sb1("t_zh", [128, CH], F32)
        dec = sb1("dec", [128, 8], F32)
        iv = sb1("iv", [128, 4, 128], F32)
        ket = sb1("ket", [128, 2, 4, 128], F32)
        pmask = sb1("pmask_s", [128, 2], F32)
        if mode == "fused":
            vis = sb1("vis_s", [128, 512], F32)
            idx_s = sb1("idx_s_s", [128, 4], mybir.dt.int32)
        attm = sb1("attm", [128, 4, 128], F32)
        Sset = [sb1("Sset%d" % i, [128, 8, 128], F32) for i in range(2)]
        sin = sb1("sin", [128, 8, 128], F32)
        dbg_t = t_lf
        P.dma("pool", dict(out=trin[:], in_=trin_d[:, :]), writes=[bf("trin")])
        P.dma("pool", dict(out=tric[:], in_=tric_d[:, :]), writes=[bf("tric")])
        P.dma("pool", dict(out=maskp[:], in_=maskp_d[:, :, :]), writes=[bf("maskp")])
        P.dma("pool", dict(out=masks[:], in_=masks_d[:, :, :]), writes=[bf("masks")])
        P.dma("sp", dict(out=hmask[:], in_=hmask_d[:, :]), writes=[bf("hmask")])
        P.dma("sp", dict(out=resetm[:], in_=resetm_d[:, :]), writes=[bf("resetm")])
        P.dma("sp", dict(out=pmask[:], in_=pmask_d[:, :]), writes=[bf("pmask")])
        if mode == "fused":
            P.dma("sp", dict(out=vis[:], in_=vis_d[:, :]), writes=[bf("vis")])
            P.dma("sp", dict(out=idx_s[:], in_=idx_s_d[:, :]), writes=[bf("idx_s")])
        P.op("dve", "memset", dict(ap=oneb[:], constant=1.0), writes=[bf("oneb")])

        def head_setup():
            P.barrier()
            P.dma("pool", dict(out=wh[:], in_=w_head[HD["h"]].rearrange("(k p) c -> p k c", p=128)), writes=[bf("wh")])
            P.dma("sp", dict(out=lbr[:, 0:2], in_=lbr_d[HD["h"]]), writes=[bf("lbr")])
            P.dma("sp", dict(out=ogain[:], in_=ogain_d[HD["h"]]), writes=[bf("ogain")])
            P.op("dve", "memset", dict(ap=Sset[(mixers.n + 1) % 2][:, 7, :], constant=0.0), writes=[bf("S_%d_7" % ((mixers.n + 1) % 2))])
            P.op("dve", "tensor_tensor", dict(out=lbr[:, 2:3], in0=lbr[:, 0:1], in1=lbr[:, 1:2], op=ALU.subtract),
                 reads=[bf("lbr")], writes=[bf("lbr")])
            P.op("act", "activation", dict(out=lbr[:, 2:3], in_=lbr[:, 2:3], func=AF.Sigmoid), reads=[bf("lbr")], writes=[bf("lbr")])
            P.op("dve", "tensor_scalar", dict(out=lbr[:, 3:4], in0=lbr[:, 2:3], scalar1=-1.0, scalar2=1.0, op0=ALU.mult, op1=ALU.add),
                 reads=[bf("lbr")], writes=[bf("lbr")])
            sample_ready["done"] = False
            if mode == "fused":
                P.dma("sp", dict(out=S_scr.ap()[0:128, :], in_=Sset[(mixers.n + 1) % 2][:, 7, :]),
                      reads=[bf("S_%d_7" % ((mixers.n + 1) % 2))])
    state = {"tile_i": 0, "tro": 0}

    def proj(j, dst, dbuf, hTb):
        for k in range(KC):
            P.op("pe", "matmul", dict(out=dst[:, :], lhsT=wh[:, k, j * 128:(j + 1) * 128], rhs=hT[:, k, :],
                                      start=(k == 0), stop=(k == KC - 1)),
                 reads=[bf("wh")] + hTb[k], writes=[dbuf])

    def headnorm(src, sbuf_, gain_ap, gbufs, out_ap, outbuf):
        P.op("act", "activation", dict(out=sq_b[:], in_=src[:, :], func=AF.Square), reads=[sbuf_], writes=[bf("e_sb0")])
        P.op("pe", "matmul", dict(out=ps[4][:, :], lhsT=ones_b[:], rhs=sq_b[:], start=True, stop=True),
             reads=[bf("ones_b"), bf("e_sb0")], writes=[psb[4]])
        P.op("act", "activation", dict(out=rstd_f[:], in_=ps[4][:, :], func=AF.Ln, scale=1.0 / 128, bias=epsb[:, 0:1]),
             reads=[psb[4], bf("epsb")], writes=[bf("rstd_f")])
        P.op("act", "activation", dict(out=rstd_f[:], in_=rstd_f[:], func=AF.Exp, scale=-0.5),
             reads=[bf("rstd_f")], writes=[bf("rstd_f")])
        P.op("dve", "scalar_tensor_tensor", dict(out=out_ap, in0=src[:, :], scalar=gain_ap, in1=rstd_f[:],
                                                 op0=ALU.mult, op1=ALU.mult),
             reads=[sbuf_, bf("rstd_f")] + gbufs, writes=[outbuf])

    def attention(nq, q_ap, qbufs, blocks, o_ap, c0):
        n = len(blocks)
        zb = [ps[0], ps[1]]
        A = ps[6]

        def stage1(j):
            blk = blocks[j]
            z = zb[j % 2]
            zbuf = psb[j % 2]
            eb_, ebuf = e_sb[j % 3], bf("e_sb%d" % (j % 3))
            sb_, sbuf_ = sp_sb[j % 3], bf("sp_sb%d" % (j % 3))
            has_mask = blk.get("mask") is not None
            P.op("pe", "matmul", dict(out=z[:, 0:nq], lhsT=blk["kT"], rhs=q_ap, start=True, stop=not has_mask),
                 reads=blk["kbufs"] + qbufs, writes=[zbuf])
            if has_mask:
                P.op("pe", "matmul", dict(out=z[:, 0:nq], lhsT=ident_b[:], rhs=blk["mask"], start=False, stop=True),
                     reads=[bf("ident_b")] + blk["mbufs"], writes=[zbuf])
            if blk.get("bias") is not None:
                P.op("act", "activation", dict(out=eb_[:, 0:nq], in_=z[:, 0:nq], func=AF.Exp, bias=blk["bias"]),
                     reads=[zbuf, bf("vis")], writes=[ebuf])
            else:
                P.op("act", "activation", dict(out=eb_[:, 0:nq], in_=z[:, 0:nq], func=AF.Exp), reads=[zbuf], writes=[ebuf])
            P.op("act", "activation", dict(out=sb_[:, 0:nq], in_=eb_[:, 0:nq], func=AF.Ln, bias=oneb[:, 0:1]),
                 reads=[ebuf, bf("oneb")], writes=[sbuf_])

        def o_mm(j):
            blk = blocks[j]
            P.op("pe", "matmul", dict(out=o_ap, lhsT=blk["v"], rhs=w_sb[j % 2][:, 0:nq], start=(j == 0), stop=(j == n - 1)),
                 reads=blk["vbufs"] + [bf("w_sb%d" % (j % 2))], writes=[psb[7]])

        stage1(0)
        if n > 1:
            stage1(1)
        for j in range(n):
            spj, spbuf = sp_sb[j % 3], bf("sp_sb%d" % (j % 3))
            P.op("pe", "matmul", dict(out=A[:, 0:nq], lhsT=trin[:], rhs=spj[:, 0:nq], start=(j == 0), stop=True, skip_group_check=(j > 0)),
                 reads=[bf("trin"), spbuf], writes=[psb[6]])
            P.op("act", "activation", dict(out=x_sb[j % 2][:, 0:nq], in_=A[:, 0:nq], func=AF.Exp),
                 reads=[psb[6]], writes=[bf("x_sb%d" % (j % 2))])
            if j + 2 < n:
                stage1(j + 2)
            if j >= 1:
                o_mm(j - 1)
            if j < n - 1:
                P.op("pe", "matmul", dict(out=A[:, 0:nq], lhsT=tric[:], rhs=spj[:, 0:nq], start=False, stop=True, skip_group_check=True),
                     reads=[bf("tric"), spbuf], writes=[psb[6]])
            P.op("dve", "tensor_tensor", dict(out=w_sb[j % 2][:, 0:nq], in0=e_sb[j % 3][:, 0:nq], in1=x_sb[j % 2][:, 0:nq], op=ALU.mult),
                 reads=[bf("e_sb%d" % (j % 3)), bf("x_sb%d" % (j % 2))], writes=[bf("w_sb%d" % (j % 2))])
        o_mm(n - 1)

    def tr_out(ch, srcT, srcbuf, dram, extra_copy=None):
        pt = ps[5]
        for tt in range(4):
            P.op("pe", "transpose", dict(out=pt[:, tt * 128:(tt + 1) * 128], in_=srcT[:, tt * 128:(tt + 1) * 128],
                                         identity=ident_f[:]),
                 reads=[srcbuf, bf("ident_f")], writes=[psb[5]])
        i = 0
        ko = kv_o[i]
        kob = bf("t_ke")
        P.op("act", "activation", dict(out=ko[:].rearrange("p t d -> p (t d)"), in_=pt[:, :], func=AF.Copy),
             reads=[psb[5]], writes=[kob])
        if extra_copy is not None:
            extra_copy(pt, psb[5])
        P.dma("sp", dict(out=dram[ch * CH:(ch + 1) * CH, :].rearrange("(t p) d -> p t d", p=128), in_=ko[:]),
              reads=[kob])

    sample_ready = {"done": False}

    def emit_ab(ch, which, src_bf, srcbuf):
        if True:
            return
        r0 = ((which * 8 + HD["h"]) * 136 + ch * 4) * 128
        P.dma("sp", dict(out=ab_scr.ap()[r0:r0 + 512, :].rearrange("(t p) c -> p t c", p=128),
                         in_=src_bf[:].rearrange("p (t c) -> p t c", c=128)), reads=[srcbuf])

    def load_cache(b, src_k=None, src_v=None):
        i = b % 2
        kc_raw = KT[:, 8192 + i * 4096: 8192 + (i + 1) * 4096].rearrange("p (j d) -> p j d", d=128)
        vc = Vr[:, i * 32:(i + 1) * 32, :]
        kcT = KT[:, i * 4096:(i + 1) * 4096]
        P.dma("pool", dict(out=kc_raw, in_=(cache_k[HD["h"], b] if src_k is None else src_k).rearrange("(j p) d -> p j d", p=128)), writes=[bf("kc_raw%d" % i)])
        P.dma("pool", dict(out=vc, in_=(cache_v[HD["h"], b] if src_v is None else src_v).rearrange("(j p) d -> p j d", p=128)), writes=[bf("vc%d" % i)])
        for g in range(4):
            pb = 2 + (g % 2)
            tp = ps[pb][:, :].bitcast(BF16)
            for jj in range(8):
                j = g * 8 + jj
                P.op("pe", "transpose", dict(out=tp[:, jj * 128:(jj + 1) * 128], in_=kc_raw[:, j, :], identity=ident_b[:]),
                     reads=[bf("kc_raw%d" % i), bf("ident_b")], writes=[psb[pb]])
            if g % 2 == 0:
                P.op("act", "activation", dict(out=kcT[:, g * 1024:(g + 1) * 1024], in_=tp[:, :], func=AF.Copy),
                     reads=[psb[pb]], writes=[bf("kcT%d_%d" % (i, g))])
            else:
                P.op("dve", "tensor_copy", dict(out=kcT[:, g * 1024:(g + 1) * 1024], in_=tp[:, :]),
                     reads=[psb[pb]], writes=[bf("kcT%d_%d" % (i, g))])

    def mixers(ch, hTb, own=None):
        n_local = mixers.n
        if own is None:
            mixers.n += 1
        is_p = ch < NCH_P
        lite = (mode == "fused" and own is None)
        if mode != "fused":
            proj(3, ps[3], psb[3], hTb)
            P.op("act", "activation", dict(out=zs_sb[:], in_=ps[3][:, :], func=AF.Silu), reads=[psb[3]], writes=[bf("zs_sb")])
        if "att" in flags and mode != "fused":
            if is_p:
                blocks = []
                for j in range(4 * ch + 3, -1, -1):
                    blk = dict(kT=KT[:, j * 128:(j + 1) * 128], kbufs=[bf("KT%d" % (j // 4))],
                               v=Vr[:, j, :], vbufs=[bf("Vr%d" % (j // 4))])
                    if j >= 4 * ch:
                        blk["mask"] = maskp[:, j - 4 * ch, :]
                        blk["mbufs"] = [bf("maskp")]
                    blocks.append(blk)
                attention(CH, QT[:], [bf("QT")], blocks, ps[7][:, :], 0)
            else:
                if not sample_ready["done"]:
                    sample_ready["done"] = True
                    P.barrier()
                    load_cache((ch - NCH_P) * 8)
                for c in range(8):
                    b = (ch - NCH_P) * 8 + c
                    i = b % 2
                    if b + 1 < NB:
                        load_cache(b + 1)
                    tt, par = c // 2, c % 2
                    kcT = KT[:, i * 4096:(i + 1) * 4096]
                    vc = Vr[:, i * 32:(i + 1) * 32, :]
                    blocks = [dict(kT=ksT[:, tt * 128:(tt + 1) * 128], kbufs=[bf("ksT")], v=vs_bf[:, tt, :], vbufs=[bf("vs_bf")],
                                   mask=masks[:, par, :], mbufs=[bf("masks")])]
                    for j in range(31, -1, -1):
                        blocks.append(dict(kT=kcT[:, j * 128:(j + 1) * 128], kbufs=[bf("kcT%d_%d" % (i, j // 8))],
                                           v=vc[:, j, :], vbufs=[bf("vc%d" % i)]))
                    attention(TSQ, QT[:, c * TSQ:(c + 1) * TSQ], [bf("QT")], blocks, ps[7][:, c * TSQ:(c + 1) * TSQ], c * TSQ)
            P.op("dve", "tensor_tensor", dict(out=aT[:], in0=ps[7][:, :], in1=zs_sb[:], op=ALU.mult),
                 reads=[psb[7], bf("zs_sb")], writes=[bf("bT")])
            emit_ab(ch, 0, aT, bf("bT"))
            if mode == "mix":
                P.op("dve", "tensor_copy", dict(out=dbg_t[:], in_=aT[:]), reads=[bf("bT")], writes=[bf("t_lf")])
                P.dma("sp", dict(out=dbg_a[:, ch * CH:(ch + 1) * CH], in_=dbg_t[:]), reads=[bf("t_lf")])
        if "hg" not in flags:
            return
        if not is_p:
            if own is None:
                b0_ = (ch - NCH_P) * 8
                P.dma("sp", dict(out=sin[:], in_=s_in[HD["h"], b0_:b0_ + 8].rearrange("b k v -> k b v")), writes=[bf("sin")])
            else:
                P.dma("sp", dict(out=sin[:, 0:2, :], in_=s_in_own[HD["h"]].rearrange("b k v -> k b v")), writes=[bf("sin")])
        proj(4, ps[2], psb[2], hTb)
        P.op("act", "activation", dict(out=t_f[:], in_=ps[2][:, :], func=AF.Sigmoid), reads=[psb[2]], writes=[bf("t_f")])
        if not lite:
            proj(6, ps[3], psb[3], hTb)
            P.op("act", "activation", dict(out=t_qs[:], in_=ps[3][:, :], func=AF.Silu), reads=[psb[3]], writes=[bf("t_qs")])
            proj(7, ps[2], psb[2], hTb)
            P.op("act", "activation", dict(out=t_zh[:], in_=ps[2][:, :], func=AF.Silu), reads=[psb[2]], writes=[bf("t_zh")])
        proj(5, ps[3], psb[3], hTb)
        P.op("act", "activation", dict(out=t_iT[:], in_=ps[3][:, :], func=AF.Copy), reads=[psb[3]], writes=[bf("t_iT")])
        P.op("dve", "tensor_scalar", dict(out=t_f[:], in0=t_f[:], scalar1=lbr[:, 3:4], scalar2=lbr[:, 2:3], op0=ALU.mult, op1=ALU.add),
             reads=[bf("t_f"), bf("lbr")], writes=[bf("t_f")])
        P.op("act", "activation", dict(out=t_lf[:], in_=t_f[:], func=AF.Ln), reads=[bf("t_f")], writes=[bf("t_lf")])
        P.op("dve", "tensor_scalar", dict(out=t_k[:], in0=t_f[:], scalar1=-1.0, scalar2=1.0, op0=ALU.mult, op1=ALU.add),
             reads=[bf("t_f")], writes=[bf("t_k")])
        P.op("dve", "tensor_tensor_scan", dict(out=t_b[:], data0=resetm[:], data1=t_lf[:], initial=0.0, op0=ALU.mult, op1=ALU.add),
             reads=[bf("resetm"), bf("t_lf")], writes=[bf("t_b")])
        if not lite:
            P.op("act", "activation", dict(out=t_eb[:], in_=t_b[:], func=AF.Exp), reads=[bf("t_b")], writes=[bf("t_eb")])
        P.op("act", "activation", dict(out=t_enb[:], in_=t_b[:], func=AF.Exp, scale=-1.0), reads=[bf("t_b")], writes=[bf("t_enb")])
        P.op("act", "activation", dict(out=dec[:].unsqueeze(2), in_=t_b[:].rearrange("p (c t) -> p c t", t=64)[:, :, 63:64], func=AF.Exp),
             reads=[bf("t_b")], writes=[bf("dec")])
        if not lite:
            P.op("dve", "tensor_tensor", dict(out=t_eb[:], in0=t_qs[:], in1=t_eb[:], op=ALU.mult),
                 reads=[bf("t_qs"), bf("t_eb")], writes=[bf("t_eb")])
        P.op("dve", "tensor_tensor", dict(out=t_enb[:], in0=t_k[:], in1=t_enb[:], op=ALU.mult),
             reads=[bf("t_k"), bf("t_enb")], writes=[bf("t_enb")])
        P.op("dve", "tensor_tensor", dict(out=t_ke[:].rearrange("p (c t) -> p c t", t=64),
                                          in0=t_enb[:].rearrange("p (c t) -> p c t", t=64),
                                          in1=dec[:].unsqueeze(2).to_broadcast([128, 8, 64]), op=ALU.mult),
             reads=[bf("t_enb"), bf("dec")], writes=[bf("t_ke")])
        hgs = [int(f[3:]) for f in flags if f.startswith("hgs")]
        hgs = hgs[0] if hgs else 99
        if hgs <= 1:
            return
        for which in range(2):
            src, sbuf_ = (t_iT, bf("t_iT")) if which == 0 else (t_ke, bf("t_ke"))
            for tt in range(4):
                P.op("pe", "transpose", dict(out=ps[5][:, tt * 128:(tt + 1) * 128], in_=src[:, tt * 128:(tt + 1) * 128], identity=ident_f[:]),
                     reads=[sbuf_, bf("ident_f")], writes=[psb[5]])
            if which == 0:
                P.op("act", "activation", dict(out=iv[:].rearrange("p t d -> p (t d)"), in_=ps[5][:, :], func=AF.Copy),
                     reads=[psb[5]], writes=[bf("iv")])
            else:
                for ab in range(2):
                    P.op("act", "activation", dict(out=ket[:, ab, :, :].rearrange("p t d -> p (t d)"), in_=ps[5][:, :], func=AF.Copy,
                                                   scale=pmask[:, ab:ab + 1]),
                         reads=[psb[5], bf("pmask")], writes=[bf("ket%d" % ab)])
        if hgs <= 2:
            return
        for c in range(8):
            pb = c // 4
            r0 = (c % 2) * 64
            P.op("pe", "matmul", dict(out=ps[pb][:, (c % 4) * 128:(c % 4 + 1) * 128], lhsT=ket[:, c % 2, c // 2, :],
                                      rhs=iv[:, c // 2, :], start=True, stop=True),
                 reads=[bf("ket%d" % (c % 2)), bf("iv")], writes=[psb[pb]])
        if hgs <= 3:
            return
        if not lite:
            for p_ in range(4):
                P.op("pe", "matmul", dict(out=ps[4][:, p_ * 128:(p_ + 1) * 128], lhsT=t_enb[:, p_ * 128:(p_ + 1) * 128],
                                          rhs=t_eb[:, p_ * 128:(p_ + 1) * 128], start=True, stop=True),
                     reads=[bf("t_enb"), bf("t_eb")], writes=[psb[4]])
            P.op("dve", "tensor_tensor", dict(out=attm[:], in0=ps[4][:, :].rearrange("p (a t) -> p a t", t=128),
                                              in1=hmask[:].unsqueeze(1).to_broadcast([128, 4, 128]), op=ALU.mult),
                 reads=[psb[4], bf("hmask")], writes=[bf("attm")])
        if hgs <= 4:
            return
        cur = Sset[n_local % 2]
        prev_set = Sset[(n_local + 1) % 2]
        Sprev = []
        for c in range(8):
            if is_p:
                if c == 0:
                    sp_ap, sp_buf = prev_set[:, 7, :], bf("S_%d_7" % ((n_local + 1) % 2))
                else:
                    sp_ap, sp_buf = cur[:, c - 1, :], bf("S_%d_%d" % (n_local % 2, c - 1))
            elif own is None:
                sp_ap, sp_buf = sin[:, c, :], bf("sin")
            else:
                sp_ap, sp_buf = sin[:, min(c, 1), :], bf("sin")
            Sprev.append((sp_ap, sp_buf))
            pb = c // 4
            P.op("dve", "scalar_tensor_tensor", dict(out=cur[:, c, :], in0=sp_ap, scalar=dec[:, c:c + 1],
                                                     in1=ps[pb][:, (c % 4) * 128:(c % 4 + 1) * 128], op0=ALU.mult, op1=ALU.add),
                 reads=[sp_buf, bf("dec"), psb[pb]], writes=[bf("S_%d_%d" % (n_local % 2, c))])
        mixers.first = False
        if lite:
            if is_p:
                P.dma("sp", dict(out=S_scr.ap()[(ch + 1) * 128:(ch + 2) * 128, :], in_=cur[:, 7, :]),
                      reads=[bf("S_%d_7" % (n_local % 2))])
                if ch == NCH_P - 1:
                    P.dma("sp", dict(out=s_out[HD["h"], 0], in_=cur[:, 7, :]), reads=[bf("S_%d_7" % (n_local % 2))])
            else:
                b0 = (ch - NCH_P) * 8
                P.dma("sp", dict(out=s_out[HD["h"], 1 + b0:1 + b0 + 8].rearrange("b k v -> k b v"), in_=cur[:]),
                      reads=[bf("S_%d_%d" % (n_local % 2, c)) for c in range(8)])
            return
        if hgs <= 5:
            return
        for p_ in range(4):
            P.op("pe", "matmul", dict(out=ps[6][:, p_ * 128:(p_ + 1) * 128], lhsT=iv[:, p_, :], rhs=attm[:, p_, :], start=True, stop=False),
                 reads=[bf("iv"), bf("attm")], writes=[psb[6]])
            for h2 in range(2):
                c = 2 * p_ + h2
                sp_ap, sp_buf = Sprev[c]
                P.op("pe", "matmul", dict(out=ps[6][:, c * 64:(c + 1) * 64], lhsT=sp_ap, rhs=t_eb[:, c * 64:(c + 1) * 64],
                                          start=False, stop=(h2 == 1)),
                     reads=[sp_buf, bf("t_eb")], writes=[psb[6]])
        if hgs <= 6:
            return
        headnorm(ps[6], psb[6], ogain[:, 0:1], [bf("ogain")], t_ke[:], bf("t_ke"))
        P.op("dve", "tensor_tensor", dict(out=bT[:], in0=t_ke[:], in1=t_zh[:], op=ALU.mult),
             reads=[bf("t_ke"), bf("t_zh")], writes=[bf("bT")])
        if own is not None:
            nt_ = 4 if is_p else 1
            P.dma("sp", dict(out=b_own_scr.ap()[HD["h"], own["slot"] * 4:own["slot"] * 4 + nt_].rearrange("t p c -> p t c"),
                             in_=bT[:, 0:nt_ * 128].rearrange("p (t c) -> p t c", c=128)), reads=[bf("bT")])
            return
        emit_ab(ch, 1, bT, bf("bT"))
        if mode == "mix":
            P.op("dve", "tensor_copy", dict(out=dbg_t[:], in_=bT[:]), reads=[bf("bT")], writes=[bf("t_lf")])
            P.dma("sp", dict(out=dbg_b[:, ch * CH:(ch + 1) * CH], in_=dbg_t[:]), reads=[bf("t_lf")])
        if is_p:
            if ch == NCH_P - 1 or (debug and ch == chunks[-1]):
                P.dma("sp", dict(out=s_out[HD["h"], 0], in_=cur[:, 7, :]), reads=[bf("S_%d_7" % (n_local % 2))])
        else:
            b0 = (ch - NCH_P) * 8
            P.dma("sp", dict(out=s_out[HD["h"], 1 + b0:1 + b0 + 8].rearrange("b k v -> k b v"), in_=cur[:]),
                  reads=[bf("S_%d_%d" % (n_local % 2, c)) for c in range(8)])
    mixers.n = 0
    mixers.first = True

    def build_hT(src_dram, row0, ntiles, segs_fn):
        for tt in range(ntiles):
            t0 = row0 + tt * 128
            i = state["tile_i"] % 2
            state["tile_i"] += 1
            xb = xbuf[i]
            xbb = bf("xbuf%d" % i)
            st = stat[i]
            stb = bf("stat%d" % i)
            P.dma("sp", dict(out=xb[:], in_=src_dram[t0:t0 + 128, :]), writes=[xbb])
            P.op("act", "activation", dict(out=junk[:], in_=xb[:], func=AF.Square, accum_out=st[:, 0:1]),
                 reads=[xbb], writes=[bf("xn"), stb])
            P.op("act", "activation", dict(out=st[:, 1:2], in_=st[:, 0:1], func=AF.Ln, scale=1.0 / D, bias=epsb[:, 0:1]),
                 reads=[stb, bf("epsb")], writes=[stb])
            P.op("act", "activation", dict(out=st[:, 2:3], in_=st[:, 1:2], func=AF.Exp, scale=-0.5),
                 reads=[stb], writes=[stb])
            P.op("dve", "tensor_scalar", dict(out=xn[:], in0=xb[:], scalar1=st[:, 2:3], scalar2=None, op0=ALU.mult),
                 reads=[xbb, stb], writes=[bf("xn")])
            for half in range(2):
                tp = ps[half][:, :].bitcast(BF16)
                for kk in range(8):
                    k = half * 8 + kk
                    P.op("pe", "transpose", dict(out=tp[:, kk * 128:(kk + 1) * 128], in_=xn[:, k * 128:(k + 1) * 128],
                                                 identity=ident_b[:]),
                         reads=[bf("xn"), bf("ident_b")], writes=[psb[half]])
                for kk in range(8):
                    k = half * 8 + kk
                    segs = segs_fn(tt)
                    for (c0, c1, r) in segs:
                        o_ap = hT[:, k, tt * 128 + c0:tt * 128 + c1]
                        i_ap = tp[:, kk * 128 + c0:kk * 128 + c1]
                        if len(segs) == 1:
                            wr = [bf("hT_%d_%d_0" % (k, tt)), bf("hT_%d_%d_64" % (k, tt))]
                        else:
                            wr = [bf("hT_%d_%d_%d" % (k, tt, c0))]
                        rd = [psb[half], bf("gm"), bf("sh")]
                        if half == 0:
                            P.op("act", "activation", dict(out=o_ap, in_=i_ap, func=AF.Identity,
                                                           scale=gm[:, k, r:r + 1], bias=sh[:, k, r:r + 1]), reads=rd, writes=wr)
                        else:
                            P.op("dve", "tensor_scalar", dict(out=o_ap, in0=i_ap, scalar1=gm[:, k, r:r + 1],
                                                              scalar2=sh[:, k, r:r + 1], op0=ALU.mult, op1=ALU.add),
                                 reads=rd, writes=wr)
        hTb = {}
        for k in range(KC):
            lst = []
            for tt in range(ntiles):
                lst.append(bf("hT_%d_%d_0" % (k, tt)))
                lst.append(bf("hT_%d_%d_64" % (k, tt)))
            hTb[k] = lst
        return hTb

    def own_slots():
        P.barrier()
        for s_ in range(5):
            is_p = s_ < 4
            nt = 4 if is_p else 1
            N = nt * 128
            if is_p:
                segs_fn = lambda tt: [(0, 128, 0)]
            else:
                segs_fn = lambda tt: [(0, 64, RS0), (64, 128, RS0 + 1)]
            hTb = build_hT(x_own, s_ * CH, nt, segs_fn)
            proj(0, ps[2], psb[2], hTb)
            headnorm(ps[2], psb[2], qkgs2[:, 0:1], [bf("qkgs2a")], QT[:], bf("QT"))
            proj(1, ps[3], psb[3], hTb)
            headnorm(ps[3], psb[3], qkgs2[:, 1:2], [bf("qkgs2b")], kn_f[:], bf("kn_f"))
            P.op("act", "activation", dict(out=ksT[:], in_=kn_f[:], func=AF.Copy), reads=[bf("kn_f")], writes=[bf("ksT")])
            proj(2, ps[2], psb[2], hTb)
            P.op("dve", "tensor_copy", dict(out=vT_f[:], in_=ps[2][:, :]), reads=[psb[2]], writes=[bf("vT_f")])
            for tt in range(nt):
                P.op("pe", "transpose", dict(out=ps[5][:, tt * 128:(tt + 1) * 128], in_=vT_f[:, tt * 128:(tt + 1) * 128],
                                             identity=ident_f[:]),
                     reads=[bf("vT_f"), bf("ident_f")], writes=[psb[5]])
            P.op("dve", "tensor_copy", dict(out=vs_bf[:, 0:nt, :].rearrange("p t d -> p (t d)"), in_=ps[5][:, 0:N]),
                 reads=[psb[5]], writes=[bf("vs_bf")])
            proj(3, ps[3], psb[3], hTb)
            P.op("act", "activation", dict(out=zs_sb[:], in_=ps[3][:, :], func=AF.Silu), reads=[psb[3]], writes=[bf("zs_sb")])
            if is_p:
                blocks = []
                for i_ in range(3, -1, -1):
                    blocks.append(dict(kT=ksT[:, i_ * 128:(i_ + 1) * 128], kbufs=[bf("ksT")], v=vs_bf[:, i_, :], vbufs=[bf("vs_bf")],
                                       mask=maskp[:, i_, :], mbufs=[bf("maskp")]))
                for j in range((128 if _CONTIG else 32 * (s_ + 1)) - 1, -1, -1):
                    blocks.append(dict(kT=KT[:, j * 128:(j + 1) * 128], kbufs=[bf("KT%d" % (j // 4))],
                                       v=Vr[:, j, :], vbufs=[bf("Vr%d" % (j // 4))], bias=vis[:, s_ * 128 + j:s_ * 128 + j + 1]))
                attention(CH, QT[:], [bf("QT")], blocks, ps[7][:, :], 0)
            else:
                P.barrier()
                load_cache(0, cache_k[HD["h"], 0], cache_v[HD["h"], 0])
                for par in range(2):
                    if par == 0:
                        load_cache(1, cache_k[HD["h"], 1], cache_v[HD["h"], 1])
                    i = par
                    kcT = KT[:, i * 4096:(i + 1) * 4096]
                    vc = Vr[:, i * 32:(i + 1) * 32, :]
                    blocks = [dict(kT=ksT[:, 0:128], kbufs=[bf("ksT")], v=vs_bf[:, 0, :], vbufs=[bf("vs_bf")],
                                   mask=masks[:, par, :], mbufs=[bf("masks")])]
                    for j in range(31, -1, -1):
                        blocks.append(dict(kT=kcT[:, j * 128:(j + 1) * 128], kbufs=[bf("kcT%d_%d" % (i, j // 8))],
                                           v=vc[:, j, :], vbufs=[bf("vc%d" % i)]))
                    attention(TSQ, QT[:, par * TSQ:(par + 1) * TSQ], [bf("QT")], blocks, ps[7][:, par * TSQ:(par + 1) * TSQ], 0)
            P.op("dve", "tensor_tensor", dict(out=aT[:, 0:N], in0=ps[7][:, 0:N], in1=zs_sb[:, 0:N], op=ALU.mult),
                 reads=[psb[7], bf("zs_sb")], writes=[bf("bT")])
            P.dma("sp", dict(out=a_own_scr.ap()[HD["h"], s_ * 4:s_ * 4 + nt].rearrange("t p c -> p t c"),
                             in_=aT[:, 0:N].rearrange("p (t c) -> p t c", c=128)), reads=[bf("bT")])
            if is_p:
                pi = (mixers.n + 1) % 2
                P.dma("pool", dict(out=Sset[pi][:, 7, :], out_offset=None, in_=S_scr.ap(),
                                   in_offset=bass.IndirectOffsetOnAxis(ap=idx_s[:, s_:s_ + 1], axis=0)),
                      reads=[bf("idx_s")], writes=[bf("S_%d_7" % pi)], meth="indirect_dma_start")
            mixers(0 if is_p else NCH_P, hTb, own=dict(slot=s_))

    if mode == "fused" and "zero_scr" in flags:
        P.op("dve", "memset", dict(ap=aT[:], constant=0.0), writes=[bf("bT")])
        P.op("dve", "memset", dict(ap=Sset[0][:, 7, :], constant=0.0), writes=[bf("S_0_7")])
        for e_ in range(33):
            P.dma("sp", dict(out=S_scr.ap()[e_ * 128:(e_ + 1) * 128, :], in_=Sset[0][:, 7, :]), reads=[bf("S_0_7")])
        P.op("dve", "memset", dict(ap=KT[:], constant=0.0), writes=[bf("KT%d" % i_) for i_ in range(32)])
        P.op("dve", "memset", dict(ap=Vr[:].rearrange("p j d -> p (j d)"), constant=0.0), writes=[bf("Vr%d" % i_) for i_ in range(32)])
        P.barrier()
    for hd in range(NH if do_mix else 0):
        HD["h"] = hd
        head_setup()
        for ch in chunks:
            if ch < NCH_P:
                segs_fn = lambda tt: [(0, 128, 0)]
            else:
                segs_fn = (lambda ch: lambda tt: [(0, 64, 1 + ((ch - NCH_P) * 4 + tt) * 2), (64, 128, 2 + ((ch - NCH_P) * 4 + tt) * 2)])(ch)
            hTb = build_hT(x_all, ch * CH - x_base, 4, segs_fn)

            if stop in ("hT", "ev_act", "ev_dve"):
                return done()
            if mode != "fused":
                proj(0, ps[2], psb[2], hTb)
                headnorm(ps[2], psb[2], qkgs2[:, 0:1], [bf("qkgs2a")], QT[:], bf("QT"))
            if stop == "q":
                return done()
            proj(1, ps[3], psb[3], hTb)
            headnorm(ps[3], psb[3], qkgs2[:, 1:2], [bf("qkgs2b")], kn_f[:], bf("kn_f"))
            if ch < NCH_P:
                P.op("act", "activation", dict(out=KT[:, ch * CH:(ch + 1) * CH], in_=kn_f[:], func=AF.Copy),
                     reads=[bf("kn_f")], writes=[bf("KT%d" % ch)])
            else:
                P.op("act", "activation", dict(out=ksT[:], in_=kn_f[:], func=AF.Copy), reads=[bf("kn_f")], writes=[bf("ksT")])
            tr_out(ch, kn_f, bf("kn_f"), k_out[HD["h"]])
            if stop == "k":
                return done()
            proj(2, ps[2], psb[2], hTb)
            P.op("dve", "tensor_copy", dict(out=vT_f[:], in_=ps[2][:, :]), reads=[psb[2]], writes=[bf("vT_f")])

            def vcopy(pt, ptb, ch=ch):
                if ch < NCH_P:
                    P.op("dve", "tensor_copy", dict(out=Vr[:, ch * 4:(ch + 1) * 4, :].rearrange("p t d -> p (t d)"), in_=pt[:, :]),
                         reads=[ptb], writes=[bf("Vr%d" % ch)])
                else:
                    P.op("dve", "tensor_copy", dict(out=vs_bf[:].rearrange("p t d -> p (t d)"), in_=pt[:, :]),
                         reads=[ptb], writes=[bf("vs_bf")])
            tr_out(ch, vT_f, bf("vT_f"), v_out[HD["h"]], extra_copy=vcopy)
            mixers(ch, hTb)
        if mode == "fused" and len(chunks) > 0:
            own_slots()

    if do_out:
        P.barrier()
        es1.close()
        gateP = sb("gateP", [128, D], F32)
        gateS = sb("gateS", [128, D], F32)
        es3 = ExitStack()
        csbc = [es3.enter_context(nc.sbuf_tensor("csbc%d" % i, [128, KC, 128], BF16)) for i in range(2)]
        bgate = es3.enter_context(nc.sbuf_tensor("bgate", [128, D], F32))
        wada2 = [es3.enter_context(nc.sbuf_tensor("wada2_%d" % i, [128, KC, 512], BF16)) for i in range(2)]
        P.dma("sp", dict(out=bgate[:], in_=b_gate_bc[:, :]), writes=[bf("bgate")])
        P.op("dve", "tensor_copy", dict(out=csbc[0][:], in_=csT[:, :, 0:1].to_broadcast([128, KC, 128])),
             reads=[bf("csT")], writes=[bf("csbc0")])
        P.op("dve", "tensor_copy", dict(out=csbc[1][:, :, 0:64], in_=csT[:, :, RS0:RS0 + 1].to_broadcast([128, KC, 64])),
             reads=[bf("csT")], writes=[bf("csbc1a")])
        P.op("dve", "tensor_copy", dict(out=csbc[1][:, :, 64:128], in_=csT[:, :, RS0 + 1:RS0 + 2].to_broadcast([128, KC, 64])),
             reads=[bf("csT")], writes=[bf("csbc1b")])
        for g in range(8, 12):
            wb = wada2[g % 2]
            wbuf = bf("wada2_%d" % (g % 2))
            P.dma("pool", dict(out=wb[:], in_=w_ada[:, g * 512:(g + 1) * 512].rearrange("(k p) c -> p k c", p=128)),
                  writes=[wbuf])
            for which in range(2):
                pb = 3 + which
                for k in range(KC):
                    P.op("pe", "matmul", dict(out=ps[pb][:, :], lhsT=csbc[which][:, k, :], rhs=wb[:, k, :],
                                              start=(k == 0), stop=(k == KC - 1)),
                         reads=[wbuf, bf("csbc0"), bf("csbc1a"), bf("csbc1b")], writes=[psb[pb]])
                gt = gateP if which == 0 else gateS
                P.op("dve", "tensor_tensor", dict(out=gt[:, (g - 8) * 512:(g - 7) * 512], in0=ps[pb][:, :],
                                                  in1=bgate[:, (g - 8) * 512:(g - 7) * 512], op=ALU.add),
                     reads=[psb[pb], bf("bgate")], writes=[bf("gate%d_%d" % (which, g - 8))])
        P.barrier()
        es3.close()
        wslot = [sb("wslot%d" % i, [128, 24576], BF16) for i in range(2)]
        abT = sb("abT", [128, 2, 8, CH], BF16)
        mT = sb("mT", [128, KC, CH], BF16)
        sgA = sb("sgA", [128, CH], F32)
        sgB = sb("sgB", [128, CH], F32)
        tmp1 = sb("tmp1", [128, CH], F32)
        tmp2 = sb("tmp2", [128, CH], F32)
        ysl = [sb("ysl%d" % i, [128, CH], F32) for i in range(2)]
        xsl = [sb("xsl%d" % i, [128, CH], F32) for i in range(2)]
        gi = 0
        yi = 0
        for o in range(5):
            N = CH if o < 4 else OWN_S
            NT = N // 128
            if o < 4:
                segs_fn = lambda tt: [(0, 128, 0)]
            else:
                segs_fn = lambda tt: [(0, 64, RS0), (64, 128, RS0 + 1)]
            hTb = build_hT(x_own, o * CH, NT, segs_fn)
            if mode == "out":
                for ab in range(2):
                    P.dma("pool", dict(out=abT[:, ab, :, 0:N], in_=ab_own[ab, :, :, o * CH:o * CH + N].rearrange("h p t -> p h t")),
                          writes=[bf("abT%d" % ab)])
                abbufs = {0: [bf("abT0")], 1: [bf("abT1")]}
            else:
                abbufs = {0: [], 1: []}
                for h in range(8):
                    P.dma("sp", dict(out=abT[:, 0, h, 0:N].rearrange("p (t c) -> p t c", c=128),
                                     in_=a_own_scr.ap()[h, o * 4:o * 4 + NT].rearrange("t p c -> p t c")),
                          writes=[bf("abT_a%d" % h)])
                    abbufs[0].append(bf("abT_a%d" % h))
                for h in range(8):
                    P.dma("sp", dict(out=abT[:, 1, h, 0:N].rearrange("p (t c) -> p t c", c=128),
                                     in_=b_own_scr.ap()[h, o * 4:o * 4 + NT].rearrange("t p c -> p t c")),
                          writes=[bf("abT_b%d" % h)])
                    abbufs[1].append(bf("abT_b%d" % h))
            for g in range(4):
                sl = gi % 2
                gi += 1
                ws = wslot[sl]
                wgA = ws[:, 0:8192].rearrange("p (k c) -> p k c", c=512)
                wgB = ws[:, 8192:16384].rearrange("p (k c) -> p k c", c=512)
                wbA = ws[:, 16384:20480].rearrange("p (h c) -> p h c", c=512)
                wbB = ws[:, 20480:24576].rearrange("p (h c) -> p h c", c=512)
                P.dma("pool", dict(out=wgA, in_=w_gate[:, g * 512:(g + 1) * 512].rearrange("(k p) c -> p k c", p=128)),
                      writes=[bf("ws%d_A" % sl)])
                P.dma("pool", dict(out=wgB, in_=w_gate[:, D + g * 512:D + (g + 1) * 512].rearrange("(k p) c -> p k c", p=128)),
                      writes=[bf("ws%d_B" % sl)])
                P.dma("pool", dict(out=wbA, in_=w_bsb[:, g * 512:(g + 1) * 512].rearrange("(h p) c -> p h c", p=128)),
                      writes=[bf("ws%d_C" % sl)])
                P.dma("pool", dict(out=wbB, in_=w_bhg[:, g * 512:(g + 1) * 512].rearrange("(h p) c -> p h c", p=128)),
                      writes=[bf("ws%d_D" % sl)])
                for cc in range(4):
                    jc = g * 4 + cc
                    for (wg, wgbuf, pb, sg, sgbuf) in ((wgA, bf("ws%d_A" % sl), 2, sgA, bf("sgA")), (wgB, bf("ws%d_B" % sl), 3, sgB, bf("sgB"))):
                        for k in range(KC):
                            P.op("pe", "matmul", dict(out=ps[pb][:, 0:N], lhsT=wg[:, k, cc * 128:(cc + 1) * 128], rhs=hT[:, k, 0:N],
                                                      start=(k == 0), stop=(k == KC - 1)),
                                 reads=[wgbuf] + hTb[k], writes=[psb[pb]])
                        P.op("act", "activation", dict(out=sg[:, 0:N], in_=ps[pb][:, 0:N], func=AF.Sigmoid),
                             reads=[psb[pb]], writes=[sgbuf])
                    for (wbr, wbbuf, pb, ab) in ((wbA, bf("ws%d_C" % sl), 4, 0), (wbB, bf("ws%d_D" % sl), 5, 1)):
                        for h in range(8):
                            P.op("pe", "matmul", dict(out=ps[pb][:, 0:N], lhsT=wbr[:, h, cc * 128:(cc + 1) * 128], rhs=abT[:, ab, h, 0:N],
                                                      start=(h == 0), stop=(h == 7)),
                                 reads=[wbbuf] + abbufs[ab], writes=[psb[pb]])
                    P.op("dve", "tensor_tensor", dict(out=tmp1[:, 0:N], in0=ps[4][:, 0:N], in1=sgA[:, 0:N], op=ALU.mult),
                         reads=[psb[4], bf("sgA")], writes=[bf("tmp1")])
                    P.op("dve", "tensor_tensor", dict(out=tmp2[:, 0:N], in0=ps[5][:, 0:N], in1=sgB[:, 0:N], op=ALU.mult),
                         reads=[psb[5], bf("sgB")], writes=[bf("tmp2")])
                    P.op("pool", "tensor_tensor", dict(out=mT[:, jc, 0:N], in0=tmp1[:, 0:N], in1=tmp2[:, 0:N], op=ALU.add),
                         reads=[bf("tmp1"), bf("tmp2")], writes=[bf("mT%d" % jc)])
            for cg in range(4):
                sl = gi % 2
                gi += 1
                ws = wslot[sl]
                wo = ws[:, 0:8192].rearrange("p (k c) -> p k c", c=512)
                P.dma("pool", dict(out=wo, in_=w_o[:, cg * 512:(cg + 1) * 512].rearrange("(k p) c -> p k c", p=128)),
                      writes=[bf("ws%d_A" % sl)])
                for tt in range(NT):
                    pb = 6 + (yi % 2)
                    ys = ysl[yi % 2]
                    xs = xsl[yi % 2]
                    ysb = bf("ysl%d" % (yi % 2))
                    xsb = bf("xsl%d" % (yi % 2))
                    yi += 1
                    r0 = o * CH + tt * 128
                    P.dma("sp", dict(out=xs[:], in_=x_own[r0:r0 + 128, cg * 512:(cg + 1) * 512]), writes=[xsb])
                    for k in range(KC):
                        P.op("pe", "matmul", dict(out=ps[pb][:, :], lhsT=mT[:, k, tt * 128:(tt + 1) * 128], rhs=wo[:, k, :],
                                                  start=(k == 0), stop=(k == KC - 1)),
                             reads=[bf("ws%d_A" % sl), bf("mT%d" % k)], writes=[psb[pb]])
                    gt = gateP if o < 4 else gateS
                    which = 0 if o < 4 else 1
                    P.op("dve", "tensor_tensor", dict(out=ys[:], in0=ps[pb][:, :], in1=gt[:, cg * 512:(cg + 1) * 512], op=ALU.mult),
                         reads=[psb[pb], bf("gate%d_%d" % (which, cg))], writes=[ysb])
                    P.op("pool", "tensor_tensor", dict(out=ys[:], in0=ys[:], in1=xs[:], op=ALU.add),
                         reads=[ysb, xsb], writes=[ysb])
                    P.dma("sp", dict(out=y_own[r0:r0 + 128, cg * 512:(cg + 1) * 512], in_=ys[:]), reads=[ysb])

    P.finish()
    P.emit(nc, es)
    if not do_out:
        es1.close()
    es.close()
    return nc


def _consts():
    j = np.arange(128)[:, None]
    k = np.arange(128)[None, :]
    trin = np.where(j >= k, -1.0, 0.0).astype(np.float32)
    tric = np.where(j < k, -1.0, 0.0).astype(np.float32)
    q = np.arange(512)[None, None, :]
    kk = np.arange(128)[:, None, None]
    i = np.arange(4)[None, :, None]
    maskp = np.where(q > kk + 128 * i, 0.0, -30000.0).astype(np.float32)
    qs = np.arange(64)[None, :]
    ks = np.arange(128)[:, None]
    m_even = np.where((ks < 64) & (qs > ks), 0.0, -30000.0)
    m_odd = np.where((ks >= 64) & (qs > ks - 64), 0.0, -30000.0)
    masks = np.stack([m_even, m_odd], axis=1).astype(np.float32)
    s_ = np.arange(128)[:, None]
    t_ = np.arange(128)[None, :]
    hmask = ((s_ // 64 == t_ // 64) & (s_ <= t_)).astype(np.float32)
    resetm = np.ones((128, 512), np.float32)
    resetm[:, ::64] = 0.0
    return {"identf": np.eye(128, dtype=np.float32), "trin": trin, "tric": tric, "maskp": maskp, "masks": masks,
            "hmask": hmask, "resetm": resetm,
            "pmask": np.stack([(np.arange(128) < 64), (np.arange(128) >= 64)], axis=1).astype(np.float32)}


def _f32(a):
    return np.ascontiguousarray(np.asarray(a, dtype=np.float32))


def make_in_maps(inp, cores=None, x_rows=TT, x_base=0):
    f32 = _f32
    x_all = f32(np.concatenate([np.asarray(inp["x_prompt"]).reshape(TP, D), np.asarray(inp["x_sample"]).reshape(TS, D)], axis=0))[x_base:x_base + x_rows]
    c_all = f32(np.concatenate([np.asarray(inp["c_prompt"]), np.asarray(inp["c_sample"])], axis=0))
    w_ada0 = f32(np.asarray(inp["w_ada"])[0])
    b_adaT = f32(np.asarray(inp["b_ada"])[0].reshape(48, 128).T)
    ngT = f32(np.asarray(inp["norm_gain"])[0].reshape(KC, 128).T)
    w_in0 = np.asarray(inp["w_in"])[0]
    qkg = f32(np.stack([np.asarray(inp["q_norm_gain"])[0], np.asarray(inp["k_norm_gain"])[0]], axis=1))
    consts = _consts()
    in_maps = []
    for c in (range(NCORES) if cores is None else cores):
        cols = np.concatenate([np.arange(j * 1024 + c * 128, j * 1024 + (c + 1) * 128) for j in range(8)])
        lbr = f32(np.asarray(inp["hgrn_lb_raw"])[:, c * 128:(c + 1) * 128].T)
        m = {"x_all": x_all, "c_all": c_all, "w_ada": w_ada0, "b_adaT": b_adaT, "ngT": ngT,
             "w_head": f32(w_in0[:, cols])[None], "qkg": qkg, "lbr": lbr[None],
             "ogain": f32(np.asarray(inp["hgrn_onorm_gain"])[0, c, :].reshape(1, 128, 1)),
             "cache_k": f32(np.asarray(inp["cache_sb_k"])[0, :, :, c, :])[None],
             "cache_v": f32(np.asarray(inp["cache_sb_v"])[0, :, :, c, :])[None],
             "s_in": f32(np.asarray(inp["state_hgrn"])[0, :, c])[None]}
        m.update(consts)
        in_maps.append(m)
    return in_maps


def make_out_maps(inp, a_all, b_all, cores=None):
    f32 = _f32
    xp = np.asarray(inp["x_prompt"]).reshape(TP, D)
    xs = np.asarray(inp["x_sample"]).reshape(TS, D)
    cp = np.asarray(inp["c_prompt"])
    cs = np.asarray(inp["c_sample"])
    w_ada0 = f32(np.asarray(inp["w_ada"])[0])
    b_ada0 = np.asarray(inp["b_ada"])[0]
    b_adaT = f32(b_ada0.reshape(48, 128).T)
    ngT = f32(np.asarray(inp["norm_gain"])[0].reshape(KC, 128).T)
    w_gate = f32(np.asarray(inp["w_in"])[0][:, 8192:])
    w_bsb = f32(np.asarray(inp["w_branch_sb"])[0])
    w_bhg = f32(np.asarray(inp["w_branch_hgrn"])[0])
    w_o = f32(np.asarray(inp["w_out"])[0])
    b_gate_bc = f32(np.broadcast_to(b_ada0[2 * D:][None, :], (128, D)))
    maps = []
    for c in (range(NCORES) if cores is None else cores):
        ptok = np.concatenate([own_chunk(c, s_) * CH + np.arange(CH) for s_ in range(4)])
        tok = np.concatenate([ptok, TP + np.arange(c * OWN_S, (c + 1) * OWN_S)])
        m = {"c_all": f32(np.concatenate([cp, cs[2 * c:2 * c + 2]], axis=0)), "w_ada": w_ada0, "b_adaT": b_adaT, "ngT": ngT,
             "identf": np.eye(128, dtype=np.float32),
             "x_own": f32(np.concatenate([xp[ptok], xs[c * OWN_S:(c + 1) * OWN_S]], axis=0)),
             "w_gate": w_gate, "w_bsb": w_bsb, "w_bhg": w_bhg, "w_o": w_o, "b_gate_bc": b_gate_bc,
             }
        if a_all is not None:
            m["ab_own"] = f32(np.stack([a_all[:, :, tok], b_all[:, :, tok]], axis=0))
        maps.append(m)
    return maps


def make_fused_maps(inp, cores=None, x_rows=TT, x_base=0):
    f32 = _f32
    base = make_out_maps(inp, None, None, cores=cores)
    x_all = f32(np.concatenate([np.asarray(inp["x_prompt"]).reshape(TP, D), np.asarray(inp["x_sample"]).reshape(TS, D)], axis=0))[x_base:x_base + x_rows]
    cp = np.asarray(inp["c_prompt"])
    cs = np.asarray(inp["c_sample"])
    w_in0 = np.asarray(inp["w_in"])[0]
    w_head = f32(np.stack([w_in0[:, np.concatenate([np.arange(j * 1024 + h * 128, j * 1024 + (h + 1) * 128) for j in range(8)])]
                           for h in range(8)], axis=0))
    qkg = f32(np.stack([np.asarray(inp["q_norm_gain"])[0], np.asarray(inp["k_norm_gain"])[0]], axis=1))
    lbr = f32(np.asarray(inp["hgrn_lb_raw"]).reshape(2, 8, 128).transpose(1, 2, 0))
    ogain = f32(np.asarray(inp["hgrn_onorm_gain"])[0].reshape(8, 128, 1))
    ck = np.asarray(inp["cache_sb_k"])[0]
    cv = np.asarray(inp["cache_sb_v"])[0]
    s_in = f32(np.asarray(inp["state_hgrn"])[0].transpose(1, 0, 2, 3))
    consts = _consts()
    maps = []
    for n, c in enumerate(range(NCORES) if cores is None else cores):
        m = dict(base[n])
        m.pop("ab_own", None)
        m["c_all"] = f32(np.concatenate([cp, cs, cs[2 * c:2 * c + 2]], axis=0))
        idx_s = np.stack([own_chunk(c, s_) * 128 + np.arange(128) for s_ in range(4)], axis=1).astype(np.int32)
        vis = np.full((128, 512), -30000.0, np.float32)
        for s_ in range(4):
            vis[:, s_ * 128:s_ * 128 + 4 * own_chunk(c, s_)] = 0.0
        m.update({"x_all": x_all, "w_head": w_head, "qkg": qkg, "lbr": lbr, "ogain": ogain,
                  "cache_k": f32(ck[2 * c:2 * c + 2].transpose(2, 0, 1, 3)), "cache_v": f32(cv[2 * c:2 * c + 2].transpose(2, 0, 1, 3)),
                  "s_in": s_in, "idx_s": idx_s, "vis": vis,
                  "s_in_own": f32(s_in[:, 2 * c:2 * c + 2])})
        m.update(consts)
        maps.append(m)
    return maps


def kernel(**inp):
    nc = build(mode="fused")
    res = run_bass_kernel_spmd(nc, make_fused_maps(inp), core_ids=list(range(NCORES)))
    r = res.results
    kall = r[0]["k_out"].reshape(8, TT, 128).transpose(1, 0, 2)
    vall = r[0]["v_out"].reshape(8, TT, 128).transpose(1, 0, 2)
    sall = r[0]["s_out"].reshape(8, 17, 128, 128).transpose(1, 0, 2, 3)
    y = [r[c]["y_own"] for c in range(NCORES)]
    y_prompt = np.zeros((TP, D), np.float32)
    for c in range(NCORES):
        for s_ in range(4):
            y_prompt[own_chunk(c, s_) * CH:(own_chunk(c, s_) + 1) * CH] = y[c][s_ * CH:(s_ + 1) * CH]
    y_prompt = y_prompt.reshape(1, TP, D)
    y_sample = np.concatenate([yy[OWN_P:] for yy in y], axis=0).reshape(NB, TSQ, D)
    new_k_prompt = kall[:TP].reshape(1, 1, TP, 8, 128)
    new_v_prompt = vall[:TP].reshape(1, 1, TP, 8, 128)
    new_k_sample = kall[TP:].reshape(1, NB, TSQ, 8, 128)
    new_v_sample = vall[TP:].reshape(1, NB, TSQ, 8, 128)
    new_s_prompt = sall[0].reshape(1, 1, 8, 128, 128)
    new_s_sample = sall[1:].reshape(1, NB, 8, 128, 128)
    c32 = lambda a: np.ascontiguousarray(a, dtype=np.float32)
    return (c32(y_prompt), c32(y_sample), c32(new_k_prompt), c32(new_v_prompt), c32(new_s_prompt),
            c32(new_k_sample), c32(new_v_sample), c32(new_s_sample))
```

```python
import numpy as np
from contextlib import ExitStack
import concourse.bass as bass
import concourse.mybir as mybir
from concourse.bass_utils import run_bass_kernel_spmd

F32 = mybir.dt.float32
BF16 = mybir.dt.bfloat16
AF = mybir.ActivationFunctionType
ALU = mybir.AluOpType

NCORES = 8
D = 2048
KC = D // 128
TP = 16384
NB = 16
TSQ = 64
TS = NB * TSQ
TT = TP + TS
CH = 512
NCH_P = TP // CH
NCH = TT // CH
PAST = 4096
EPS = 1e-6
SEM_CHUNK = 16000
NSLOT = 8
OWN_P = TP // NCORES
OWN_S = TS // NCORES
OWN = OWN_P + OWN_S


import os
_CONTIG = bool(os.environ.get("OWN_CONTIG"))


def own_chunk(c, s):
    if _CONTIG:
        return 4 * c + s
    return [c, 15 - c, 16 + c, 31 - c][s]


class Buf:
    __slots__ = ("name", "w", "r", "psum")

    def __init__(self, name, psum=False):
        self.name = name
        self.w = None
        self.r = {}
        self.psum = psum


class Q:
    def __init__(self, name):
        self.name = name
        self.ops = []
        self.n = 0
        self.waited = {}
        self.maxchunk = {}
        self.dma_k = 0
        self.slot_tot = [0] * NSLOT


class Prog:
    def __init__(self):
        self.q = {n: Q(n) for n in ("pe", "act", "dve", "pool", "sp")}
        self.keys = []
        self.keyset = set()

    def _key(self, key):
        if key not in self.keyset:
            self.keyset.add(key)
            self.keys.append(key)

    def _deps(self, q, reads, writes, extra):
        deps = {}

        def add(t):
            if t is None:
                return
            k, v = t
            if deps.get(k, 0) < v:
                deps[k] = v

        for b in reads:
            add(b.w)
            if b.psum:
                for rk, t in b.r.items():
                    if rk != q.name:
                        add(t)
        for b in writes:
            add(b.w)
            for t in b.r.values():
                add(t)
        for t in extra:
            add(t)
        waits = []
        for key, val in deps.items():
            if q.name == "pe" and key[0] == "pe":
                continue
            if isinstance(key[1], int) and key[0] in ("pe", "act", "dve", "pool", "sp"):
                mc = q.maxchunk.get(key[0], -1)
                if key[1] < mc:
                    continue
                if key[1] > mc:
                    q.maxchunk[key[0]] = key[1]
            if q.waited.get(key, 0) >= val:
                continue
            q.waited[key] = val
            waits.append((key, val))
        return waits

    def _mark(self, tok, reads, writes, rkey):
        for b in writes:
            b.w = tok
            b.r = {}
        for b in reads:
            b.r[rkey] = tok

    def op(self, qn, meth, kw, reads=(), writes=(), extra=()):
        fn = (meth, kw)
        q = self.q[qn]
        waits = self._deps(q, reads, writes, extra)
        idx = q.n
        q.n += 1
        key = (qn, idx // SEM_CHUNK)
        self._key(key)
        tok = (key, idx % SEM_CHUNK + 1)
        q.ops.append((waits, fn, (key, 1)))
        self._mark(tok, reads, writes, qn)
        return tok

    def dma(self, qn, kw, reads=(), writes=(), extra=(), meth="dma_start"):
        fn = (meth, kw)
        q = self.q[qn]
        slot = q.dma_k % NSLOT
        q.dma_k += 1
        key = (qn + "_dma", slot)
        self._key(key)
        prev = q.slot_tot[slot]
        ex = list(extra)
        if prev > 0:
            ex.append((key, prev))
        waits = self._deps(q, reads, writes, ex)
        q.slot_tot[slot] = prev + 16
        tok = (key, prev + 16)
        q.ops.append((waits, fn, (key, 16)))
        self._mark(tok, reads, writes, key)
        return tok

    def barrier(self):
        toks = []
        for q in self.q.values():
            for sl in range(NSLOT):
                if q.slot_tot[sl] > 0:
                    toks.append(((q.name + "_dma", sl), q.slot_tot[sl]))
            if q.n > 0:
                idx = q.n - 1
                toks.append(((q.name, idx // SEM_CHUNK), idx % SEM_CHUNK + 1))
        for q in self.q.values():
            waits = self._deps(q, (), (), toks)
            if waits:
                q.ops.append((waits, None, None))

    def finish(self):
        ex = []
        for q in self.q.values():
            for s in range(NSLOT):
                if q.slot_tot[s] > 0:
                    ex.append(((q.name + "_dma", s), q.slot_tot[s]))
            if q.n > 0 and q.name != "sp":
                idx = q.n - 1
                ex.append(((q.name, idx // SEM_CHUNK), idx % SEM_CHUNK + 1))
        q = self.q["sp"]
        waits = self._deps(q, (), (), ex)
        q.ops.append((waits, None, None))

    def emit(self, nc, es):
        sems = {}
        for key in self.keys:
            sems[key] = es.enter_context(nc.semaphore("s_%s_%s" % (key[0], key[1])))
        with nc.Block() as block:
            self._emit_block(block, sems)

    def _emit_block(self, block, sems):

        def run(q):
            def body(e):
                for waits, fn, inc in q.ops:
                    for key, val in waits:
                        e.wait_ge(sems[key], val)
                    if fn is None:
                        continue
                    ins = getattr(e, fn[0])(**fn[1])
                    ins.then_inc(sems[inc[0]], inc[1])
            return body

        block.tensor(run(self.q["pe"]))
        block.scalar(run(self.q["act"]))
        block.vector(run(self.q["dve"]))
        block.gpsimd(run(self.q["pool"]))
        block.sync(run(self.q["sp"]))


def build(mode="mix", chunks=None, x_rows=TT, stop=None, x_base=0, flags=("att", "hg"), debug=False):
    chunks = list(range(NCH)) if chunks is None else chunks
    if mode == "out":
        chunks = []
    do_mix = mode in ("mix", "fused")
    do_out = mode in ("out", "fused")
    NR = {"mix": 17, "out": 3, "fused": 19}[mode]
    RS0 = NR - 2
    if do_mix:
        debug = True
    nc = bass.Bass("TRN2", target_bir_lowering=False)
    es = ExitStack()
    P = Prog()

    def dram_in(name, shape, dt=F32):
        return nc.dram_tensor(name, list(shape), dt, kind="ExternalInput").ap()

    def dram_out(name, shape, dt=F32):
        return nc.dram_tensor(name, list(shape), dt, kind="ExternalOutput").ap()

    def sb(name, shape, dt):
        return es.enter_context(nc.sbuf_tensor(name, list(shape), dt))

    c_all = dram_in("c_all", [NR, D])
    w_ada = dram_in("w_ada", [D, 3 * D])
    b_adaT = dram_in("b_adaT", [128, 48])
    ngT = dram_in("ngT", [128, KC])
    identf_d = dram_in("identf", [128, 128])
    if do_mix:
        NH = 8 if mode == "fused" else 1
        HD = {"h": 0}
        x_all = dram_in("x_all", [x_rows, D])
        w_head = dram_in("w_head", [NH, D, 1024])
        qkg = dram_in("qkg", [128, 2])
        k_out = dram_out("k_out", [NH, TT, 128])
        v_out = dram_out("v_out", [NH, TT, 128])
        s_out = dram_out("s_out", [NH, 17, 128, 128])
        trin_d = dram_in("trin", [128, 128])
        tric_d = dram_in("tric", [128, 128])
        maskp_d = dram_in("maskp", [128, 4, 512])
        masks_d = dram_in("masks", [128, 2, 64])
        hmask_d = dram_in("hmask", [128, 128])
        resetm_d = dram_in("resetm", [128, 512])
        lbr_d = dram_in("lbr", [NH, 128, 2])
        ogain_d = dram_in("ogain", [NH, 128, 1])
        pmask_d = dram_in("pmask", [128, 2])
        if mode == "fused":
            cache_k = dram_in("cache_k", [NH, 2, PAST, 128])
            cache_v = dram_in("cache_v", [NH, 2, PAST, 128])
            vis_d = dram_in("vis", [128, 512])
            a_own_scr = nc.dram_tensor("a_own_scr", [8, 17, 128, 128], BF16)
            b_own_scr = nc.dram_tensor("b_own_scr", [8, 17, 128, 128], BF16)
            hT_scr = nc.dram_tensor("hT_scr", [NCH, 128, KC * CH], BF16)
            S_scr = nc.dram_tensor("S_scr", [33 * 128, 128], F32)
            s_in_own = dram_in("s_in_own", [NH, 2, 128, 128])
            idx_s_d = nc.dram_tensor("idx_s", [128, 4], mybir.dt.int32, kind="ExternalInput").ap()
        else:
            cache_k = dram_in("cache_k", [NH, NB, PAST, 128])
            cache_v = dram_in("cache_v", [NH, NB, PAST, 128])
        s_in = dram_in("s_in", [NH, NB, 128, 128])
        if mode == "mix":
            dbg_a = dram_out("dbg_a", [128, TT])
            dbg_b = dram_out("dbg_b", [128, TT])
    if do_out:
        x_own = dram_in("x_own", [OWN, D])
        w_gate = dram_in("w_gate", [D, 2 * D])
        w_bsb = dram_in("w_bsb", [1024, D])
        w_bhg = dram_in("w_bhg", [1024, D])
        w_o = dram_in("w_o", [D, D])
        b_gate_bc = dram_in("b_gate_bc", [128, D])
        y_own = dram_out("y_own", [OWN, D])
        if mode == "out":
            ab_own = dram_in("ab_own", [2, 8, 128, OWN])

    es1 = ExitStack()

    def sb1(name, shape, dt):
        return es1.enter_context(nc.sbuf_tensor(name, list(shape), dt))

    ident_f = sb("ident_f", [128, 128], F32)
    ident_b = sb("ident_b", [128, 128], BF16)
    ones_b = sb("ones_b", [128, 128], BF16)
    csT = sb("csT", [128, KC, NR], BF16)
    gm = sb("gm", [128, KC, NR], F32)
    sh = sb("sh", [128, KC, NR], F32)
    badaT = sb("badaT", [128, 48], F32)
    ngTs = sb("ngTs", [128, KC], F32)
    epsb = sb("epsb", [128, 1], F32)
    xbuf = [sb("xbuf%d" % i, [128, D], F32) for i in range(2)]
    xn = sb("xn", [128, D], BF16)
    junk = xn
    stat = [sb("stat%d" % i, [128, 4], F32) for i in range(2)]
    hT = sb("hT", [128, KC, CH], BF16)
    if do_mix:
        wh = sb1("wh", [128, KC, 1024], BF16)
        qkgs = sb1("qkgs", [128, 2], F32)
        qkgs2 = sb1("qkgs2", [128, 2], F32)
        sq_b = sb1("sq_b", [128, CH], BF16)
        rstd_f = sb1("rstd_f", [128, CH], F32)
        kn_f = sb1("kn_f", [128, CH], F32)
        vT_f = sb1("vT_f", [128, CH], F32)
        KT = sb1("KT", [128, TP], BF16)
        Vr = sb1("Vr", [128, TP // 128, 128], BF16)
        QT = sb1("QT", [128, CH], BF16)
    es0 = ExitStack()
    c_sb = es0.enter_context(nc.sbuf_tensor("c_sb", [NR, D], F32))
    cs_sb = es0.enter_context(nc.sbuf_tensor("cs_sb", [NR, D], F32))
    wada = [es0.enter_context(nc.sbuf_tensor("wada%d" % i, [128, KC, 512], BF16)) for i in range(2)]
    modsb = es0.enter_context(nc.sbuf_tensor("modsb", [128, 48, NR], F32))

    ps = [es.enter_context(nc.psum_tensor("ps%d" % i, [128, 512], F32)) for i in range(8)]
    psb = [Buf("ps%d" % i, psum=True) for i in range(8)]

    B = {}

    def bf(name):
        if name not in B:
            B[name] = Buf(name)
        return B[name]

    def done():
        P.finish()
        P.emit(nc, es)
        es.close()
        return nc
    P.dma("sp", dict(out=ident_f[:], in_=identf_d[:, :]), writes=[bf("ident_f")])
    P.dma("sp", dict(out=badaT[:], in_=b_adaT[:, :]), writes=[bf("badaT")])
    P.dma("sp", dict(out=ngTs[:], in_=ngT[:, :]), writes=[bf("ngTs")])
    P.dma("sp", dict(out=c_sb[:], in_=c_all[:, :]), writes=[bf("c_sb")])
    P.op("dve", "tensor_copy", dict(out=ident_b[:], in_=ident_f[:]), reads=[bf("ident_f")], writes=[bf("ident_b")])
    P.op("dve", "memset", dict(ap=ones_b[:], constant=1.0), writes=[bf("ones_b")])
    P.op("dve", "memset", dict(ap=epsb[:], constant=EPS), writes=[bf("epsb")])
    if do_mix:
        P.dma("sp", dict(out=qkgs[:], in_=qkg[:, :]), writes=[bf("qkgs")])
        P.op("dve", "tensor_scalar", dict(out=qkgs2[:, 0:1], in0=qkgs[:, 0:1], scalar1=float(128 ** -0.5),
                                          scalar2=None, op0=ALU.mult), reads=[bf("qkgs")], writes=[bf("qkgs2a")])
        P.op("dve", "tensor_copy", dict(out=qkgs2[:, 1:2], in_=qkgs[:, 1:2]), reads=[bf("qkgs")], writes=[bf("qkgs2b")])

    P.op("act", "activation", dict(out=cs_sb[:], in_=c_sb[:], func=AF.Silu), reads=[bf("c_sb")], writes=[bf("cs_sb")])
    pT = ps[0]
    for k in range(KC):
        P.op("pe", "transpose", dict(out=pT[0:128, k * NR:(k + 1) * NR], in_=cs_sb[:, k * 128:(k + 1) * 128],
                                     identity=ident_f[0:NR, 0:NR]),
             reads=[bf("cs_sb"), bf("ident_f")], writes=[psb[0]])
    P.op("dve", "tensor_copy", dict(out=csT[:].rearrange("p k r -> p (k r)"), in_=pT[:, 0:KC * NR]),
         reads=[psb[0]], writes=[bf("csT")])
    modps = [ps[1], ps[2]]
    for g in range(12):
        wb = wada[g % 2]
        wbuf = bf("wada%d" % (g % 2))
        P.dma("pool", dict(out=wb[:], in_=w_ada[:, g * 512:(g + 1) * 512].rearrange("(k p) c -> p k c", p=128)),
              writes=[wbuf])
        for cc in range(4):
            j = g * 4 + cc
            dst = modps[j // 24]
            dbuf = psb[1 + j // 24]
            jj = j % 24
            for k in range(KC):
                P.op("pe", "matmul", dict(out=dst[:, jj * NR:(jj + 1) * NR], lhsT=wb[:, k, cc * 128:(cc + 1) * 128],
                                          rhs=csT[:, k, :], start=(k == 0), stop=(k == KC - 1)),
                     reads=[wbuf, bf("csT")], writes=[dbuf])
    for half in range(2):
        P.op("dve", "tensor_tensor", dict(
            out=modsb[:, half * 24:(half + 1) * 24, :],
            in0=modps[half][:, 0:24 * NR].rearrange("p (j r) -> p j r", r=NR),
            in1=badaT[:, half * 24:(half + 1) * 24].unsqueeze(2).to_broadcast([128, 24, NR]), op=ALU.add),
            reads=[psb[1 + half], bf("badaT")], writes=[bf("modsb%d" % half)])
    P.op("dve", "tensor_copy", dict(out=sh[:], in_=modsb[:, 0:16, :]), reads=[bf("modsb0")], writes=[bf("sh")])
    P.op("dve", "tensor_scalar", dict(out=gm[:], in0=modsb[:, 16:32, :], scalar1=1.0, scalar2=None, op0=ALU.add),
         reads=[bf("modsb0"), bf("modsb1")], writes=[bf("gm")])
    P.op("dve", "tensor_tensor", dict(out=gm[:], in0=gm[:], in1=ngTs[:].unsqueeze(2).to_broadcast([128, KC, NR]),
                                      op=ALU.mult), reads=[bf("gm"), bf("ngTs")], writes=[bf("gm")])

    if stop == "p0":
        return done()
    P.barrier()
    es0.close()
    if do_mix:
        trin = sb1("trin_s", [128, 128], BF16)
        tric = sb1("tric_s", [128, 128], BF16)
        maskp = sb1("maskp_s", [128, 4, 512], BF16)
        masks = sb1("masks_s", [128, 2, 64], BF16)
        hmask = sb1("hmask_s", [128, 128], F32)
        resetm = sb1("resetm_s", [128, 512], F32)
        lbr = sb1("lbr_s", [128, 4], F32)
        ogain = sb1("ogain_s", [128, 1], F32)
        oneb = sb1("oneb", [128, 1], F32)
        zs_sb = sb1("zs_sb", [128, CH], F32)
        e_sb = [sq_b, sb1("e_sb1", [128, CH], BF16), sb1("e_sb2", [128, CH], BF16)]
        sp_sb = [sb1("sp_sb%d" % i, [128, CH], BF16) for i in range(3)]
        x_sb = [sb1("x_sb%d" % i, [128, CH], BF16) for i in range(2)]
        w_sb = [sb1("w_sb%d" % i, [128, CH], BF16) for i in range(2)]
        bT = sb1("bT", [128, CH], BF16)
        aT = bT
        ksT = sb1("ksT", [128, CH], BF16)
        vs_bf = sb1("vs_bf", [128, 4, 128], BF16)
        t_f = sb1("t_f", [128, CH], F32)
        t_k = sb1("t_k", [128, CH], F32)
        t_lf = sb1("t_lf", [128, CH], F32)
        t_b = sb1("t_b", [128, CH], F32)
        t_eb = sb1("t_eb", [128, CH], F32)
        t_enb = sb1("t_enb", [128, CH], F32)
        t_qs = sb1("t_qs", [128, CH], F32)
        t_ke = sb1("t_ke", [128, CH], F32)
        kv_o = [t_ke[:].rearrange("p (t d) -> p t d", d=128)]
        t_iT = sb1("t_iT", [128, CH], F32)
        t_zh = sb1("t_zh", [128, CH], F32)
        dec = sb1("dec", [128, 8], F32)
        iv = sb1("iv", [128, 4, 128], F32)
        ket = sb1("ket", [128, 2, 4, 128], F32)
        pmask = sb1("pmask_s", [128, 2], F32)
        if mode == "fused":
            vis = sb1("vis_s", [128, 512], F32)
            idx_s = sb1("idx_s_s", [128, 4], mybir.dt.int32)
        attm = sb1("attm", [128, 4, 128], F32)
        Sset = [sb1("Sset%d" % i, [128, 8, 128], F32) for i in range(2)]
        sin = sb1("sin", [128, 8, 128], F32)
        dbg_t = t_lf
        P.dma("pool", dict(out=trin[:], in_=trin_d[:, :]), writes=[bf("trin")])
        P.dma("pool", dict(out=tric[:], in_=tric_d[:, :]), writes=[bf("tric")])
        P.dma("pool", dict(out=maskp[:], in_=maskp_d[:, :, :]), writes=[bf("maskp")])
        P.dma("pool", dict(out=masks[:], in_=masks_d[:, :, :]), writes=[bf("masks")])
        P.dma("sp", dict(out=hmask[:], in_=hmask_d[:, :]), writes=[bf("hmask")])
        P.dma("sp", dict(out=resetm[:], in_=resetm_d[:, :]), writes=[bf("resetm")])
        P.dma("sp", dict(out=pmask[:], in_=pmask_d[:, :]), writes=[bf("pmask")])
        if mode == "fused":
            P.dma("sp", dict(out=vis[:], in_=vis_d[:, :]), writes=[bf("vis")])
            P.dma("sp", dict(out=idx_s[:], in_=idx_s_d[:, :]), writes=[bf("idx_s")])
        P.op("dve", "memset", dict(ap=oneb[:], constant=1.0), writes=[bf("oneb")])

        def head_setup():
            P.barrier()
            P.dma("pool", dict(out=wh[:], in_=w_head[HD["h"]].rearrange("(k p) c -> p k c", p=128)), writes=[bf("wh")])
            P.dma("sp", dict(out=lbr[:, 0:2], in_=lbr_d[HD["h"]]), writes=[bf("lbr")])
            P.dma("sp", dict(out=ogain[:], in_=ogain_d[HD["h"]]), writes=[bf("ogain")])
            P.op("dve", "memset", dict(ap=Sset[(mixers.n + 1) % 2][:, 7, :], constant=0.0), writes=[bf("S_%d_7" % ((mixers.n + 1) % 2))])
            P.op("dve", "tensor_tensor", dict(out=lbr[:, 2:3], in0=lbr[:, 0:1], in1=lbr[:, 1:2], op=ALU.subtract),
                 reads=[bf("lbr")], writes=[bf("lbr")])
            P.op("act", "activation", dict(out=lbr[:, 2:3], in_=lbr[:, 2:3], func=AF.Sigmoid), reads=[bf("lbr")], writes=[bf("lbr")])
            P.op("dve", "tensor_scalar", dict(out=lbr[:, 3:4], in0=lbr[:, 2:3], scalar1=-1.0, scalar2=1.0, op0=ALU.mult, op1=ALU.add),
                 reads=[bf("lbr")], writes=[bf("lbr")])
            sample_ready["done"] = False
            if mode == "fused":
                P.dma("sp", dict(out=S_scr.ap()[0:128, :], in_=Sset[(mixers.n + 1) % 2][:, 7, :]),
                      reads=[bf("S_%d_7" % ((mixers.n + 1) % 2))])
    state = {"tile_i": 0, "tro": 0}

    def proj(j, dst, dbuf, hTb):
        for k in range(KC):
            P.op("pe", "matmul", dict(out=dst[:, :], lhsT=wh[:, k, j * 128:(j + 1) * 128], rhs=hT[:, k, :],
                                      start=(k == 0), stop=(k == KC - 1)),
                 reads=[bf("wh")] + hTb[k], writes=[dbuf])

    def headnorm(src, sbuf_, gain_ap, gbufs, out_ap, outbuf):
        P.op("act", "activation", dict(out=sq_b[:], in_=src[:, :], func=AF.Square), reads=[sbuf_], writes=[bf("e_sb0")])
        P.op("pe", "matmul", dict(out=ps[4][:, :], lhsT=ones_b[:], rhs=sq_b[:], start=True, stop=True),
             reads=[bf("ones_b"), bf("e_sb0")], writes=[psb[4]])
        P.op("act", "activation", dict(out=rstd_f[:], in_=ps[4][:, :], func=AF.Ln, scale=1.0 / 128, bias=epsb[:, 0:1]),
             reads=[psb[4], bf("epsb")], writes=[bf("rstd_f")])
        P.op("act", "activation", dict(out=rstd_f[:], in_=rstd_f[:], func=AF.Exp, scale=-0.5),
             reads=[bf("rstd_f")], writes=[bf("rstd_f")])
        P.op("dve", "scalar_tensor_tensor", dict(out=out_ap, in0=src[:, :], scalar=gain_ap, in1=rstd_f[:],
                                                 op0=ALU.mult, op1=ALU.mult),
             reads=[sbuf_, bf("rstd_f")] + gbufs, writes=[outbuf])

    def attention(nq, q_ap, qbufs, blocks, o_ap, c0):
        n = len(blocks)
        zb = [ps[0], ps[1]]
        A = ps[6]

        def stage1(j):
            blk = blocks[j]
            z = zb[j % 2]
            zbuf = psb[j % 2]
            eb_, ebuf = e_sb[j % 3], bf("e_sb%d" % (j % 3))
            sb_, sbuf_ = sp_sb[j % 3], bf("sp_sb%d" % (j % 3))
            has_mask = blk.get("mask") is not None
            P.op("pe", "matmul", dict(out=z[:, 0:nq], lhsT=blk["kT"], rhs=q_ap, start=True, stop=not has_mask),
                 reads=blk["kbufs"] + qbufs, writes=[zbuf])
            if has_mask:
                P.op("pe", "matmul", dict(out=z[:, 0:nq], lhsT=ident_b[:], rhs=blk["mask"], start=False, stop=True),
                     reads=[bf("ident_b")] + blk["mbufs"], writes=[zbuf])
            if blk.get("bias") is not None:
                P.op("act", "activation", dict(out=eb_[:, 0:nq], in_=z[:, 0:nq], func=AF.Exp, bias=blk["bias"]),
                     reads=[zbuf, bf("vis")], writes=[ebuf])
            else:
                P.op("act", "activation", dict(out=eb_[:, 0:nq], in_=z[:, 0:nq], func=AF.Exp), reads=[zbuf], writes=[ebuf])
            P.op("act", "activation", dict(out=sb_[:, 0:nq], in_=eb_[:, 0:nq], func=AF.Ln, bias=oneb[:, 0:1]),
                 reads=[ebuf, bf("oneb")], writes=[sbuf_])

        def o_mm(j):
            blk = blocks[j]
            P.op("pe", "matmul", dict(out=o_ap, lhsT=blk["v"], rhs=w_sb[j % 2][:, 0:nq], start=(j == 0), stop=(j == n - 1)),
                 reads=blk["vbufs"] + [bf("w_sb%d" % (j % 2))], writes=[psb[7]])

        stage1(0)
        if n > 1:
            stage1(1)
        for j in range(n):
            spj, spbuf = sp_sb[j % 3], bf("sp_sb%d" % (j % 3))
            P.op("pe", "matmul", dict(out=A[:, 0:nq], lhsT=trin[:], rhs=spj[:, 0:nq], start=(j == 0), stop=True, skip_group_check=(j > 0)),
                 reads=[bf("trin"), spbuf], writes=[psb[6]])
            P.op("act", "activation", dict(out=x_sb[j % 2][:, 0:nq], in_=A[:, 0:nq], func=AF.Exp),
                 reads=[psb[6]], writes=[bf("x_sb%d" % (j % 2))])
            if j + 2 < n:
                stage1(j + 2)
            if j >= 1:
                o_mm(j - 1)
            if j < n - 1:
                P.op("pe", "matmul", dict(out=A[:, 0:nq], lhsT=tric[:], rhs=spj[:, 0:nq], start=False, stop=True, skip_group_check=True),
                     reads=[bf("tric"), spbuf], writes=[psb[6]])
            P.op("dve", "tensor_tensor", dict(out=w_sb[j % 2][:, 0:nq], in0=e_sb[j % 3][:, 0:nq], in1=x_sb[j % 2][:, 0:nq], op=ALU.mult),
                 reads=[bf("e_sb%d" % (j % 3)), bf("x_sb%d" % (j % 2))], writes=[bf("w_sb%d" % (j % 2))])
        o_mm(n - 1)

    def tr_out(ch, srcT, srcbuf, dram, extra_copy=None):
        pt = ps[5]
        for tt in range(4):
            P.op("pe", "transpose", dict(out=pt[:, tt * 128:(tt + 1) * 128], in_=srcT[:, tt * 128:(tt + 1) * 128],
                                         identity=ident_f[:]),
                 reads=[srcbuf, bf("ident_f")], writes=[psb[5]])
        i = 0
        ko = kv_o[i]
        kob = bf("t_ke")
        P.op("act", "activation", dict(out=ko[:].rearrange("p t d -> p (t d)"), in_=pt[:, :], func=AF.Copy),
             reads=[psb[5]], writes=[kob])
        if extra_copy is not None:
            extra_copy(pt, psb[5])
        P.dma("sp", dict(out=dram[ch * CH:(ch + 1) * CH, :].rearrange("(t p) d -> p t d", p=128), in_=ko[:]),
              reads=[kob])

    sample_ready = {"done": False}

    def emit_ab(ch, which, src_bf, srcbuf):
        if True:
            return
        r0 = ((which * 8 + HD["h"]) * 136 + ch * 4) * 128
        P.dma("sp", dict(out=ab_scr.ap()[r0:r0 + 512, :].rearrange("(t p) c -> p t c", p=128),
                         in_=src_bf[:].rearrange("p (t c) -> p t c", c=128)), reads=[srcbuf])

    def load_cache(b, src_k=None, src_v=None):
        i = b % 2
        kc_raw = KT[:, 8192 + i * 4096: 8192 + (i + 1) * 4096].rearrange("p (j d) -> p j d", d=128)
        vc = Vr[:, i * 32:(i + 1) * 32, :]
        kcT = KT[:, i * 4096:(i + 1) * 4096]
        P.dma("pool", dict(out=kc_raw, in_=(cache_k[HD["h"], b] if src_k is None else src_k).rearrange("(j p) d -> p j d", p=128)), writes=[bf("kc_raw%d" % i)])
        P.dma("pool", dict(out=vc, in_=(cache_v[HD["h"], b] if src_v is None else src_v).rearrange("(j p) d -> p j d", p=128)), writes=[bf("vc%d" % i)])
        for g in range(4):
            pb = 2 + (g % 2)
            tp = ps[pb][:, :].bitcast(BF16)
            for jj in range(8):
                j = g * 8 + jj
                P.op("pe", "transpose", dict(out=tp[:, jj * 128:(jj + 1) * 128], in_=kc_raw[:, j, :], identity=ident_b[:]),
                     reads=[bf("kc_raw%d" % i), bf("ident_b")], writes=[psb[pb]])
            if g % 2 == 0:
                P.op("act", "activation", dict(out=kcT[:, g * 1024:(g + 1) * 1024], in_=tp[:, :], func=AF.Copy),
                     reads=[psb[pb]], writes=[bf("kcT%d_%d" % (i, g))])
            else:
                P.op("dve", "tensor_copy", dict(out=kcT[:, g * 1024:(g + 1) * 1024], in_=tp[:, :]),
                     reads=[psb[pb]], writes=[bf("kcT%d_%d" % (i, g))])

    def mixers(ch, hTb, own=None):
        n_local = mixers.n
        if own is None:
            mixers.n += 1
        is_p = ch < NCH_P
        lite = (mode == "fused" and own is None)
        if mode != "fused":
            proj(3, ps[3], psb[3], hTb)
            P.op("act", "activation", dict(out=zs_sb[:], in_=ps[3][:, :], func=AF.Silu), reads=[psb[3]], writes=[bf("zs_sb")])
        if "att" in flags and mode != "fused":
            if is_p:
                blocks = []
                for j in range(4 * ch + 3, -1, -1):
                    blk = dict(kT=KT[:, j * 128:(j + 1) * 128], kbufs=[bf("KT%d" % (j // 4))],
                               v=Vr[:, j, :], vbufs=[bf("Vr%d" % (j // 4))])
                    if j >= 4 * ch:
                        blk["mask"] = maskp[:, j - 4 * ch, :]
                        blk["mbufs"] = [bf("maskp")]
                    blocks.append(blk)
                attention(CH, QT[:], [bf("QT")], blocks, ps[7][:, :], 0)
            else:
                if not sample_ready["done"]:
                    sample_ready["done"] = True
                    P.barrier()
                    load_cache((ch - NCH_P) * 8)
                for c in range(8):
                    b = (ch - NCH_P) * 8 + c
                    i = b % 2
                    if b + 1 < NB:
                        load_cache(b + 1)
                    tt, par = c // 2, c % 2
                    kcT = KT[:, i * 4096:(i + 1) * 4096]
                    vc = Vr[:, i * 32:(i + 1) * 32, :]
                    blocks = [dict(kT=ksT[:, tt * 128:(tt + 1) * 128], kbufs=[bf("ksT")], v=vs_bf[:, tt, :], vbufs=[bf("vs_bf")],
                                   mask=masks[:, par, :], mbufs=[bf("masks")])]
                    for j in range(31, -1, -1):
                        blocks.append(dict(kT=kcT[:, j * 128:(j + 1) * 128], kbufs=[bf("kcT%d_%d" % (i, j // 8))],
                                           v=vc[:, j, :], vbufs=[bf("vc%d" % i)]))
                    attention(TSQ, QT[:, c * TSQ:(c + 1) * TSQ], [bf("QT")], blocks, ps[7][:, c * TSQ:(c + 1) * TSQ], c * TSQ)
            P.op("dve", "tensor_tensor", dict(out=aT[:], in0=ps[7][:, :], in1=zs_sb[:], op=ALU.mult),
                 reads=[psb[7], bf("zs_sb")], writes=[bf("bT")])
            emit_ab(ch, 0, aT, bf("bT"))
            if mode == "mix":
                P.op("dve", "tensor_copy", dict(out=dbg_t[:], in_=aT[:]), reads=[bf("bT")], writes=[bf("t_lf")])
                P.dma("sp", dict(out=dbg_a[:, ch * CH:(ch + 1) * CH], in_=dbg_t[:]), reads=[bf("t_lf")])
        if "hg" not in flags:
            return
        if not is_p:
            if own is None:
                b0_ = (ch - NCH_P) * 8
                P.dma("sp", dict(out=sin[:], in_=s_in[HD["h"], b0_:b0_ + 8].rearrange("b k v -> k b v")), writes=[bf("sin")])
            else:
                P.dma("sp", dict(out=sin[:, 0:2, :], in_=s_in_own[HD["h"]].rearrange("b k v -> k b v")), writes=[bf("sin")])
        proj(4, ps[2], psb[2], hTb)
        P.op("act", "activation", dict(out=t_f[:], in_=ps[2][:, :], func=AF.Sigmoid), reads=[psb[2]], writes=[bf("t_f")])
        if not lite:
            proj(6, ps[3], psb[3], hTb)
            P.op("act", "activation", dict(out=t_qs[:], in_=ps[3][:, :], func=AF.Silu), reads=[psb[3]], writes=[bf("t_qs")])
            proj(7, ps[2], psb[2], hTb)
            P.op("act", "activation", dict(out=t_zh[:], in_=ps[2][:, :], func=AF.Silu), reads=[psb[2]], writes=[bf("t_zh")])
        proj(5, ps[3], psb[3], hTb)
        P.op("act", "activation", dict(out=t_iT[:], in_=ps[3][:, :], func=AF.Copy), reads=[psb[3]], writes=[bf("t_iT")])
        P.op("dve", "tensor_scalar", dict(out=t_f[:], in0=t_f[:], scalar1=lbr[:, 3:4], scalar2=lbr[:, 2:3], op0=ALU.mult, op1=ALU.add),
             reads=[bf("t_f"), bf("lbr")], writes=[bf("t_f")])
        P.op("act", "activation", dict(out=t_lf[:], in_=t_f[:], func=AF.Ln), reads=[bf("t_f")], writes=[bf("t_lf")])
        P.op("dve", "tensor_scalar", dict(out=t_k[:], in0=t_f[:], scalar1=-1.0, scalar2=1.0, op0=ALU.mult, op1=ALU.add),
             reads=[bf("t_f")], writes=[bf("t_k")])
        P.op("dve", "tensor_tensor_scan", dict(out=t_b[:], data0=resetm[:], data1=t_lf[:], initial=0.0, op0=ALU.mult, op1=ALU.add),
             reads=[bf("resetm"), bf("t_lf")], writes=[bf("t_b")])
        if not lite:
            P.op("act", "activation", dict(out=t_eb[:], in_=t_b[:], func=AF.Exp), reads=[bf("t_b")], writes=[bf("t_eb")])
        P.op("act", "activation", dict(out=t_enb[:], in_=t_b[:], func=AF.Exp, scale=-1.0), reads=[bf("t_b")], writes=[bf("t_enb")])
        P.op("act", "activation", dict(out=dec[:].unsqueeze(2), in_=t_b[:].rearrange("p (c t) -> p c t", t=64)[:, :, 63:64], func=AF.Exp),
             reads=[bf("t_b")], writes=[bf("dec")])
        if not lite:
            P.op("dve", "tensor_tensor", dict(out=t_eb[:], in0=t_qs[:], in1=t_eb[:], op=ALU.mult),
                 reads=[bf("t_qs"), bf("t_eb")], writes=[bf("t_eb")])
        P.op("dve", "tensor_tensor", dict(out=t_enb[:], in0=t_k[:], in1=t_enb[:], op=ALU.mult),
             reads=[bf("t_k"), bf("t_enb")], writes=[bf("t_enb")])
        P.op("dve", "tensor_tensor", dict(out=t_ke[:].rearrange("p (c t) -> p c t", t=64),
                                          in0=t_enb[:].rearrange("p (c t) -> p c t", t=64),
                                          in1=dec[:].unsqueeze(2).to_broadcast([128, 8, 64]), op=ALU.mult),
             reads=[bf("t_enb"), bf("dec")], writes=[bf("t_ke")])
        hgs = [int(f[3:]) for f in flags if f.startswith("hgs")]
        hgs = hgs[0] if hgs else 99
        if hgs <= 1:
            return
        for which in range(2):
            src, sbuf_ = (t_iT, bf("t_iT")) if which == 0 else (t_ke, bf("t_ke"))
            for tt in range(4):
                P.op("pe", "transpose", dict(out=ps[5][:, tt * 128:(tt + 1) * 128], in_=src[:, tt * 128:(tt + 1) * 128], identity=ident_f[:]),
                     reads=[sbuf_, bf("ident_f")], writes=[psb[5]])
            if which == 0:
                P.op("act", "activation", dict(out=iv[:].rearrange("p t d -> p (t d)"), in_=ps[5][:, :], func=AF.Copy),
                     reads=[psb[5]], writes=[bf("iv")])
            else:
                for ab in range(2):
                    P.op("act", "activation", dict(out=ket[:, ab, :, :].rearrange("p t d -> p (t d)"), in_=ps[5][:, :], func=AF.Copy,
                                                   scale=pmask[:, ab:ab + 1]),
                         reads=[psb[5], bf("pmask")], writes=[bf("ket%d" % ab)])
        if hgs <= 2:
            return
        for c in range(8):
            pb = c // 4
            r0 = (c % 2) * 64
            P.op("pe", "matmul", dict(out=ps[pb][:, (c % 4) * 128:(c % 4 + 1) * 128], lhsT=ket[:, c % 2, c // 2, :],
                                      rhs=iv[:, c // 2, :], start=True, stop=True),
                 reads=[bf("ket%d" % (c % 2)), bf("iv")], writes=[psb[pb]])
        if hgs <= 3:
            return
        if not lite:
            for p_ in range(4):
                P.op("pe", "matmul", dict(out=ps[4][:, p_ * 128:(p_ + 1) * 128], lhsT=t_enb[:, p_ * 128:(p_ + 1) * 128],
                                          rhs=t_eb[:, p_ * 128:(p_ + 1) * 128], start=True, stop=True),
                     reads=[bf("t_enb"), bf("t_eb")], writes=[psb[4]])
            P.op("dve", "tensor_tensor", dict(out=attm[:], in0=ps[4][:, :].rearrange("p (a t) -> p a t", t=128),
                                              in1=hmask[:].unsqueeze(1).to_broadcast([128, 4, 128]), op=ALU.mult),
                 reads=[psb[4], bf("hmask")], writes=[bf("attm")])
        if hgs <= 4:
            return
        cur = Sset[n_local % 2]
        prev_set = Sset[(n_local + 1) % 2]
        Sprev = []
        for c in range(8):
            if is_p:
                if c == 0:
                    sp_ap, sp_buf = prev_set[:, 7, :], bf("S_%d_7" % ((n_local + 1) % 2))
                else:
                    sp_ap, sp_buf = cur[:, c - 1, :], bf("S_%d_%d" % (n_local % 2, c - 1))
            elif own is None:
                sp_ap, sp_buf = sin[:, c, :], bf("sin")
            else:
                sp_ap, sp_buf = sin[:, min(c, 1), :], bf("sin")
            Sprev.append((sp_ap, sp_buf))
            pb = c // 4
            P.op("dve", "scalar_tensor_tensor", dict(out=cur[:, c, :], in0=sp_ap, scalar=dec[:, c:c + 1],
                                                     in1=ps[pb][:, (c % 4) * 128:(c % 4 + 1) * 128], op0=ALU.mult, op1=ALU.add),
                 reads=[sp_buf, bf("dec"), psb[pb]], writes=[bf("S_%d_%d" % (n_local % 2, c))])
        mixers.first = False
        if lite:
            if is_p:
                P.dma("sp", dict(out=S_scr.ap()[(ch + 1) * 128:(ch + 2) * 128, :], in_=cur[:, 7, :]),
                      reads=[bf("S_%d_7" % (n_local % 2))])
                if ch == NCH_P - 1:
                    P.dma("sp", dict(out=s_out[HD["h"], 0], in_=cur[:, 7, :]), reads=[bf("S_%d_7" % (n_local % 2))])
            else:
                b0 = (ch - NCH_P) * 8
                P.dma("sp", dict(out=s_out[HD["h"], 1 + b0:1 + b0 + 8].rearrange("b k v -> k b v"), in_=cur[:]),
                      reads=[bf("S_%d_%d" % (n_local % 2, c)) for c in range(8)])
            return
        if hgs <= 5:
            return
        for p_ in range(4):
            P.op("pe", "matmul", dict(out=ps[6][:, p_ * 128:(p_ + 1) * 128], lhsT=iv[:, p_, :], rhs=attm[:, p_, :], start=True, stop=False),
                 reads=[bf("iv"), bf("attm")], writes=[psb[6]])
            for h2 in range(2):
                c = 2 * p_ + h2
                sp_ap, sp_buf = Sprev[c]
                P.op("pe", "matmul", dict(out=ps[6][:, c * 64:(c + 1) * 64], lhsT=sp_ap, rhs=t_eb[:, c * 64:(c + 1) * 64],
                                          start=False, stop=(h2 == 1)),
                     reads=[sp_buf, bf("t_eb")], writes=[psb[6]])
        if hgs <= 6:
            return
        headnorm(ps[6], psb[6], ogain[:, 0:1], [bf("ogain")], t_ke[:], bf("t_ke"))
        P.op("dve", "tensor_tensor", dict(out=bT[:], in0=t_ke[:], in1=t_zh[:], op=ALU.mult),
             reads=[bf("t_ke"), bf("t_zh")], writes=[bf("bT")])
        if own is not None:
            nt_ = 4 if is_p else 1
            P.dma("sp", dict(out=b_own_scr.ap()[HD["h"], own["slot"] * 4:own["slot"] * 4 + nt_].rearrange("t p c -> p t c"),
                             in_=bT[:, 0:nt_ * 128].rearrange("p (t c) -> p t c", c=128)), reads=[bf("bT")])
            return
        emit_ab(ch, 1, bT, bf("bT"))
        if mode == "mix":
            P.op("dve", "tensor_copy", dict(out=dbg_t[:], in_=bT[:]), reads=[bf("bT")], writes=[bf("t_lf")])
            P.dma("sp", dict(out=dbg_b[:, ch * CH:(ch + 1) * CH], in_=dbg_t[:]), reads=[bf("t_lf")])
        if is_p:
            if ch == NCH_P - 1 or (debug and ch == chunks[-1]):
                P.dma("sp", dict(out=s_out[HD["h"], 0], in_=cur[:, 7, :]), reads=[bf("S_%d_7" % (n_local % 2))])
        else:
            b0 = (ch - NCH_P) * 8
            P.dma("sp", dict(out=s_out[HD["h"], 1 + b0:1 + b0 + 8].rearrange("b k v -> k b v"), in_=cur[:]),
                  reads=[bf("S_%d_%d" % (n_local % 2, c)) for c in range(8)])
    mixers.n = 0
    mixers.first = True

    def build_hT(src_dram, row0, ntiles, segs_fn):
        for tt in range(ntiles):
            t0 = row0 + tt * 128
            i = state["tile_i"] % 2
            state["tile_i"] += 1
            xb = xbuf[i]
            xbb = bf("xbuf%d" % i)
            st = stat[i]
            stb = bf("stat%d" % i)
            P.dma("sp", dict(out=xb[:], in_=src_dram[t0:t0 + 128, :]), writes=[xbb])
            P.op("act", "activation", dict(out=junk[:], in_=xb[:], func=AF.Square, accum_out=st[:, 0:1]),
                 reads=[xbb], writes=[bf("xn"), stb])
            P.op("act", "activation", dict(out=st[:, 1:2], in_=st[:, 0:1], func=AF.Ln, scale=1.0 / D, bias=epsb[:, 0:1]),
                 reads=[stb, bf("epsb")], writes=[stb])
            P.op("act", "activation", dict(out=st[:, 2:3], in_=st[:, 1:2], func=AF.Exp, scale=-0.5),
                 reads=[stb], writes=[stb])
            P.op("dve", "tensor_scalar", dict(out=xn[:], in0=xb[:], scalar1=st[:, 2:3], scalar2=None, op0=ALU.mult),
                 reads=[xbb, stb], writes=[bf("xn")])
            for half in range(2):
                tp = ps[half][:, :].bitcast(BF16)
                for kk in range(8):
                    k = half * 8 + kk
                    P.op("pe", "transpose", dict(out=tp[:, kk * 128:(kk + 1) * 128], in_=xn[:, k * 128:(k + 1) * 128],
                                                 identity=ident_b[:]),
                         reads=[bf("xn"), bf("ident_b")], writes=[psb[half]])
                for kk in range(8):
                    k = half * 8 + kk
                    segs = segs_fn(tt)
                    for (c0, c1, r) in segs:
                        o_ap = hT[:, k, tt * 128 + c0:tt * 128 + c1]
                        i_ap = tp[:, kk * 128 + c0:kk * 128 + c1]
                        if len(segs) == 1:
                            wr = [bf("hT_%d_%d_0" % (k, tt)), bf("hT_%d_%d_64" % (k, tt))]
                        else:
                            wr = [bf("hT_%d_%d_%d" % (k, tt, c0))]
                        rd = [psb[half], bf("gm"), bf("sh")]
                        if half == 0:
                            P.op("act", "activation", dict(out=o_ap, in_=i_ap, func=AF.Identity,
                                                           scale=gm[:, k, r:r + 1], bias=sh[:, k, r:r + 1]), reads=rd, writes=wr)
                        else:
                            P.op("dve", "tensor_scalar", dict(out=o_ap, in0=i_ap, scalar1=gm[:, k, r:r + 1],
                                                              scalar2=sh[:, k, r:r + 1], op0=ALU.mult, op1=ALU.add),
                                 reads=rd, writes=wr)
        hTb = {}
        for k in range(KC):
            lst = []
            for tt in range(ntiles):
                lst.append(bf("hT_%d_%d_0" % (k, tt)))
                lst.append(bf("hT_%d_%d_64" % (k, tt)))
            hTb[k] = lst
        return hTb

    def own_slots():
        P.barrier()
        for s_ in range(5):
            is_p = s_ < 4
            nt = 4 if is_p else 1
            N = nt * 128
            if is_p:
                segs_fn = lambda tt: [(0, 128, 0)]
            else:
                segs_fn = lambda tt: [(0, 64, RS0), (64, 128, RS0 + 1)]
            hTb = build_hT(x_own, s_ * CH, nt, segs_fn)
            proj(0, ps[2], psb[2], hTb)
            headnorm(ps[2], psb[2], qkgs2[:, 0:1], [bf("qkgs2a")], QT[:], bf("QT"))
            proj(1, ps[3], psb[3], hTb)
            headnorm(ps[3], psb[3], qkgs2[:, 1:2], [bf("qkgs2b")], kn_f[:], bf("kn_f"))
            P.op("act", "activation", dict(out=ksT[:], in_=kn_f[:], func=AF.Copy), reads=[bf("kn_f")], writes=[bf("ksT")])
            proj(2, ps[2], psb[2], hTb)
            P.op("dve", "tensor_copy", dict(out=vT_f[:], in_=ps[2][:, :]), reads=[psb[2]], writes=[bf("vT_f")])
            for tt in range(nt):
                P.op("pe", "transpose", dict(out=ps[5][:, tt * 128:(tt + 1) * 128], in_=vT_f[:, tt * 128:(tt + 1) * 128],
                                             identity=ident_f[:]),
                     reads=[bf("vT_f"), bf("ident_f")], writes=[psb[5]])
            P.op("dve", "tensor_copy", dict(out=vs_bf[:, 0:nt, :].rearrange("p t d -> p (t d)"), in_=ps[5][:, 0:N]),
                 reads=[psb[5]], writes=[bf("vs_bf")])
            proj(3, ps[3], psb[3], hTb)
            P.op("act", "activation", dict(out=zs_sb[:], in_=ps[3][:, :], func=AF.Silu), reads=[psb[3]], writes=[bf("zs_sb")])
            if is_p:
                blocks = []
                for i_ in range(3, -1, -1):
                    blocks.append(dict(kT=ksT[:, i_ * 128:(i_ + 1) * 128], kbufs=[bf("ksT")], v=vs_bf[:, i_, :], vbufs=[bf("vs_bf")],
                                       mask=maskp[:, i_, :], mbufs=[bf("maskp")]))
                for j in range((128 if _CONTIG else 32 * (s_ + 1)) - 1, -1, -1):
                    blocks.append(dict(kT=KT[:, j * 128:(j + 1) * 128], kbufs=[bf("KT%d" % (j // 4))],
                                       v=Vr[:, j, :], vbufs=[bf("Vr%d" % (j // 4))], bias=vis[:, s_ * 128 + j:s_ * 128 + j + 1]))
                attention(CH, QT[:], [bf("QT")], blocks, ps[7][:, :], 0)
            else:
                P.barrier()
                load_cache(0, cache_k[HD["h"], 0], cache_v[HD["h"], 0])
                for par in range(2):
                    if par == 0:
                        load_cache(1, cache_k[HD["h"], 1], cache_v[HD["h"], 1])
                    i = par
                    kcT = KT[:, i * 4096:(i + 1) * 4096]
                    vc = Vr[:, i * 32:(i + 1) * 32, :]
                    blocks = [dict(kT=ksT[:, 0:128], kbufs=[bf("ksT")], v=vs_bf[:, 0, :], vbufs=[bf("vs_bf")],
                                   mask=masks[:, par, :], mbufs=[bf("masks")])]
                    for j in range(31, -1, -1):
                        blocks.append(dict(kT=kcT[:, j * 128:(j + 1) * 128], kbufs=[bf("kcT%d_%d" % (i, j // 8))],
                                           v=vc[:, j, :], vbufs=[bf("vc%d" % i)]))
                    attention(TSQ, QT[:, par * TSQ:(par + 1) * TSQ], [bf("QT")], blocks, ps[7][:, par * TSQ:(par + 1) * TSQ], 0)
            P.op("dve", "tensor_tensor", dict(out=aT[:, 0:N], in0=ps[7][:, 0:N], in1=zs_sb[:, 0:N], op=ALU.mult),
                 reads=[psb[7], bf("zs_sb")], writes=[bf("bT")])
            P.dma("sp", dict(out=a_own_scr.ap()[HD["h"], s_ * 4:s_ * 4 + nt].rearrange("t p c -> p t c"),
                             in_=aT[:, 0:N].rearrange("p (t c) -> p t c", c=128)), reads=[bf("bT")])
            if is_p:
                pi = (mixers.n + 1) % 2
                P.dma("pool", dict(out=Sset[pi][:, 7, :], out_offset=None, in_=S_scr.ap(),
                                   in_offset=bass.IndirectOffsetOnAxis(ap=idx_s[:, s_:s_ + 1], axis=0)),
                      reads=[bf("idx_s")], writes=[bf("S_%d_7" % pi)], meth="indirect_dma_start")
            mixers(0 if is_p else NCH_P, hTb, own=dict(slot=s_))

    if mode == "fused" and "zero_scr" in flags:
        P.op("dve", "memset", dict(ap=aT[:], constant=0.0), writes=[bf("bT")])
        P.op("dve", "memset", dict(ap=Sset[0][:, 7, :], constant=0.0), writes=[bf("S_0_7")])
        for e_ in range(33):
            P.dma("sp", dict(out=S_scr.ap()[e_ * 128:(e_ + 1) * 128, :], in_=Sset[0][:, 7, :]), reads=[bf("S_0_7")])
        P.op("dve", "memset", dict(ap=KT[:], constant=0.0), writes=[bf("KT%d" % i_) for i_ in range(32)])
        P.op("dve", "memset", dict(ap=Vr[:].rearrange("p j d -> p (j d)"), constant=0.0), writes=[bf("Vr%d" % i_) for i_ in range(32)])
        P.barrier()
    for hd in range(NH if do_mix else 0):
        HD["h"] = hd
        head_setup()
        for ch in chunks:
            if ch < NCH_P:
                segs_fn = lambda tt: [(0, 128, 0)]
            else:
                segs_fn = (lambda ch: lambda tt: [(0, 64, 1 + ((ch - NCH_P) * 4 + tt) * 2), (64, 128, 2 + ((ch - NCH_P) * 4 + tt) * 2)])(ch)
            if mode == "fused" and hd > 0:
                hTb = {k: [bf("hT_%d_%d_0" % (k, tt)) for tt in range(4)] + [bf("hT_%d_%d_64" % (k, tt)) for tt in range(4)]
                       for k in range(KC)}
                P.dma("pool", dict(out=hT[:].rearrange("p k c -> p (k c)"), in_=hT_scr.ap()[ch]),
                      writes=[b_ for k in range(KC) for b_ in hTb[k]])
            else:
                hTb = build_hT(x_all, ch * CH - x_base, 4, segs_fn)
                if mode == "fused":
                    P.dma("sp", dict(out=hT_scr.ap()[ch], in_=hT[:].rearrange("p k c -> p (k c)")),
                          reads=[b_ for k in range(KC) for b_ in hTb[k]])

            if stop in ("hT", "ev_act", "ev_dve"):
                return done()
            if mode != "fused":
                proj(0, ps[2], psb[2], hTb)
                headnorm(ps[2], psb[2], qkgs2[:, 0:1], [bf("qkgs2a")], QT[:], bf("QT"))
            if stop == "q":
                return done()
            proj(1, ps[3], psb[3], hTb)
            headnorm(ps[3], psb[3], qkgs2[:, 1:2], [bf("qkgs2b")], kn_f[:], bf("kn_f"))
            if ch < NCH_P:
                P.op("act", "activation", dict(out=KT[:, ch * CH:(ch + 1) * CH], in_=kn_f[:], func=AF.Copy),
                     reads=[bf("kn_f")], writes=[bf("KT%d" % ch)])
            else:
                P.op("act", "activation", dict(out=ksT[:], in_=kn_f[:], func=AF.Copy), reads=[bf("kn_f")], writes=[bf("ksT")])
            tr_out(ch, kn_f, bf("kn_f"), k_out[HD["h"]])
            if stop == "k":
                return done()
            proj(2, ps[2], psb[2], hTb)
            P.op("dve", "tensor_copy", dict(out=vT_f[:], in_=ps[2][:, :]), reads=[psb[2]], writes=[bf("vT_f")])

            def vcopy(pt, ptb, ch=ch):
                if ch < NCH_P:
                    P.op("dve", "tensor_copy", dict(out=Vr[:, ch * 4:(ch + 1) * 4, :].rearrange("p t d -> p (t d)"), in_=pt[:, :]),
                         reads=[ptb], writes=[bf("Vr%d" % ch)])
                else:
                    P.op("dve", "tensor_copy", dict(out=vs_bf[:].rearrange("p t d -> p (t d)"), in_=pt[:, :]),
                         reads=[ptb], writes=[bf("vs_bf")])
            tr_out(ch, vT_f, bf("vT_f"), v_out[HD["h"]], extra_copy=vcopy)
            mixers(ch, hTb)
        if mode == "fused" and len(chunks) > 0:
            own_slots()

    if do_out:
        P.barrier()
        es1.close()
        gateP = sb("gateP", [128, D], F32)
        gateS = sb("gateS", [128, D], F32)
        es3 = ExitStack()
        csbc = [es3.enter_context(nc.sbuf_tensor("csbc%d" % i, [128, KC, 128], BF16)) for i in range(2)]
        bgate = es3.enter_context(nc.sbuf_tensor("bgate", [128, D], F32))
        wada2 = [es3.enter_context(nc.sbuf_tensor("wada2_%d" % i, [128, KC, 512], BF16)) for i in range(2)]
        P.dma("sp", dict(out=bgate[:], in_=b_gate_bc[:, :]), writes=[bf("bgate")])
        P.op("dve", "tensor_copy", dict(out=csbc[0][:], in_=csT[:, :, 0:1].to_broadcast([128, KC, 128])),
             reads=[bf("csT")], writes=[bf("csbc0")])
        P.op("dve", "tensor_copy", dict(out=csbc[1][:, :, 0:64], in_=csT[:, :, RS0:RS0 + 1].to_broadcast([128, KC, 64])),
             reads=[bf("csT")], writes=[bf("csbc1a")])
        P.op("dve", "tensor_copy", dict(out=csbc[1][:, :, 64:128], in_=csT[:, :, RS0 + 1:RS0 + 2].to_broadcast([128, KC, 64])),
             reads=[bf("csT")], writes=[bf("csbc1b")])
        for g in range(8, 12):
            wb = wada2[g % 2]
            wbuf = bf("wada2_%d" % (g % 2))
            P.dma("pool", dict(out=wb[:], in_=w_ada[:, g * 512:(g + 1) * 512].rearrange("(k p) c -> p k c", p=128)),
                  writes=[wbuf])
            for which in range(2):
                pb = 3 + which
                for k in range(KC):
                    P.op("pe", "matmul", dict(out=ps[pb][:, :], lhsT=csbc[which][:, k, :], rhs=wb[:, k, :],
                                              start=(k == 0), stop=(k == KC - 1)),
                         reads=[wbuf, bf("csbc0"), bf("csbc1a"), bf("csbc1b")], writes=[psb[pb]])
                gt = gateP if which == 0 else gateS
                P.op("dve", "tensor_tensor", dict(out=gt[:, (g - 8) * 512:(g - 7) * 512], in0=ps[pb][:, :],
                                                  in1=bgate[:, (g - 8) * 512:(g - 7) * 512], op=ALU.add),
                     reads=[psb[pb], bf("bgate")], writes=[bf("gate%d_%d" % (which, g - 8))])
        P.barrier()
        es3.close()
        wslot = [sb("wslot%d" % i, [128, 24576], BF16) for i in range(2)]
        abT = sb("abT", [128, 2, 8, CH], BF16)
        mT = sb("mT", [128, KC, CH], BF16)
        sgA = sb("sgA", [128, CH], F32)
        sgB = sb("sgB", [128, CH], F32)
        tmp1 = sb("tmp1", [128, CH], F32)
        tmp2 = sb("tmp2", [128, CH], F32)
        ysl = [sb("ysl%d" % i, [128, CH], F32) for i in range(2)]
        xsl = [sb("xsl%d" % i, [128, CH], F32) for i in range(2)]
        gi = 0
        yi = 0
        for o in range(5):
            N = CH if o < 4 else OWN_S
            NT = N // 128
            if o < 4:
                segs_fn = lambda tt: [(0, 128, 0)]
            else:
                segs_fn = lambda tt: [(0, 64, RS0), (64, 128, RS0 + 1)]
            hTb = build_hT(x_own, o * CH, NT, segs_fn)
            if mode == "out":
                for ab in range(2):
                    P.dma("pool", dict(out=abT[:, ab, :, 0:N], in_=ab_own[ab, :, :, o * CH:o * CH + N].rearrange("h p t -> p h t")),
                          writes=[bf("abT%d" % ab)])
                abbufs = {0: [bf("abT0")], 1: [bf("abT1")]}
            else:
                abbufs = {0: [], 1: []}
                for h in range(8):
                    P.dma("sp", dict(out=abT[:, 0, h, 0:N].rearrange("p (t c) -> p t c", c=128),
                                     in_=a_own_scr.ap()[h, o * 4:o * 4 + NT].rearrange("t p c -> p t c")),
                          writes=[bf("abT_a%d" % h)])
                    abbufs[0].append(bf("abT_a%d" % h))
                for h in range(8):
                    P.dma("sp", dict(out=abT[:, 1, h, 0:N].rearrange("p (t c) -> p t c", c=128),
                                     in_=b_own_scr.ap()[h, o * 4:o * 4 + NT].rearrange("t p c -> p t c")),
                          writes=[bf("abT_b%d" % h)])
                    abbufs[1].append(bf("abT_b%d" % h))
            for g in range(4):
                sl = gi % 2
                gi += 1
                ws = wslot[sl]
                wgA = ws[:, 0:8192].rearrange("p (k c) -> p k c", c=512)
                wgB = ws[:, 8192:16384].rearrange("p (k c) -> p k c", c=512)
                wbA = ws[:, 16384:20480].rearrange("p (h c) -> p h c", c=512)
                wbB = ws[:, 20480:24576].rearrange("p (h c) -> p h c", c=512)
                P.dma("pool", dict(out=wgA, in_=w_gate[:, g * 512:(g + 1) * 512].rearrange("(k p) c -> p k c", p=128)),
                      writes=[bf("ws%d_A" % sl)])
                P.dma("pool", dict(out=wgB, in_=w_gate[:, D + g * 512:D + (g + 1) * 512].rearrange("(k p) c -> p k c", p=128)),
                      writes=[bf("ws%d_B" % sl)])
                P.dma("pool", dict(out=wbA, in_=w_bsb[:, g * 512:(g + 1) * 512].rearrange("(h p) c -> p h c", p=128)),
                      writes=[bf("ws%d_C" % sl)])
                P.dma("pool", dict(out=wbB, in_=w_bhg[:, g * 512:(g + 1) * 512].rearrange("(h p) c -> p h c", p=128)),
                      writes=[bf("ws%d_D" % sl)])
                for cc in range(4):
                    jc = g * 4 + cc
                    for (wg, wgbuf, pb, sg, sgbuf) in ((wgA, bf("ws%d_A" % sl), 2, sgA, bf("sgA")), (wgB, bf("ws%d_B" % sl), 3, sgB, bf("sgB"))):
                        for k in range(KC):
                            P.op("pe", "matmul", dict(out=ps[pb][:, 0:N], lhsT=wg[:, k, cc * 128:(cc + 1) * 128], rhs=hT[:, k, 0:N],
                                                      start=(k == 0), stop=(k == KC - 1)),
                                 reads=[wgbuf] + hTb[k], writes=[psb[pb]])
                        P.op("act", "activation", dict(out=sg[:, 0:N], in_=ps[pb][:, 0:N], func=AF.Sigmoid),
                             reads=[psb[pb]], writes=[sgbuf])
                    for (wbr, wbbuf, pb, ab) in ((wbA, bf("ws%d_C" % sl), 4, 0), (wbB, bf("ws%d_D" % sl), 5, 1)):
                        for h in range(8):
                            P.op("pe", "matmul", dict(out=ps[pb][:, 0:N], lhsT=wbr[:, h, cc * 128:(cc + 1) * 128], rhs=abT[:, ab, h, 0:N],
                                                      start=(h == 0), stop=(h == 7)),
                                 reads=[wbbuf] + abbufs[ab], writes=[psb[pb]])
                    P.op("dve", "tensor_tensor", dict(out=tmp1[:, 0:N], in0=ps[4][:, 0:N], in1=sgA[:, 0:N], op=ALU.mult),
                         reads=[psb[4], bf("sgA")], writes=[bf("tmp1")])
                    P.op("dve", "tensor_tensor", dict(out=tmp2[:, 0:N], in0=ps[5][:, 0:N], in1=sgB[:, 0:N], op=ALU.mult),
                         reads=[psb[5], bf("sgB")], writes=[bf("tmp2")])
                    P.op("pool", "tensor_tensor", dict(out=mT[:, jc, 0:N], in0=tmp1[:, 0:N], in1=tmp2[:, 0:N], op=ALU.add),
                         reads=[bf("tmp1"), bf("tmp2")], writes=[bf("mT%d" % jc)])
            for cg in range(4):
                sl = gi % 2
                gi += 1
                ws = wslot[sl]
                wo = ws[:, 0:8192].rearrange("p (k c) -> p k c", c=512)
                P.dma("pool", dict(out=wo, in_=w_o[:, cg * 512:(cg + 1) * 512].rearrange("(k p) c -> p k c", p=128)),
                      writes=[bf("ws%d_A" % sl)])
                for tt in range(NT):
                    pb = 6 + (yi % 2)
                    ys = ysl[yi % 2]
                    xs = xsl[yi % 2]
                    ysb = bf("ysl%d" % (yi % 2))
                    xsb = bf("xsl%d" % (yi % 2))
                    yi += 1
                    r0 = o * CH + tt * 128
                    P.dma("sp", dict(out=xs[:], in_=x_own[r0:r0 + 128, cg * 512:(cg + 1) * 512]), writes=[xsb])
                    for k in range(KC):
                        P.op("pe", "matmul", dict(out=ps[pb][:, :], lhsT=mT[:, k, tt * 128:(tt + 1) * 128], rhs=wo[:, k, :],
                                                  start=(k == 0), stop=(k == KC - 1)),
                             reads=[bf("ws%d_A" % sl), bf("mT%d" % k)], writes=[psb[pb]])
                    gt = gateP if o < 4 else gateS
                    which = 0 if o < 4 else 1
                    P.op("dve", "tensor_tensor", dict(out=ys[:], in0=ps[pb][:, :], in1=gt[:, cg * 512:(cg + 1) * 512], op=ALU.mult),
                         reads=[psb[pb], bf("gate%d_%d" % (which, cg))], writes=[ysb])
                    P.op("pool", "tensor_tensor", dict(out=ys[:], in0=ys[:], in1=xs[:], op=ALU.add),
                         reads=[ysb, xsb], writes=[ysb])
                    P.dma("sp", dict(out=y_own[r0:r0 + 128, cg * 512:(cg + 1) * 512], in_=ys[:]), reads=[ysb])

    P.finish()
    P.emit(nc, es)
    if not do_out:
        es1.close()
    es.close()
    return nc


def _consts():
    j = np.arange(128)[:, None]
    k = np.arange(128)[None, :]
    trin = np.where(j >= k, -1.0, 0.0).astype(np.float32)
    tric = np.where(j < k, -1.0, 0.0).astype(np.float32)
    q = np.arange(512)[None, None, :]
    kk = np.arange(128)[:, None, None]
    i = np.arange(4)[None, :, None]
    maskp = np.where(q > kk + 128 * i, 0.0, -30000.0).astype(np.float32)
    qs = np.arange(64)[None, :]
    ks = np.arange(128)[:, None]
    m_even = np.where((ks < 64) & (qs > ks), 0.0, -30000.0)
    m_odd = np.where((ks >= 64) & (qs > ks - 64), 0.0, -30000.0)
    masks = np.stack([m_even, m_odd], axis=1).astype(np.float32)
    s_ = np.arange(128)[:, None]
    t_ = np.arange(128)[None, :]
    hmask = ((s_ // 64 == t_ // 64) & (s_ <= t_)).astype(np.float32)
    resetm = np.ones((128, 512), np.float32)
    resetm[:, ::64] = 0.0
    return {"identf": np.eye(128, dtype=np.float32), "trin": trin, "tric": tric, "maskp": maskp, "masks": masks,
            "hmask": hmask, "resetm": resetm,
            "pmask": np.stack([(np.arange(128) < 64), (np.arange(128) >= 64)], axis=1).astype(np.float32)}


def _f32(a):
    return np.ascontiguousarray(np.asarray(a, dtype=np.float32))


def make_in_maps(inp, cores=None, x_rows=TT, x_base=0):
    f32 = _f32
    x_all = f32(np.concatenate([np.asarray(inp["x_prompt"]).reshape(TP, D), np.asarray(inp["x_sample"]).reshape(TS, D)], axis=0))[x_base:x_base + x_rows]
    c_all = f32(np.concatenate([np.asarray(inp["c_prompt"]), np.asarray(inp["c_sample"])], axis=0))
    w_ada0 = f32(np.asarray(inp["w_ada"])[0])
    b_adaT = f32(np.asarray(inp["b_ada"])[0].reshape(48, 128).T)
    ngT = f32(np.asarray(inp["norm_gain"])[0].reshape(KC, 128).T)
    w_in0 = np.asarray(inp["w_in"])[0]
    qkg = f32(np.stack([np.asarray(inp["q_norm_gain"])[0], np.asarray(inp["k_norm_gain"])[0]], axis=1))
    consts = _consts()
    in_maps = []
    for c in (range(NCORES) if cores is None else cores):
        cols = np.concatenate([np.arange(j * 1024 + c * 128, j * 1024 + (c + 1) * 128) for j in range(8)])
        lbr = f32(np.asarray(inp["hgrn_lb_raw"])[:, c * 128:(c + 1) * 128].T)
        m = {"x_all": x_all, "c_all": c_all, "w_ada": w_ada0, "b_adaT": b_adaT, "ngT": ngT,
             "w_head": f32(w_in0[:, cols])[None], "qkg": qkg, "lbr": lbr[None],
             "ogain": f32(np.asarray(inp["hgrn_onorm_gain"])[0, c, :].reshape(1, 128, 1)),
             "cache_k": f32(np.asarray(inp["cache_sb_k"])[0, :, :, c, :])[None],
             "cache_v": f32(np.asarray(inp["cache_sb_v"])[0, :, :, c, :])[None],
             "s_in": f32(np.asarray(inp["state_hgrn"])[0, :, c])[None]}
        m.update(consts)
        in_maps.append(m)
    return in_maps


def make_out_maps(inp, a_all, b_all, cores=None):
    f32 = _f32
    xp = np.asarray(inp["x_prompt"]).reshape(TP, D)
    xs = np.asarray(inp["x_sample"]).reshape(TS, D)
    cp = np.asarray(inp["c_prompt"])
    cs = np.asarray(inp["c_sample"])
    w_ada0 = f32(np.asarray(inp["w_ada"])[0])
    b_ada0 = np.asarray(inp["b_ada"])[0]
    b_adaT = f32(b_ada0.reshape(48, 128).T)
    ngT = f32(np.asarray(inp["norm_gain"])[0].reshape(KC, 128).T)
    w_gate = f32(np.asarray(inp["w_in"])[0][:, 8192:])
    w_bsb = f32(np.asarray(inp["w_branch_sb"])[0])
    w_bhg = f32(np.asarray(inp["w_branch_hgrn"])[0])
    w_o = f32(np.asarray(inp["w_out"])[0])
    b_gate_bc = f32(np.broadcast_to(b_ada0[2 * D:][None, :], (128, D)))
    maps = []
    for c in (range(NCORES) if cores is None else cores):
        ptok = np.concatenate([own_chunk(c, s_) * CH + np.arange(CH) for s_ in range(4)])
        tok = np.concatenate([ptok, TP + np.arange(c * OWN_S, (c + 1) * OWN_S)])
        m = {"c_all": f32(np.concatenate([cp, cs[2 * c:2 * c + 2]], axis=0)), "w_ada": w_ada0, "b_adaT": b_adaT, "ngT": ngT,
             "identf": np.eye(128, dtype=np.float32),
             "x_own": f32(np.concatenate([xp[ptok], xs[c * OWN_S:(c + 1) * OWN_S]], axis=0)),
             "w_gate": w_gate, "w_bsb": w_bsb, "w_bhg": w_bhg, "w_o": w_o, "b_gate_bc": b_gate_bc,
             }
        if a_all is not None:
            m["ab_own"] = f32(np.stack([a_all[:, :, tok], b_all[:, :, tok]], axis=0))
        maps.append(m)
    return maps


def make_fused_maps(inp, cores=None, x_rows=TT, x_base=0):
    f32 = _f32
    base = make_out_maps(inp, None, None, cores=cores)
    x_all = f32(np.concatenate([np.asarray(inp["x_prompt"]).reshape(TP, D), np.asarray(inp["x_sample"]).reshape(TS, D)], axis=0))[x_base:x_base + x_rows]
    cp = np.asarray(inp["c_prompt"])
    cs = np.asarray(inp["c_sample"])
    w_in0 = np.asarray(inp["w_in"])[0]
    w_head = f32(np.stack([w_in0[:, np.concatenate([np.arange(j * 1024 + h * 128, j * 1024 + (h + 1) * 128) for j in range(8)])]
                           for h in range(8)], axis=0))
    qkg = f32(np.stack([np.asarray(inp["q_norm_gain"])[0], np.asarray(inp["k_norm_gain"])[0]], axis=1))
    lbr = f32(np.asarray(inp["hgrn_lb_raw"]).reshape(2, 8, 128).transpose(1, 2, 0))
    ogain = f32(np.asarray(inp["hgrn_onorm_gain"])[0].reshape(8, 128, 1))
    ck = np.asarray(inp["cache_sb_k"])[0]
    cv = np.asarray(inp["cache_sb_v"])[0]
    s_in = f32(np.asarray(inp["state_hgrn"])[0].transpose(1, 0, 2, 3))
    consts = _consts()
    maps = []
    for n, c in enumerate(range(NCORES) if cores is None else cores):
        m = dict(base[n])
        m.pop("ab_own", None)
        m["c_all"] = f32(np.concatenate([cp, cs, cs[2 * c:2 * c + 2]], axis=0))
        idx_s = np.stack([own_chunk(c, s_) * 128 + np.arange(128) for s_ in range(4)], axis=1).astype(np.int32)
        vis = np.full((128, 512), -30000.0, np.float32)
        for s_ in range(4):
            vis[:, s_ * 128:s_ * 128 + 4 * own_chunk(c, s_)] = 0.0
        m.update({"x_all": x_all, "w_head": w_head, "qkg": qkg, "lbr": lbr, "ogain": ogain,
                  "cache_k": f32(ck[2 * c:2 * c + 2].transpose(2, 0, 1, 3)), "cache_v": f32(cv[2 * c:2 * c + 2].transpose(2, 0, 1, 3)),
                  "s_in": s_in, "idx_s": idx_s, "vis": vis,
                  "s_in_own": f32(s_in[:, 2 * c:2 * c + 2])})
        m.update(consts)
        maps.append(m)
    return maps


def kernel(**inp):
    nc = build(mode="fused")
    res = run_bass_kernel_spmd(nc, make_fused_maps(inp), core_ids=list(range(NCORES)))
    r = res.results
    kall = r[0]["k_out"].reshape(8, TT, 128).transpose(1, 0, 2)
    vall = r[0]["v_out"].reshape(8, TT, 128).transpose(1, 0, 2)
    sall = r[0]["s_out"].reshape(8, 17, 128, 128).transpose(1, 0, 2, 3)
    y = [r[c]["y_own"] for c in range(NCORES)]
    y_prompt = np.zeros((TP, D), np.float32)
    for c in range(NCORES):
        for s_ in range(4):
            y_prompt[own_chunk(c, s_) * CH:(own_chunk(c, s_) + 1) * CH] = y[c][s_ * CH:(s_ + 1) * CH]
    y_prompt = y_prompt.reshape(1, TP, D)
    y_sample = np.concatenate([yy[OWN_P:] for yy in y], axis=0).reshape(NB, TSQ, D)
    new_k_prompt = kall[:TP].reshape(1, 1, TP, 8, 128)
    new_v_prompt = vall[:TP].reshape(1, 1, TP, 8, 128)
    new_k_sample = kall[TP:].reshape(1, NB, TSQ, 8, 128)
    new_v_sample = vall[TP:].reshape(1, NB, TSQ, 8, 128)
    new_s_prompt = sall[0].reshape(1, 1, 8, 128, 128)
    new_s_sample = sall[1:].reshape(1, NB, 8, 128, 128)
    c32 = lambda a: np.ascontiguousarray(a, dtype=np.float32)
    return (c32(y_prompt), c32(y_sample), c32(new_k_prompt), c32(new_v_prompt), c32(new_s_prompt),
            c32(new_k_sample), c32(new_v_sample), c32(new_s_sample))
```

```python
import numpy as np
from contextlib import ExitStack
import concourse.bass as bass
import concourse.mybir as mybir
from concourse.bass_utils import run_bass_kernel_spmd

F32 = mybir.dt.float32
BF16 = mybir.dt.bfloat16
AF = mybir.ActivationFunctionType
ALU = mybir.AluOpType

NCORES = 8
D = 2048
KC = D // 128
TP = 16384
NB = 16
TSQ = 64
TS = NB * TSQ
TT = TP + TS
CH = 512
NCH_P = TP // CH
NCH = TT // CH
PAST = 4096
EPS = 1e-6
SEM_CHUNK = 16000
NSLOT = 8
OWN_P = TP // NCORES
OWN_S = TS // NCORES
OWN = OWN_P + OWN_S


import os
_CONTIG = bool(os.environ.get("OWN_CONTIG"))


def own_chunk(c, s):
    if _CONTIG:
        return 4 * c + s
    return [c, 15 - c, 16 + c, 31 - c][s]


class Buf:
    __slots__ = ("name", "w", "r", "psum")

    def __init__(self, name, psum=False):
        self.name = name
        self.w = None
        self.r = {}
        self.psum = psum


class Q:
    def __init__(self, name):
        self.name = name
        self.ops = []
        self.n = 0
        self.waited = {}
        self.maxchunk = {}
        self.dma_k = 0
        self.slot_tot = [0] * NSLOT


class Prog:
    def __init__(self):
        self.q = {n: Q(n) for n in ("pe", "act", "dve", "pool", "sp")}
        self.keys = []
        self.keyset = set()

    def _key(self, key):
        if key not in self.keyset:
            self.keyset.add(key)
            self.keys.append(key)

    def _deps(self, q, reads, writes, extra):
        deps = {}

        def add(t):
            if t is None:
                return
            k, v = t
            if deps.get(k, 0) < v:
                deps[k] = v

        for b in reads:
            add(b.w)
            if b.psum:
                for rk, t in b.r.items():
                    if rk != q.name:
                        add(t)
        for b in writes:
            add(b.w)
            for t in b.r.values():
                add(t)
        for t in extra:
            add(t)
        waits = []
        for key, val in deps.items():
            if q.name == "pe" and key[0] == "pe":
                continue
            if isinstance(key[1], int) and key[0] in ("pe", "act", "dve", "pool", "sp"):
                mc = q.maxchunk.get(key[0], -1)
                if key[1] < mc:
                    continue
                if key[1] > mc:
                    q.maxchunk[key[0]] = key[1]
            if q.waited.get(key, 0) >= val:
                continue
            q.waited[key] = val
            waits.append((key, val))
        return waits

    def _mark(self, tok, reads, writes, rkey):
        for b in writes:
            b.w = tok
            b.r = {}
        for b in reads:
            b.r[rkey] = tok

    def op(self, qn, meth, kw, reads=(), writes=(), extra=()):
        fn = (meth, kw)
        q = self.q[qn]
        waits = self._deps(q, reads, writes, extra)
        idx = q.n
        q.n += 1
        key = (qn, idx // SEM_CHUNK)
        self._key(key)
        tok = (key, idx % SEM_CHUNK + 1)
        q.ops.append((waits, fn, (key, 1)))
        self._mark(tok, reads, writes, qn)
        return tok

    def dma(self, qn, kw, reads=(), writes=(), extra=(), meth="dma_start"):
        fn = (meth, kw)
        q = self.q[qn]
        slot = q.dma_k % NSLOT
        q.dma_k += 1
        key = (qn + "_dma", slot)
        self._key(key)
        prev = q.slot_tot[slot]
        ex = list(extra)
        if prev > 0:
            ex.append((key, prev))
        waits = self._deps(q, reads, writes, ex)
        q.slot_tot[slot] = prev + 16
        tok = (key, prev + 16)
        q.ops.append((waits, fn, (key, 16)))
        self._mark(tok, reads, writes, key)
        return tok

    def barrier(self):
        toks = []
        for q in self.q.values():
            for sl in range(NSLOT):
                if q.slot_tot[sl] > 0:
                    toks.append(((q.name + "_dma", sl), q.slot_tot[sl]))
            if q.n > 0:
                idx = q.n - 1
                toks.append(((q.name, idx // SEM_CHUNK), idx % SEM_CHUNK + 1))
        for q in self.q.values():
            waits = self._deps(q, (), (), toks)
            if waits:
                q.ops.append((waits, None, None))

    def finish(self):
        ex = []
        for q in self.q.values():
            for s in range(NSLOT):
                if q.slot_tot[s] > 0:
                    ex.append(((q.name + "_dma", s), q.slot_tot[s]))
            if q.n > 0 and q.name != "sp":
                idx = q.n - 1
                ex.append(((q.name, idx // SEM_CHUNK), idx % SEM_CHUNK + 1))
        q = self.q["sp"]
        waits = self._deps(q, (), (), ex)
        q.ops.append((waits, None, None))

    def emit(self, nc, es):
        sems = {}
        for key in self.keys:
            sems[key] = es.enter_context(nc.semaphore("s_%s_%s" % (key[0], key[1])))
        with nc.Block() as block:
            self._emit_block(block, sems)

    def _emit_block(self, block, sems):

        def run(q):
            def body(e):
                for waits, fn, inc in q.ops:
                    for key, val in waits:
                        e.wait_ge(sems[key], val)
                    if fn is None:
                        continue
                    ins = getattr(e, fn[0])(**fn[1])
                    ins.then_inc(sems[inc[0]], inc[1])
            return body

        block.tensor(run(self.q["pe"]))
        block.scalar(run(self.q["act"]))
        block.vector(run(self.q["dve"]))
        block.gpsimd(run(self.q["pool"]))
        block.sync(run(self.q["sp"]))


def build(mode="mix", chunks=None, x_rows=TT, stop=None, x_base=0, flags=("att", "hg"), debug=False):
    chunks = list(range(NCH)) if chunks is None else chunks
    if mode == "out":
        chunks = []
    do_mix = mode in ("mix", "fused")
    do_out = mode in ("out", "fused")
    NR = {"mix": 17, "out": 3, "fused": 19}[mode]
    RS0 = NR - 2
    if do_mix:
        debug = True
    nc = bass.Bass("TRN2", target_bir_lowering=False)
    es = ExitStack()
    P = Prog()

    def dram_in(name, shape, dt=F32):
        return nc.dram_tensor(name, list(shape), dt, kind="ExternalInput").ap()

    def dram_out(name, shape, dt=F32):
        return nc.dram_tensor(name, list(shape), dt, kind="ExternalOutput").ap()

    def sb(name, shape, dt):
        return es.enter_context(nc.sbuf_tensor(name, list(shape), dt))

    c_all = dram_in("c_all", [NR, D])
    w_ada = dram_in("w_ada", [D, 3 * D])
    b_adaT = dram_in("b_adaT", [128, 48])
    ngT = dram_in("ngT", [128, KC])
    identf_d = dram_in("identf", [128, 128])
    if do_mix:
        NH = 8 if mode == "fused" else 1
        HD = {"h": 0}
        x_all = dram_in("x_all", [x_rows, D])
        w_head = dram_in("w_head", [NH, D, 1024])
        qkg = dram_in("qkg", [128, 2])
        k_out = dram_out("k_out", [NH, TT, 128])
        v_out = dram_out("v_out", [NH, TT, 128])
        s_out = dram_out("s_out", [NH, 17, 128, 128])
        trin_d = dram_in("trin", [128, 128])
        tric_d = dram_in("tric", [128, 128])
        maskp_d = dram_in("maskp", [128, 4, 512])
        masks_d = dram_in("masks", [128, 2, 64])
        hmask_d = dram_in("hmask", [128, 128])
        resetm_d = dram_in("resetm", [128, 512])
        lbr_d = dram_in("lbr", [NH, 128, 2])
        ogain_d = dram_in("ogain", [NH, 128, 1])
        pmask_d = dram_in("pmask", [128, 2])
        if mode == "fused":
            cache_k = dram_in("cache_k", [NH, 2, PAST, 128])
            cache_v = dram_in("cache_v", [NH, 2, PAST, 128])
            vis_d = dram_in("vis", [128, 512])
            a_own_scr = nc.dram_tensor("a_own_scr", [8, 17, 128, 128], BF16)
            b_own_scr = nc.dram_tensor("b_own_scr", [8, 17, 128, 128], BF16)
            hT_scr = nc.dram_tensor("hT_scr", [NCH, 128, KC * CH], BF16)
            hTo_scr = nc.dram_tensor("hTo_scr", [5, 128, KC * CH], BF16)
            S_scr = nc.dram_tensor("S_scr", [33 * 128, 128], F32)
            s_in_own = dram_in("s_in_own", [NH, 2, 128, 128])
            idx_s_d = nc.dram_tensor("idx_s", [128, 4], mybir.dt.int32, kind="ExternalInput").ap()
        else:
            cache_k = dram_in("cache_k", [NH, NB, PAST, 128])
            cache_v = dram_in("cache_v", [NH, NB, PAST, 128])
        s_in = dram_in("s_in", [NH, NB, 128, 128])
        if mode == "mix":
            dbg_a = dram_out("dbg_a", [128, TT])
            dbg_b = dram_out("dbg_b", [128, TT])
    if do_out:
        x_own = dram_in("x_own", [OWN, D])
        w_gate = dram_in("w_gate", [D, 2 * D])
        w_bsb = dram_in("w_bsb", [1024, D])
        w_bhg = dram_in("w_bhg", [1024, D])
        w_o = dram_in("w_o", [D, D])
        b_gate_bc = dram_in("b_gate_bc", [128, D])
        y_own = dram_out("y_own", [OWN, D])
        if mode == "out":
            ab_own = dram_in("ab_own", [2, 8, 128, OWN])

    es1 = ExitStack()

    def sb1(name, shape, dt):
        return es1.enter_context(nc.sbuf_tensor(name, list(shape), dt))

    ident_f = sb("ident_f", [128, 128], F32)
    ident_b = sb("ident_b", [128, 128], BF16)
    ones_b = sb("ones_b", [128, 128], BF16)
    csT = sb("csT", [128, KC, NR], BF16)
    gm = sb("gm", [128, KC, NR], F32)
    sh = sb("sh", [128, KC, NR], F32)
    badaT = sb("badaT", [128, 48], F32)
    ngTs = sb("ngTs", [128, KC], F32)
    epsb = sb("epsb", [128, 1], F32)
    xbuf = [sb("xbuf%d" % i, [128, D], F32) for i in range(2)]
    xn = sb("xn", [128, D], BF16)
    junk = xn
    stat = [sb("stat%d" % i, [128, 4], F32) for i in range(2)]
    hT = sb("hT", [128, KC, CH], BF16)
    if do_mix:
        wh = sb1("wh", [128, KC, 1024], BF16)
        qkgs = sb1("qkgs", [128, 2], F32)
        qkgs2 = sb1("qkgs2", [128, 2], F32)
        sq_b = sb1("sq_b", [128, CH], BF16)
        rstd_f = sb1("rstd_f", [128, CH], F32)
        kn_f = sb1("kn_f", [128, CH], F32)
        vT_f = sb1("vT_f", [128, CH], F32)
        KT = sb1("KT", [128, TP], BF16)
        Vr = sb1("Vr", [128, TP // 128, 128], BF16)
        QT = sb1("QT", [128, CH], BF16)
    es0 = ExitStack()
    c_sb = es0.enter_context(nc.sbuf_tensor("c_sb", [NR, D], F32))
    cs_sb = es0.enter_context(nc.sbuf_tensor("cs_sb", [NR, D], F32))
    wada = [es0.enter_context(nc.sbuf_tensor("wada%d" % i, [128, KC, 512], BF16)) for i in range(2)]
    modsb = es0.enter_context(nc.sbuf_tensor("modsb", [128, 48, NR], F32))

    ps = [es.enter_context(nc.psum_tensor("ps%d" % i, [128, 512], F32)) for i in range(8)]
    psb = [Buf("ps%d" % i, psum=True) for i in range(8)]

    B = {}

    def bf(name):
        if name not in B:
            B[name] = Buf(name)
        return B[name]

    def done():
        P.finish()
        P.emit(nc, es)
        es.close()
        return nc
    P.dma("sp", dict(out=ident_f[:], in_=identf_d[:, :]), writes=[bf("ident_f")])
    P.dma("sp", dict(out=badaT[:], in_=b_adaT[:, :]), writes=[bf("badaT")])
    P.dma("sp", dict(out=ngTs[:], in_=ngT[:, :]), writes=[bf("ngTs")])
    P.dma("sp", dict(out=c_sb[:], in_=c_all[:, :]), writes=[bf("c_sb")])
    P.op("dve", "tensor_copy", dict(out=ident_b[:], in_=ident_f[:]), reads=[bf("ident_f")], writes=[bf("ident_b")])
    P.op("dve", "memset", dict(ap=ones_b[:], constant=1.0), writes=[bf("ones_b")])
    P.op("dve", "memset", dict(ap=epsb[:], constant=EPS), writes=[bf("epsb")])
    if do_mix:
        P.dma("sp", dict(out=qkgs[:], in_=qkg[:, :]), writes=[bf("qkgs")])
        P.op("dve", "tensor_scalar", dict(out=qkgs2[:, 0:1], in0=qkgs[:, 0:1], scalar1=float(128 ** -0.5),
                                          scalar2=None, op0=ALU.mult), reads=[bf("qkgs")], writes=[bf("qkgs2a")])
        P.op("dve", "tensor_copy", dict(out=qkgs2[:, 1:2], in_=qkgs[:, 1:2]), reads=[bf("qkgs")], writes=[bf("qkgs2b")])

    P.op("act", "activation", dict(out=cs_sb[:], in_=c_sb[:], func=AF.Silu), reads=[bf("c_sb")], writes=[bf("cs_sb")])
    pT = ps[0]
    for k in range(KC):
        P.op("pe", "transpose", dict(out=pT[0:128, k * NR:(k + 1) * NR], in_=cs_sb[:, k * 128:(k + 1) * 128],
                                     identity=ident_f[0:NR, 0:NR]),
             reads=[bf("cs_sb"), bf("ident_f")], writes=[psb[0]])
    P.op("dve", "tensor_copy", dict(out=csT[:].rearrange("p k r -> p (k r)"), in_=pT[:, 0:KC * NR]),
         reads=[psb[0]], writes=[bf("csT")])
    modps = [ps[1], ps[2]]
    for g in range(12):
        wb = wada[g % 2]
        wbuf = bf("wada%d" % (g % 2))
        P.dma("pool", dict(out=wb[:], in_=w_ada[:, g * 512:(g + 1) * 512].rearrange("(k p) c -> p k c", p=128)),
              writes=[wbuf])
        for cc in range(4):
            j = g * 4 + cc
            dst = modps[j // 24]
            dbuf = psb[1 + j // 24]
            jj = j % 24
            for k in range(KC):
                P.op("pe", "matmul", dict(out=dst[:, jj * NR:(jj + 1) * NR], lhsT=wb[:, k, cc * 128:(cc + 1) * 128],
                                          rhs=csT[:, k, :], start=(k == 0), stop=(k == KC - 1)),
                     reads=[wbuf, bf("csT")], writes=[dbuf])
    for half in range(2):
        P.op("dve", "tensor_tensor", dict(
            out=modsb[:, half * 24:(half + 1) * 24, :],
            in0=modps[half][:, 0:24 * NR].rearrange("p (j r) -> p j r", r=NR),
            in1=badaT[:, half * 24:(half + 1) * 24].unsqueeze(2).to_broadcast([128, 24, NR]), op=ALU.add),
            reads=[psb[1 + half], bf("badaT")], writes=[bf("modsb%d" % half)])
    P.op("dve", "tensor_copy", dict(out=sh[:], in_=modsb[:, 0:16, :]), reads=[bf("modsb0")], writes=[bf("sh")])
    P.op("dve", "tensor_scalar", dict(out=gm[:], in0=modsb[:, 16:32, :], scalar1=1.0, scalar2=None, op0=ALU.add),
         reads=[bf("modsb0"), bf("modsb1")], writes=[bf("gm")])
    P.op("dve", "tensor_tensor", dict(out=gm[:], in0=gm[:], in1=ngTs[:].unsqueeze(2).to_broadcast([128, KC, NR]),
                                      op=ALU.mult), reads=[bf("gm"), bf("ngTs")], writes=[bf("gm")])

    if stop == "p0":
        return done()
    P.barrier()
    es0.close()
    if do_mix:
        trin = sb1("trin_s", [128, 128], BF16)
        tric = sb1("tric_s", [128, 128], BF16)
        maskp = sb1("maskp_s", [128, 4, 512], BF16)
        masks = sb1("masks_s", [128, 2, 64], BF16)
        hmask = sb1("hmask_s", [128, 128], F32)
        resetm = sb1("resetm_s", [128, 512], F32)
        lbr = sb1("lbr_s", [128, 4], F32)
        ogain = sb1("ogain_s", [128, 1], F32)
        oneb = sb1("oneb", [128, 1], F32)
        zs_sb = sb1("zs_sb", [128, CH], F32)
        e_sb = [sq_b, sb1("e_sb1", [128, CH], BF16), sb1("e_sb2", [128, CH], BF16)]
        sp_sb = [sb1("sp_sb%d" % i, [128, CH], BF16) for i in range(3)]
        x_sb = [sb1("x_sb%d" % i, [128, CH], BF16) for i in range(2)]
        w_sb = [sb1("w_sb%d" % i, [128, CH], BF16) for i in range(2)]
        bT = sb1("bT", [128, CH], BF16)
        aT = bT
        ksT = sb1("ksT", [128, CH], BF16)
        vs_bf = sb1("vs_bf", [128, 4, 128], BF16)
        t_f = sb1("t_f", [128, CH], F32)
        t_k = sb1("t_k", [128, CH], F32)
        t_lf = sb1("t_lf", [128, CH], F32)
        t_b = sb1("t_b", [128, CH], F32)
        t_eb = sb1("t_eb", [128, CH], F32)
        t_enb = sb1("t_enb", [128, CH], F32)
        t_qs = sb1("t_qs", [128, CH], F32)
        t_ke = sb1("t_ke", [128, CH], F32)
        kv_o = [t_ke[:].rearrange("p (t d) -> p t d", d=128)]
        t_iT = sb1("t_iT", [128, CH], F32)
        t_zh = sb1("t_zh", [128, CH], F32)
        dec = sb1("dec", [128, 8], F32)
        iv = sb1("iv", [128, 4, 128], F32)
        ket = sb1("ket", [128, 2, 4, 128], F32)
        pmask = sb1("pmask_s", [128, 2], F32)
        if mode == "fused":
            vis = sb1("vis_s", [128, 512], F32)
            idx_s = sb1("idx_s_s", [128, 4], mybir.dt.int32)
        attm = sb1("attm", [128, 4, 128], F32)
        Sset = [sb1("Sset%d" % i, [128, 8, 128], F32) for i in range(2)]
        sin = sb1("sin", [128, 8, 128], F32)
        dbg_t = t_lf
        P.dma("pool", dict(out=trin[:], in_=trin_d[:, :]), writes=[bf("trin")])
        P.dma("pool", dict(out=tric[:], in_=tric_d[:, :]), writes=[bf("tric")])
        P.dma("pool", dict(out=maskp[:], in_=maskp_d[:, :, :]), writes=[bf("maskp")])
        P.dma("pool", dict(out=masks[:], in_=masks_d[:, :, :]), writes=[bf("masks")])
        P.dma("sp", dict(out=hmask[:], in_=hmask_d[:, :]), writes=[bf("hmask")])
        P.dma("sp", dict(out=resetm[:], in_=resetm_d[:, :]), writes=[bf("resetm")])
        P.dma("sp", dict(out=pmask[:], in_=pmask_d[:, :]), writes=[bf("pmask")])
        if mode == "fused":
            P.dma("sp", dict(out=vis[:], in_=vis_d[:, :]), writes=[bf("vis")])
            P.dma("sp", dict(out=idx_s[:], in_=idx_s_d[:, :]), writes=[bf("idx_s")])
        P.op("dve", "memset", dict(ap=oneb[:], constant=1.0), writes=[bf("oneb")])

        def head_setup():
            P.barrier()
            P.dma("pool", dict(out=wh[:], in_=w_head[HD["h"]].rearrange("(k p) c -> p k c", p=128)), writes=[bf("wh")])
            P.dma("sp", dict(out=lbr[:, 0:2], in_=lbr_d[HD["h"]]), writes=[bf("lbr")])
            P.dma("sp", dict(out=ogain[:], in_=ogain_d[HD["h"]]), writes=[bf("ogain")])
            P.op("dve", "memset", dict(ap=Sset[(mixers.n + 1) % 2][:, 7, :], constant=0.0), writes=[bf("S_%d_7" % ((mixers.n + 1) % 2))])
            P.op("dve", "tensor_tensor", dict(out=lbr[:, 2:3], in0=lbr[:, 0:1], in1=lbr[:, 1:2], op=ALU.subtract),
                 reads=[bf("lbr")], writes=[bf("lbr")])
            P.op("act", "activation", dict(out=lbr[:, 2:3], in_=lbr[:, 2:3], func=AF.Sigmoid), reads=[bf("lbr")], writes=[bf("lbr")])
            P.op("dve", "tensor_scalar", dict(out=lbr[:, 3:4], in0=lbr[:, 2:3], scalar1=-1.0, scalar2=1.0, op0=ALU.mult, op1=ALU.add),
                 reads=[bf("lbr")], writes=[bf("lbr")])
            sample_ready["done"] = False
            if mode == "fused":
                P.dma("sp", dict(out=S_scr.ap()[0:128, :], in_=Sset[(mixers.n + 1) % 2][:, 7, :]),
                      reads=[bf("S_%d_7" % ((mixers.n + 1) % 2))])
    state = {"tile_i": 0, "tro": 0}

    def proj(j, dst, dbuf, hTb):
        for k in range(KC):
            P.op("pe", "matmul", dict(out=dst[:, :], lhsT=wh[:, k, j * 128:(j + 1) * 128], rhs=hT[:, k, :],
                                      start=(k == 0), stop=(k == KC - 1)),
                 reads=[bf("wh")] + hTb[k], writes=[dbuf])

    def headnorm(src, sbuf_, gain_ap, gbufs, out_ap, outbuf):
        P.op("act", "activation", dict(out=sq_b[:], in_=src[:, :], func=AF.Square), reads=[sbuf_], writes=[bf("e_sb0")])
        P.op("pe", "matmul", dict(out=ps[4][:, :], lhsT=ones_b[:], rhs=sq_b[:], start=True, stop=True),
             reads=[bf("ones_b"), bf("e_sb0")], writes=[psb[4]])
        P.op("act", "activation", dict(out=rstd_f[:], in_=ps[4][:, :], func=AF.Ln, scale=1.0 / 128, bias=epsb[:, 0:1]),
             reads=[psb[4], bf("epsb")], writes=[bf("rstd_f")])
        P.op("act", "activation", dict(out=rstd_f[:], in_=rstd_f[:], func=AF.Exp, scale=-0.5),
             reads=[bf("rstd_f")], writes=[bf("rstd_f")])
        P.op("dve", "scalar_tensor_tensor", dict(out=out_ap, in0=src[:, :], scalar=gain_ap, in1=rstd_f[:],
                                                 op0=ALU.mult, op1=ALU.mult),
             reads=[sbuf_, bf("rstd_f")] + gbufs, writes=[outbuf])

    def attention(nq, q_ap, qbufs, blocks, o_ap, c0):
        n = len(blocks)
        zb = [ps[0], ps[1]]
        A = ps[6]

        def stage1(j):
            blk = blocks[j]
            z = zb[j % 2]
            zbuf = psb[j % 2]
            eb_, ebuf = e_sb[j % 3], bf("e_sb%d" % (j % 3))
            sb_, sbuf_ = sp_sb[j % 3], bf("sp_sb%d" % (j % 3))
            has_mask = blk.get("mask") is not None
            P.op("pe", "matmul", dict(out=z[:, 0:nq], lhsT=blk["kT"], rhs=q_ap, start=True, stop=not has_mask),
                 reads=blk["kbufs"] + qbufs, writes=[zbuf])
            if has_mask:
                P.op("pe", "matmul", dict(out=z[:, 0:nq], lhsT=ident_b[:], rhs=blk["mask"], start=False, stop=True),
                     reads=[bf("ident_b")] + blk["mbufs"], writes=[zbuf])
            if blk.get("bias") is not None:
                P.op("act", "activation", dict(out=eb_[:, 0:nq], in_=z[:, 0:nq], func=AF.Exp, bias=blk["bias"]),
                     reads=[zbuf, bf("vis")], writes=[ebuf])
            else:
                P.op("act", "activation", dict(out=eb_[:, 0:nq], in_=z[:, 0:nq], func=AF.Exp), reads=[zbuf], writes=[ebuf])
            P.op("act", "activation", dict(out=sb_[:, 0:nq], in_=eb_[:, 0:nq], func=AF.Ln, bias=oneb[:, 0:1]),
                 reads=[ebuf, bf("oneb")], writes=[sbuf_])

        def o_mm(j):
            blk = blocks[j]
            P.op("pe", "matmul", dict(out=o_ap, lhsT=blk["v"], rhs=w_sb[j % 2][:, 0:nq], start=(j == 0), stop=(j == n - 1)),
                 reads=blk["vbufs"] + [bf("w_sb%d" % (j % 2))], writes=[psb[7]])

        stage1(0)
        if n > 1:
            stage1(1)
        for j in range(n):
            spj, spbuf = sp_sb[j % 3], bf("sp_sb%d" % (j % 3))
            P.op("pe", "matmul", dict(out=A[:, 0:nq], lhsT=trin[:], rhs=spj[:, 0:nq], start=(j == 0), stop=True, skip_group_check=(j > 0)),
                 reads=[bf("trin"), spbuf], writes=[psb[6]])
            P.op("act", "activation", dict(out=x_sb[j % 2][:, 0:nq], in_=A[:, 0:nq], func=AF.Exp),
                 reads=[psb[6]], writes=[bf("x_sb%d" % (j % 2))])
            if j + 2 < n:
                stage1(j + 2)
            if j >= 1:
                o_mm(j - 1)
            if j < n - 1:
                P.op("pe", "matmul", dict(out=A[:, 0:nq], lhsT=tric[:], rhs=spj[:, 0:nq], start=False, stop=True, skip_group_check=True),
                     reads=[bf("tric"), spbuf], writes=[psb[6]])
            P.op("dve", "tensor_tensor", dict(out=w_sb[j % 2][:, 0:nq], in0=e_sb[j % 3][:, 0:nq], in1=x_sb[j % 2][:, 0:nq], op=ALU.mult),
                 reads=[bf("e_sb%d" % (j % 3)), bf("x_sb%d" % (j % 2))], writes=[bf("w_sb%d" % (j % 2))])
        o_mm(n - 1)

    def tr_out(ch, srcT, srcbuf, dram, extra_copy=None):
        pt = ps[5]
        for tt in range(4):
            P.op("pe", "transpose", dict(out=pt[:, tt * 128:(tt + 1) * 128], in_=srcT[:, tt * 128:(tt + 1) * 128],
                                         identity=ident_f[:]),
                 reads=[srcbuf, bf("ident_f")], writes=[psb[5]])
        i = 0
        ko = kv_o[i]
        kob = bf("t_ke")
        P.op("act", "activation", dict(out=ko[:].rearrange("p t d -> p (t d)"), in_=pt[:, :], func=AF.Copy),
             reads=[psb[5]], writes=[kob])
        if extra_copy is not None:
            extra_copy(pt, psb[5])
        P.dma("sp", dict(out=dram[ch * CH:(ch + 1) * CH, :].rearrange("(t p) d -> p t d", p=128), in_=ko[:]),
              reads=[kob])

    sample_ready = {"done": False}

    def emit_ab(ch, which, src_bf, srcbuf):
        if True:
            return
        r0 = ((which * 8 + HD["h"]) * 136 + ch * 4) * 128
        P.dma("sp", dict(out=ab_scr.ap()[r0:r0 + 512, :].rearrange("(t p) c -> p t c", p=128),
                         in_=src_bf[:].rearrange("p (t c) -> p t c", c=128)), reads=[srcbuf])

    def load_cache(b, src_k=None, src_v=None):
        i = b % 2
        kc_raw = KT[:, 8192 + i * 4096: 8192 + (i + 1) * 4096].rearrange("p (j d) -> p j d", d=128)
        vc = Vr[:, i * 32:(i + 1) * 32, :]
        kcT = KT[:, i * 4096:(i + 1) * 4096]
        P.dma("pool", dict(out=kc_raw, in_=(cache_k[HD["h"], b] if src_k is None else src_k).rearrange("(j p) d -> p j d", p=128)), writes=[bf("kc_raw%d" % i)])
        P.dma("pool", dict(out=vc, in_=(cache_v[HD["h"], b] if src_v is None else src_v).rearrange("(j p) d -> p j d", p=128)), writes=[bf("vc%d" % i)])
        for g in range(4):
            pb = 2 + (g % 2)
            tp = ps[pb][:, :].bitcast(BF16)
            for jj in range(8):
                j = g * 8 + jj
                P.op("pe", "transpose", dict(out=tp[:, jj * 128:(jj + 1) * 128], in_=kc_raw[:, j, :], identity=ident_b[:]),
                     reads=[bf("kc_raw%d" % i), bf("ident_b")], writes=[psb[pb]])
            if g % 2 == 0:
                P.op("act", "activation", dict(out=kcT[:, g * 1024:(g + 1) * 1024], in_=tp[:, :], func=AF.Copy),
                     reads=[psb[pb]], writes=[bf("kcT%d_%d" % (i, g))])
            else:
                P.op("dve", "tensor_copy", dict(out=kcT[:, g * 1024:(g + 1) * 1024], in_=tp[:, :]),
                     reads=[psb[pb]], writes=[bf("kcT%d_%d" % (i, g))])

    def mixers(ch, hTb, own=None):
        n_local = mixers.n
        if own is None:
            mixers.n += 1
        is_p = ch < NCH_P
        lite = (mode == "fused" and own is None)
        if mode != "fused":
            proj(3, ps[3], psb[3], hTb)
            P.op("act", "activation", dict(out=zs_sb[:], in_=ps[3][:, :], func=AF.Silu), reads=[psb[3]], writes=[bf("zs_sb")])
        if "att" in flags and mode != "fused":
            if is_p:
                blocks = []
                for j in range(4 * ch + 3, -1, -1):
                    blk = dict(kT=KT[:, j * 128:(j + 1) * 128], kbufs=[bf("KT%d" % (j // 4))],
                               v=Vr[:, j, :], vbufs=[bf("Vr%d" % (j // 4))])
                    if j >= 4 * ch:
                        blk["mask"] = maskp[:, j - 4 * ch, :]
                        blk["mbufs"] = [bf("maskp")]
                    blocks.append(blk)
                attention(CH, QT[:], [bf("QT")], blocks, ps[7][:, :], 0)
            else:
                if not sample_ready["done"]:
                    sample_ready["done"] = True
                    P.barrier()
                    load_cache((ch - NCH_P) * 8)
                for c in range(8):
                    b = (ch - NCH_P) * 8 + c
                    i = b % 2
                    if b + 1 < NB:
                        load_cache(b + 1)
                    tt, par = c // 2, c % 2
                    kcT = KT[:, i * 4096:(i + 1) * 4096]
                    vc = Vr[:, i * 32:(i + 1) * 32, :]
                    blocks = [dict(kT=ksT[:, tt * 128:(tt + 1) * 128], kbufs=[bf("ksT")], v=vs_bf[:, tt, :], vbufs=[bf("vs_bf")],
                                   mask=masks[:, par, :], mbufs=[bf("masks")])]
                    for j in range(31, -1, -1):
                        blocks.append(dict(kT=kcT[:, j * 128:(j + 1) * 128], kbufs=[bf("kcT%d_%d" % (i, j // 8))],
                                           v=vc[:, j, :], vbufs=[bf("vc%d" % i)]))
                    attention(TSQ, QT[:, c * TSQ:(c + 1) * TSQ], [bf("QT")], blocks, ps[7][:, c * TSQ:(c + 1) * TSQ], c * TSQ)
            P.op("dve", "tensor_tensor", dict(out=aT[:], in0=ps[7][:, :], in1=zs_sb[:], op=ALU.mult),
                 reads=[psb[7], bf("zs_sb")], writes=[bf("bT")])
            emit_ab(ch, 0, aT, bf("bT"))
            if mode == "mix":
                P.op("dve", "tensor_copy", dict(out=dbg_t[:], in_=aT[:]), reads=[bf("bT")], writes=[bf("t_lf")])
                P.dma("sp", dict(out=dbg_a[:, ch * CH:(ch + 1) * CH], in_=dbg_t[:]), reads=[bf("t_lf")])
        if "hg" not in flags:
            return
        if not is_p:
            if own is None:
                b0_ = (ch - NCH_P) * 8
                P.dma("sp", dict(out=sin[:], in_=s_in[HD["h"], b0_:b0_ + 8].rearrange("b k v -> k b v")), writes=[bf("sin")])
            else:
                P.dma("sp", dict(out=sin[:, 0:2, :], in_=s_in_own[HD["h"]].rearrange("b k v -> k b v")), writes=[bf("sin")])
        pf = 7 if lite else 2
        pi_ = 6 if lite else 3
        proj(4, ps[pf], psb[pf], hTb)
        P.op("act", "activation", dict(out=t_f[:], in_=ps[pf][:, :], func=AF.Sigmoid), reads=[psb[pf]], writes=[bf("t_f")])
        if not lite:
            proj(6, ps[3], psb[3], hTb)
            P.op("act", "activation", dict(out=t_qs[:], in_=ps[3][:, :], func=AF.Silu), reads=[psb[3]], writes=[bf("t_qs")])
            proj(7, ps[2], psb[2], hTb)
            P.op("act", "activation", dict(out=t_zh[:], in_=ps[2][:, :], func=AF.Silu), reads=[psb[2]], writes=[bf("t_zh")])
        proj(5, ps[pi_], psb[pi_], hTb)
        P.op("act", "activation", dict(out=t_iT[:], in_=ps[pi_][:, :], func=AF.Copy), reads=[psb[pi_]], writes=[bf("t_iT")])
        P.op("dve", "tensor_scalar", dict(out=t_f[:], in0=t_f[:], scalar1=lbr[:, 3:4], scalar2=lbr[:, 2:3], op0=ALU.mult, op1=ALU.add),
             reads=[bf("t_f"), bf("lbr")], writes=[bf("t_f")])
        P.op("act", "activation", dict(out=t_lf[:], in_=t_f[:], func=AF.Ln), reads=[bf("t_f")], writes=[bf("t_lf")])
        P.op("dve", "tensor_scalar", dict(out=t_k[:], in0=t_f[:], scalar1=-1.0, scalar2=1.0, op0=ALU.mult, op1=ALU.add),
             reads=[bf("t_f")], writes=[bf("t_k")])
        P.op("dve", "tensor_tensor_scan", dict(out=t_b[:], data0=resetm[:], data1=t_lf[:], initial=0.0, op0=ALU.mult, op1=ALU.add),
             reads=[bf("resetm"), bf("t_lf")], writes=[bf("t_b")])
        if not lite:
            P.op("act", "activation", dict(out=t_eb[:], in_=t_b[:], func=AF.Exp), reads=[bf("t_b")], writes=[bf("t_eb")])
        P.op("act", "activation", dict(out=t_enb[:], in_=t_b[:], func=AF.Exp, scale=-1.0), reads=[bf("t_b")], writes=[bf("t_enb")])
        P.op("act", "activation", dict(out=dec[:].unsqueeze(2), in_=t_b[:].rearrange("p (c t) -> p c t", t=64)[:, :, 63:64], func=AF.Exp),
             reads=[bf("t_b")], writes=[bf("dec")])
        if not lite:
            P.op("dve", "tensor_tensor", dict(out=t_eb[:], in0=t_qs[:], in1=t_eb[:], op=ALU.mult),
                 reads=[bf("t_qs"), bf("t_eb")], writes=[bf("t_eb")])
        P.op("dve", "tensor_tensor", dict(out=t_enb[:], in0=t_k[:], in1=t_enb[:], op=ALU.mult),
             reads=[bf("t_k"), bf("t_enb")], writes=[bf("t_enb")])
        P.op("dve", "tensor_tensor", dict(out=t_ke[:].rearrange("p (c t) -> p c t", t=64),
                                          in0=t_enb[:].rearrange("p (c t) -> p c t", t=64),
                                          in1=dec[:].unsqueeze(2).to_broadcast([128, 8, 64]), op=ALU.mult),
             reads=[bf("t_enb"), bf("dec")], writes=[bf("t_ke")])
        hgs = [int(f[3:]) for f in flags if f.startswith("hgs")]
        hgs = hgs[0] if hgs else 99
        if hgs <= 1:
            return
        for which in range(2):
            src, sbuf_ = (t_iT, bf("t_iT")) if which == 0 else (t_ke, bf("t_ke"))
            for tt in range(4):
                P.op("pe", "transpose", dict(out=ps[5][:, tt * 128:(tt + 1) * 128], in_=src[:, tt * 128:(tt + 1) * 128], identity=ident_f[:]),
                     reads=[sbuf_, bf("ident_f")], writes=[psb[5]])
            if which == 0:
                P.op("act", "activation", dict(out=iv[:].rearrange("p t d -> p (t d)"), in_=ps[5][:, :], func=AF.Copy),
                     reads=[psb[5]], writes=[bf("iv")])
            else:
                for ab in range(2):
                    P.op("act", "activation", dict(out=ket[:, ab, :, :].rearrange("p t d -> p (t d)"), in_=ps[5][:, :], func=AF.Copy,
                                                   scale=pmask[:, ab:ab + 1]),
                         reads=[psb[5], bf("pmask")], writes=[bf("ket%d" % ab)])
        if hgs <= 2:
            return
        for c in range(8):
            pb = c // 4
            r0 = (c % 2) * 64
            P.op("pe", "matmul", dict(out=ps[pb][:, (c % 4) * 128:(c % 4 + 1) * 128], lhsT=ket[:, c % 2, c // 2, :],
                                      rhs=iv[:, c // 2, :], start=True, stop=True),
                 reads=[bf("ket%d" % (c % 2)), bf("iv")], writes=[psb[pb]])
        if hgs <= 3:
            return
        if not lite:
            for p_ in range(4):
                P.op("pe", "matmul", dict(out=ps[4][:, p_ * 128:(p_ + 1) * 128], lhsT=t_enb[:, p_ * 128:(p_ + 1) * 128],
                                          rhs=t_eb[:, p_ * 128:(p_ + 1) * 128], start=True, stop=True),
                     reads=[bf("t_enb"), bf("t_eb")], writes=[psb[4]])
            P.op("dve", "tensor_tensor", dict(out=attm[:], in0=ps[4][:, :].rearrange("p (a t) -> p a t", t=128),
                                              in1=hmask[:].unsqueeze(1).to_broadcast([128, 4, 128]), op=ALU.mult),
                 reads=[psb[4], bf("hmask")], writes=[bf("attm")])
        if hgs <= 4:
            return
        cur = Sset[n_local % 2]
        prev_set = Sset[(n_local + 1) % 2]
        Sprev = []
        for c in range(8):
            if is_p:
                if c == 0:
                    sp_ap, sp_buf = prev_set[:, 7, :], bf("S_%d_7" % ((n_local + 1) % 2))
                else:
                    sp_ap, sp_buf = cur[:, c - 1, :], bf("S_%d_%d" % (n_local % 2, c - 1))
            elif own is None:
                sp_ap, sp_buf = sin[:, c, :], bf("sin")
            else:
                sp_ap, sp_buf = sin[:, min(c, 1), :], bf("sin")
            Sprev.append((sp_ap, sp_buf))
            pb = c // 4
            P.op("dve", "scalar_tensor_tensor", dict(out=cur[:, c, :], in0=sp_ap, scalar=dec[:, c:c + 1],
                                                     in1=ps[pb][:, (c % 4) * 128:(c % 4 + 1) * 128], op0=ALU.mult, op1=ALU.add),
                 reads=[sp_buf, bf("dec"), psb[pb]], writes=[bf("S_%d_%d" % (n_local % 2, c))])
        mixers.first = False
        if lite:
            if is_p:
                P.dma("sp", dict(out=S_scr.ap()[(ch + 1) * 128:(ch + 2) * 128, :], in_=cur[:, 7, :]),
                      reads=[bf("S_%d_7" % (n_local % 2))])
                if ch == NCH_P - 1:
                    P.dma("sp", dict(out=s_out[HD["h"], 0], in_=cur[:, 7, :]), reads=[bf("S_%d_7" % (n_local % 2))])
            else:
                b0 = (ch - NCH_P) * 8
                P.dma("sp", dict(out=s_out[HD["h"], 1 + b0:1 + b0 + 8].rearrange("b k v -> k b v"), in_=cur[:]),
                      reads=[bf("S_%d_%d" % (n_local % 2, c)) for c in range(8)])
            return
        if hgs <= 5:
            return
        for p_ in range(4):
            P.op("pe", "matmul", dict(out=ps[6][:, p_ * 128:(p_ + 1) * 128], lhsT=iv[:, p_, :], rhs=attm[:, p_, :], start=True, stop=False),
                 reads=[bf("iv"), bf("attm")], writes=[psb[6]])
            for h2 in range(2):
                c = 2 * p_ + h2
                sp_ap, sp_buf = Sprev[c]
                P.op("pe", "matmul", dict(out=ps[6][:, c * 64:(c + 1) * 64], lhsT=sp_ap, rhs=t_eb[:, c * 64:(c + 1) * 64],
                                          start=False, stop=(h2 == 1)),
                     reads=[sp_buf, bf("t_eb")], writes=[psb[6]])
        if hgs <= 6:
            return
        headnorm(ps[6], psb[6], ogain[:, 0:1], [bf("ogain")], t_ke[:], bf("t_ke"))
        P.op("dve", "tensor_tensor", dict(out=bT[:], in0=t_ke[:], in1=t_zh[:], op=ALU.mult),
             reads=[bf("t_ke"), bf("t_zh")], writes=[bf("bT")])
        if own is not None:
            nt_ = 4 if is_p else 1
            P.dma("sp", dict(out=b_own_scr.ap()[HD["h"], own["slot"] * 4:own["slot"] * 4 + nt_].rearrange("t p c -> p t c"),
                             in_=bT[:, 0:nt_ * 128].rearrange("p (t c) -> p t c", c=128)), reads=[bf("bT")])
            return
        emit_ab(ch, 1, bT, bf("bT"))
        if mode == "mix":
            P.op("dve", "tensor_copy", dict(out=dbg_t[:], in_=bT[:]), reads=[bf("bT")], writes=[bf("t_lf")])
            P.dma("sp", dict(out=dbg_b[:, ch * CH:(ch + 1) * CH], in_=dbg_t[:]), reads=[bf("t_lf")])
        if is_p:
            if ch == NCH_P - 1 or (debug and ch == chunks[-1]):
                P.dma("sp", dict(out=s_out[HD["h"], 0], in_=cur[:, 7, :]), reads=[bf("S_%d_7" % (n_local % 2))])
        else:
            b0 = (ch - NCH_P) * 8
            P.dma("sp", dict(out=s_out[HD["h"], 1 + b0:1 + b0 + 8].rearrange("b k v -> k b v"), in_=cur[:]),
                  reads=[bf("S_%d_%d" % (n_local % 2, c)) for c in range(8)])
    mixers.n = 0
    mixers.first = True

    def build_hT(src_dram, row0, ntiles, segs_fn):
        for tt in range(ntiles):
            t0 = row0 + tt * 128
            i = state["tile_i"] % 2
            state["tile_i"] += 1
            xb = xbuf[i]
            xbb = bf("xbuf%d" % i)
            st = stat[i]
            stb = bf("stat%d" % i)
            P.dma("sp", dict(out=xb[:], in_=src_dram[t0:t0 + 128, :]), writes=[xbb])
            P.op("act", "activation", dict(out=junk[:], in_=xb[:], func=AF.Square, accum_out=st[:, 0:1]),
                 reads=[xbb], writes=[bf("xn"), stb])
            P.op("act", "activation", dict(out=st[:, 1:2], in_=st[:, 0:1], func=AF.Ln, scale=1.0 / D, bias=epsb[:, 0:1]),
                 reads=[stb, bf("epsb")], writes=[stb])
            P.op("act", "activation", dict(out=st[:, 2:3], in_=st[:, 1:2], func=AF.Exp, scale=-0.5),
                 reads=[stb], writes=[stb])
            P.op("dve", "tensor_scalar", dict(out=xn[:], in0=xb[:], scalar1=st[:, 2:3], scalar2=None, op0=ALU.mult),
                 reads=[xbb, stb], writes=[bf("xn")])
            for half in range(2):
                tp = ps[half][:, :].bitcast(BF16)
                for kk in range(8):
                    k = half * 8 + kk
                    P.op("pe", "transpose", dict(out=tp[:, kk * 128:(kk + 1) * 128], in_=xn[:, k * 128:(k + 1) * 128],
                                                 identity=ident_b[:]),
                         reads=[bf("xn"), bf("ident_b")], writes=[psb[half]])
                for kk in range(8):
                    k = half * 8 + kk
                    segs = segs_fn(tt)
                    for (c0, c1, r) in segs:
                        o_ap = hT[:, k, tt * 128 + c0:tt * 128 + c1]
                        i_ap = tp[:, kk * 128 + c0:kk * 128 + c1]
                        if len(segs) == 1:
                            wr = [bf("hT_%d_%d_0" % (k, tt)), bf("hT_%d_%d_64" % (k, tt))]
                        else:
                            wr = [bf("hT_%d_%d_%d" % (k, tt, c0))]
                        rd = [psb[half], bf("gm"), bf("sh")]
                        if half == 0:
                            P.op("act", "activation", dict(out=o_ap, in_=i_ap, func=AF.Identity,
                                                           scale=gm[:, k, r:r + 1], bias=sh[:, k, r:r + 1]), reads=rd, writes=wr)
                        else:
                            P.op("dve", "tensor_scalar", dict(out=o_ap, in0=i_ap, scalar1=gm[:, k, r:r + 1],
                                                              scalar2=sh[:, k, r:r + 1], op0=ALU.mult, op1=ALU.add),
                                 reads=rd, writes=wr)
        hTb = {}
        for k in range(KC):
            lst = []
            for tt in range(ntiles):
                lst.append(bf("hT_%d_%d_0" % (k, tt)))
                lst.append(bf("hT_%d_%d_64" % (k, tt)))
            hTb[k] = lst
        return hTb

    def own_slots():
        P.barrier()
        for s_ in range(5):
            is_p = s_ < 4
            nt = 4 if is_p else 1
            N = nt * 128
            if is_p:
                segs_fn = lambda tt: [(0, 128, 0)]
            else:
                segs_fn = lambda tt: [(0, 64, RS0), (64, 128, RS0 + 1)]
            allb = [bf("hT_%d_%d_%d" % (k, tt, c0_)) for k in range(KC) for tt in range(4) for c0_ in (0, 64)]
            if HD["h"] > 0:
                hTb = {k: [bf("hT_%d_%d_0" % (k, tt)) for tt in range(nt)] + [bf("hT_%d_%d_64" % (k, tt)) for tt in range(nt)]
                       for k in range(KC)}
                P.dma("pool", dict(out=hT[:].rearrange("p k c -> p (k c)"), in_=hTo_scr.ap()[s_]), writes=allb)
            else:
                hTb = build_hT(x_own, s_ * CH, nt, segs_fn)
                P.dma("sp", dict(out=hTo_scr.ap()[s_], in_=hT[:].rearrange("p k c -> p (k c)")), reads=allb)
            proj(0, ps[2], psb[2], hTb)
            headnorm(ps[2], psb[2], qkgs2[:, 0:1], [bf("qkgs2a")], QT[:], bf("QT"))
            proj(1, ps[3], psb[3], hTb)
            headnorm(ps[3], psb[3], qkgs2[:, 1:2], [bf("qkgs2b")], kn_f[:], bf("kn_f"))
            P.op("act", "activation", dict(out=ksT[:], in_=kn_f[:], func=AF.Copy), reads=[bf("kn_f")], writes=[bf("ksT")])
            proj(2, ps[2], psb[2], hTb)
            P.op("dve", "tensor_copy", dict(out=vT_f[:], in_=ps[2][:, :]), reads=[psb[2]], writes=[bf("vT_f")])
            for tt in range(nt):
                P.op("pe", "transpose", dict(out=ps[5][:, tt * 128:(tt + 1) * 128], in_=vT_f[:, tt * 128:(tt + 1) * 128],
                                             identity=ident_f[:]),
                     reads=[bf("vT_f"), bf("ident_f")], writes=[psb[5]])
            P.op("dve", "tensor_copy", dict(out=vs_bf[:, 0:nt, :].rearrange("p t d -> p (t d)"), in_=ps[5][:, 0:N]),
                 reads=[psb[5]], writes=[bf("vs_bf")])
            proj(3, ps[3], psb[3], hTb)
            P.op("act", "activation", dict(out=zs_sb[:], in_=ps[3][:, :], func=AF.Silu), reads=[psb[3]], writes=[bf("zs_sb")])
            if is_p:
                blocks = []
                for i_ in range(3, -1, -1):
                    blocks.append(dict(kT=ksT[:, i_ * 128:(i_ + 1) * 128], kbufs=[bf("ksT")], v=vs_bf[:, i_, :], vbufs=[bf("vs_bf")],
                                       mask=maskp[:, i_, :], mbufs=[bf("maskp")]))
                for j in range((128 if _CONTIG else 32 * (s_ + 1)) - 1, -1, -1):
                    blocks.append(dict(kT=KT[:, j * 128:(j + 1) * 128], kbufs=[bf("KT%d" % (j // 4))],
                                       v=Vr[:, j, :], vbufs=[bf("Vr%d" % (j // 4))], bias=vis[:, s_ * 128 + j:s_ * 128 + j + 1]))
                attention(CH, QT[:], [bf("QT")], blocks, ps[7][:, :], 0)
            else:
                P.barrier()
                load_cache(0, cache_k[HD["h"], 0], cache_v[HD["h"], 0])
                for par in range(2):
                    if par == 0:
                        load_cache(1, cache_k[HD["h"], 1], cache_v[HD["h"], 1])
                    i = par
                    kcT = KT[:, i * 4096:(i + 1) * 4096]
                    vc = Vr[:, i * 32:(i + 1) * 32, :]
                    blocks = [dict(kT=ksT[:, 0:128], kbufs=[bf("ksT")], v=vs_bf[:, 0, :], vbufs=[bf("vs_bf")],
                                   mask=masks[:, par, :], mbufs=[bf("masks")])]
                    for j in range(31, -1, -1):
                        blocks.append(dict(kT=kcT[:, j * 128:(j + 1) * 128], kbufs=[bf("kcT%d_%d" % (i, j // 8))],
                                           v=vc[:, j, :], vbufs=[bf("vc%d" % i)]))
                    attention(TSQ, QT[:, par * TSQ:(par + 1) * TSQ], [bf("QT")], blocks, ps[7][:, par * TSQ:(par + 1) * TSQ], 0)
            P.op("dve", "tensor_tensor", dict(out=aT[:, 0:N], in0=ps[7][:, 0:N], in1=zs_sb[:, 0:N], op=ALU.mult),
                 reads=[psb[7], bf("zs_sb")], writes=[bf("bT")])
            P.dma("sp", dict(out=a_own_scr.ap()[HD["h"], s_ * 4:s_ * 4 + nt].rearrange("t p c -> p t c"),
                             in_=aT[:, 0:N].rearrange("p (t c) -> p t c", c=128)), reads=[bf("bT")])
            if is_p:
                pi = (mixers.n + 1) % 2
                P.dma("pool", dict(out=Sset[pi][:, 7, :], out_offset=None, in_=S_scr.ap(),
                                   in_offset=bass.IndirectOffsetOnAxis(ap=idx_s[:, s_:s_ + 1], axis=0)),
                      reads=[bf("idx_s")], writes=[bf("S_%d_7" % pi)], meth="indirect_dma_start")
            mixers(0 if is_p else NCH_P, hTb, own=dict(slot=s_))

    if mode == "fused" and "zero_scr" in flags:
        P.op("dve", "memset", dict(ap=aT[:], constant=0.0), writes=[bf("bT")])
        P.op("dve", "memset", dict(ap=Sset[0][:, 7, :], constant=0.0), writes=[bf("S_0_7")])
        for e_ in range(33):
            P.dma("sp", dict(out=S_scr.ap()[e_ * 128:(e_ + 1) * 128, :], in_=Sset[0][:, 7, :]), reads=[bf("S_0_7")])
        P.op("dve", "memset", dict(ap=KT[:], constant=0.0), writes=[bf("KT%d" % i_) for i_ in range(32)])
        P.op("dve", "memset", dict(ap=Vr[:].rearrange("p j d -> p (j d)"), constant=0.0), writes=[bf("Vr%d" % i_) for i_ in range(32)])
        P.barrier()
    for hd in range(NH if do_mix else 0):
        HD["h"] = hd
        head_setup()
        for ch in chunks:
            if ch < NCH_P:
                segs_fn = lambda tt: [(0, 128, 0)]
            else:
                segs_fn = (lambda ch: lambda tt: [(0, 64, 1 + ((ch - NCH_P) * 4 + tt) * 2), (64, 128, 2 + ((ch - NCH_P) * 4 + tt) * 2)])(ch)
            if mode == "fused" and hd > 0:
                hTb = {k: [bf("hT_%d_%d_0" % (k, tt)) for tt in range(4)] + [bf("hT_%d_%d_64" % (k, tt)) for tt in range(4)]
                       for k in range(KC)}
                P.dma("pool", dict(out=hT[:].rearrange("p k c -> p (k c)"), in_=hT_scr.ap()[ch]),
                      writes=[b_ for k in range(KC) for b_ in hTb[k]])
            else:
                hTb = build_hT(x_all, ch * CH - x_base, 4, segs_fn)
                if mode == "fused":
                    P.dma("sp", dict(out=hT_scr.ap()[ch], in_=hT[:].rearrange("p k c -> p (k c)")),
                          reads=[b_ for k in range(KC) for b_ in hTb[k]])

            if stop in ("hT", "ev_act", "ev_dve"):
                return done()
            if mode != "fused":
                proj(0, ps[2], psb[2], hTb)
                headnorm(ps[2], psb[2], qkgs2[:, 0:1], [bf("qkgs2a")], QT[:], bf("QT"))
            if stop == "q":
                return done()
            proj(1, ps[3], psb[3], hTb)
            headnorm(ps[3], psb[3], qkgs2[:, 1:2], [bf("qkgs2b")], kn_f[:], bf("kn_f"))
            if ch < NCH_P:
                P.op("act", "activation", dict(out=KT[:, ch * CH:(ch + 1) * CH], in_=kn_f[:], func=AF.Copy),
                     reads=[bf("kn_f")], writes=[bf("KT%d" % ch)])
            else:
                P.op("act", "activation", dict(out=ksT[:], in_=kn_f[:], func=AF.Copy), reads=[bf("kn_f")], writes=[bf("ksT")])
            tr_out(ch, kn_f, bf("kn_f"), k_out[HD["h"]])
            if stop == "k":
                return done()
            proj(2, ps[2], psb[2], hTb)
            P.op("dve", "tensor_copy", dict(out=vT_f[:], in_=ps[2][:, :]), reads=[psb[2]], writes=[bf("vT_f")])

            def vcopy(pt, ptb, ch=ch):
                if ch < NCH_P:
                    P.op("dve", "tensor_copy", dict(out=Vr[:, ch * 4:(ch + 1) * 4, :].rearrange("p t d -> p (t d)"), in_=pt[:, :]),
                         reads=[ptb], writes=[bf("Vr%d" % ch)])
                else:
                    P.op("dve", "tensor_copy", dict(out=vs_bf[:].rearrange("p t d -> p (t d)"), in_=pt[:, :]),
                         reads=[ptb], writes=[bf("vs_bf")])
            tr_out(ch, vT_f, bf("vT_f"), v_out[HD["h"]], extra_copy=vcopy)
            mixers(ch, hTb)
        if mode == "fused" and len(chunks) > 0:
            own_slots()

    if do_out:
        P.barrier()
        es1.close()
        gateP = sb("gateP", [128, D], F32)
        gateS = sb("gateS", [128, D], F32)
        es3 = ExitStack()
        csbc = [es3.enter_context(nc.sbuf_tensor("csbc%d" % i, [128, KC, 128], BF16)) for i in range(2)]
        bgate = es3.enter_context(nc.sbuf_tensor("bgate", [128, D], F32))
        wada2 = [es3.enter_context(nc.sbuf_tensor("wada2_%d" % i, [128, KC, 512], BF16)) for i in range(2)]
        P.dma("sp", dict(out=bgate[:], in_=b_gate_bc[:, :]), writes=[bf("bgate")])
        P.op("dve", "tensor_copy", dict(out=csbc[0][:], in_=csT[:, :, 0:1].to_broadcast([128, KC, 128])),
             reads=[bf("csT")], writes=[bf("csbc0")])
        P.op("dve", "tensor_copy", dict(out=csbc[1][:, :, 0:64], in_=csT[:, :, RS0:RS0 + 1].to_broadcast([128, KC, 64])),
             reads=[bf("csT")], writes=[bf("csbc1a")])
        P.op("dve", "tensor_copy", dict(out=csbc[1][:, :, 64:128], in_=csT[:, :, RS0 + 1:RS0 + 2].to_broadcast([128, KC, 64])),
             reads=[bf("csT")], writes=[bf("csbc1b")])
        for g in range(8, 12):
            wb = wada2[g % 2]
            wbuf = bf("wada2_%d" % (g % 2))
            P.dma("pool", dict(out=wb[:], in_=w_ada[:, g * 512:(g + 1) * 512].rearrange("(k p) c -> p k c", p=128)),
                  writes=[wbuf])
            for which in range(2):
                pb = 3 + which
                for k in range(KC):
                    P.op("pe", "matmul", dict(out=ps[pb][:, :], lhsT=csbc[which][:, k, :], rhs=wb[:, k, :],
                                              start=(k == 0), stop=(k == KC - 1)),
                         reads=[wbuf, bf("csbc0"), bf("csbc1a"), bf("csbc1b")], writes=[psb[pb]])
                gt = gateP if which == 0 else gateS
                P.op("dve", "tensor_tensor", dict(out=gt[:, (g - 8) * 512:(g - 7) * 512], in0=ps[pb][:, :],
                                                  in1=bgate[:, (g - 8) * 512:(g - 7) * 512], op=ALU.add),
                     reads=[psb[pb], bf("bgate")], writes=[bf("gate%d_%d" % (which, g - 8))])
        P.barrier()
        es3.close()
        wslot = [sb("wslot%d" % i, [128, 24576], BF16) for i in range(2)]
        abT = sb("abT", [128, 2, 8, CH], BF16)
        mT = sb("mT", [128, KC, CH], BF16)
        sgA = sb("sgA", [128, CH], F32)
        sgB = sb("sgB", [128, CH], F32)
        tmp1 = sb("tmp1", [128, CH], F32)
        tmp2 = sb("tmp2", [128, CH], F32)
        ysl = [sb("ysl%d" % i, [128, CH], F32) for i in range(2)]
        xsl = [sb("xsl%d" % i, [128, CH], F32) for i in range(2)]
        gi = 0
        yi = 0
        for o in range(5):
            N = CH if o < 4 else OWN_S
            NT = N // 128
            if o < 4:
                segs_fn = lambda tt: [(0, 128, 0)]
            else:
                segs_fn = lambda tt: [(0, 64, RS0), (64, 128, RS0 + 1)]
            hTb = build_hT(x_own, o * CH, NT, segs_fn)
            if mode == "out":
                for ab in range(2):
                    P.dma("pool", dict(out=abT[:, ab, :, 0:N], in_=ab_own[ab, :, :, o * CH:o * CH + N].rearrange("h p t -> p h t")),
                          writes=[bf("abT%d" % ab)])
                abbufs = {0: [bf("abT0")], 1: [bf("abT1")]}
            else:
                abbufs = {0: [], 1: []}
                for h in range(8):
                    P.dma("sp", dict(out=abT[:, 0, h, 0:N].rearrange("p (t c) -> p t c", c=128),
                                     in_=a_own_scr.ap()[h, o * 4:o * 4 + NT].rearrange("t p c -> p t c")),
                          writes=[bf("abT_a%d" % h)])
                    abbufs[0].append(bf("abT_a%d" % h))
                for h in range(8):
                    P.dma("sp", dict(out=abT[:, 1, h, 0:N].rearrange("p (t c) -> p t c", c=128),
                                     in_=b_own_scr.ap()[h, o * 4:o * 4 + NT].rearrange("t p c -> p t c")),
                          writes=[bf("abT_b%d" % h)])
                    abbufs[1].append(bf("abT_b%d" % h))
            for g in range(4):
                sl = gi % 2
                gi += 1
                ws = wslot[sl]
                wgA = ws[:, 0:8192].rearrange("p (k c) -> p k c", c=512)
                wgB = ws[:, 8192:16384].rearrange("p (k c) -> p k c", c=512)
                wbA = ws[:, 16384:20480].rearrange("p (h c) -> p h c", c=512)
                wbB = ws[:, 20480:24576].rearrange("p (h c) -> p h c", c=512)
                P.dma("pool", dict(out=wgA, in_=w_gate[:, g * 512:(g + 1) * 512].rearrange("(k p) c -> p k c", p=128)),
                      writes=[bf("ws%d_A" % sl)])
                P.dma("pool", dict(out=wgB, in_=w_gate[:, D + g * 512:D + (g + 1) * 512].rearrange("(k p) c -> p k c", p=128)),
                      writes=[bf("ws%d_B" % sl)])
                P.dma("pool", dict(out=wbA, in_=w_bsb[:, g * 512:(g + 1) * 512].rearrange("(h p) c -> p h c", p=128)),
                      writes=[bf("ws%d_C" % sl)])
                P.dma("pool", dict(out=wbB, in_=w_bhg[:, g * 512:(g + 1) * 512].rearrange("(h p) c -> p h c", p=128)),
                      writes=[bf("ws%d_D" % sl)])
                for cc in range(4):
                    jc = g * 4 + cc
                    for (wg, wgbuf, pb, sg, sgbuf) in ((wgA, bf("ws%d_A" % sl), 2, sgA, bf("sgA")), (wgB, bf("ws%d_B" % sl), 3, sgB, bf("sgB"))):
                        for k in range(KC):
                            P.op("pe", "matmul", dict(out=ps[pb][:, 0:N], lhsT=wg[:, k, cc * 128:(cc + 1) * 128], rhs=hT[:, k, 0:N],
                                                      start=(k == 0), stop=(k == KC - 1)),
                                 reads=[wgbuf] + hTb[k], writes=[psb[pb]])
                        P.op("act", "activation", dict(out=sg[:, 0:N], in_=ps[pb][:, 0:N], func=AF.Sigmoid),
                             reads=[psb[pb]], writes=[sgbuf])
                    for (wbr, wbbuf, pb, ab) in ((wbA, bf("ws%d_C" % sl), 4, 0), (wbB, bf("ws%d_D" % sl), 5, 1)):
                        for h in range(8):
                            P.op("pe", "matmul", dict(out=ps[pb][:, 0:N], lhsT=wbr[:, h, cc * 128:(cc + 1) * 128], rhs=abT[:, ab, h, 0:N],
                                                      start=(h == 0), stop=(h == 7)),
                                 reads=[wbbuf] + abbufs[ab], writes=[psb[pb]])
                    P.op("dve", "tensor_tensor", dict(out=tmp1[:, 0:N], in0=ps[4][:, 0:N], in1=sgA[:, 0:N], op=ALU.mult),
                         reads=[psb[4], bf("sgA")], writes=[bf("tmp1")])
                    P.op("dve", "tensor_tensor", dict(out=tmp2[:, 0:N], in0=ps[5][:, 0:N], in1=sgB[:, 0:N], op=ALU.mult),
                         reads=[psb[5], bf("sgB")], writes=[bf("tmp2")])
                    P.op("pool", "tensor_tensor", dict(out=mT[:, jc, 0:N], in0=tmp1[:, 0:N], in1=tmp2[:, 0:N], op=ALU.add),
                         reads=[bf("tmp1"), bf("tmp2")], writes=[bf("mT%d" % jc)])
            for cg in range(4):
                sl = gi % 2
                gi += 1
                ws = wslot[sl]
                wo = ws[:, 0:8192].rearrange("p (k c) -> p k c", c=512)
                P.dma("pool", dict(out=wo, in_=w_o[:, cg * 512:(cg + 1) * 512].rearrange("(k p) c -> p k c", p=128)),
                      writes=[bf("ws%d_A" % sl)])
                for tt in range(NT):
                    pb = 6 + (yi % 2)
                    ys = ysl[yi % 2]
                    xs = xsl[yi % 2]
                    ysb = bf("ysl%d" % (yi % 2))
                    xsb = bf("xsl%d" % (yi % 2))
                    yi += 1
                    r0 = o * CH + tt * 128
                    P.dma("sp", dict(out=xs[:], in_=x_own[r0:r0 + 128, cg * 512:(cg + 1) * 512]), writes=[xsb])
                    for k in range(KC):
                        P.op("pe", "matmul", dict(out=ps[pb][:, :], lhsT=mT[:, k, tt * 128:(tt + 1) * 128], rhs=wo[:, k, :],
                                                  start=(k == 0), stop=(k == KC - 1)),
                             reads=[bf("ws%d_A" % sl), bf("mT%d" % k)], writes=[psb[pb]])
                    gt = gateP if o < 4 else gateS
                    which = 0 if o < 4 else 1
                    P.op("dve", "tensor_tensor", dict(out=ys[:], in0=ps[pb][:, :], in1=gt[:, cg * 512:(cg + 1) * 512], op=ALU.mult),
                         reads=[psb[pb], bf("gate%d_%d" % (which, cg))], writes=[ysb])
                    P.op("pool", "tensor_tensor", dict(out=ys[:], in0=ys[:], in1=xs[:], op=ALU.add),
                         reads=[ysb, xsb], writes=[ysb])
                    P.dma("sp", dict(out=y_own[r0:r0 + 128, cg * 512:(cg + 1) * 512], in_=ys[:]), reads=[ysb])

    P.finish()
    P.emit(nc, es)
    if not do_out:
        es1.close()
    es.close()
    return nc


def _consts():
    j = np.arange(128)[:, None]
    k = np.arange(128)[None, :]
    trin = np.where(j >= k, -1.0, 0.0).astype(np.float32)
    tric = np.where(j < k, -1.0, 0.0).astype(np.float32)
    q = np.arange(512)[None, None, :]
    kk = np.arange(128)[:, None, None]
    i = np.arange(4)[None, :, None]
    maskp = np.where(q > kk + 128 * i, 0.0, -30000.0).astype(np.float32)
    qs = np.arange(64)[None, :]
    ks = np.arange(128)[:, None]
    m_even = np.where((ks < 64) & (qs > ks), 0.0, -30000.0)
    m_odd = np.where((ks >= 64) & (qs > ks - 64), 0.0, -30000.0)
    masks = np.stack([m_even, m_odd], axis=1).astype(np.float32)
    s_ = np.arange(128)[:, None]
    t_ = np.arange(128)[None, :]
    hmask = ((s_ // 64 == t_ // 64) & (s_ <= t_)).astype(np.float32)
    resetm = np.ones((128, 512), np.float32)
    resetm[:, ::64] = 0.0
    return {"identf": np.eye(128, dtype=np.float32), "trin": trin, "tric": tric, "maskp": maskp, "masks": masks,
            "hmask": hmask, "resetm": resetm,
            "pmask": np.stack([(np.arange(128) < 64), (np.arange(128) >= 64)], axis=1).astype(np.float32)}


def _f32(a):
    return np.ascontiguousarray(np.asarray(a, dtype=np.float32))


def make_in_maps(inp, cores=None, x_rows=TT, x_base=0):
    f32 = _f32
    x_all = f32(np.concatenate([np.asarray(inp["x_prompt"]).reshape(TP, D), np.asarray(inp["x_sample"]).reshape(TS, D)], axis=0))[x_base:x_base + x_rows]
    c_all = f32(np.concatenate([np.asarray(inp["c_prompt"]), np.asarray(inp["c_sample"])], axis=0))
    w_ada0 = f32(np.asarray(inp["w_ada"])[0])
    b_adaT = f32(np.asarray(inp["b_ada"])[0].reshape(48, 128).T)
    ngT = f32(np.asarray(inp["norm_gain"])[0].reshape(KC, 128).T)
    w_in0 = np.asarray(inp["w_in"])[0]
    qkg = f32(np.stack([np.asarray(inp["q_norm_gain"])[0], np.asarray(inp["k_norm_gain"])[0]], axis=1))
    consts = _consts()
    in_maps = []
    for c in (range(NCORES) if cores is None else cores):
        cols = np.concatenate([np.arange(j * 1024 + c * 128, j * 1024 + (c + 1) * 128) for j in range(8)])
        lbr = f32(np.asarray(inp["hgrn_lb_raw"])[:, c * 128:(c + 1) * 128].T)
        m = {"x_all": x_all, "c_all": c_all, "w_ada": w_ada0, "b_adaT": b_adaT, "ngT": ngT,
             "w_head": f32(w_in0[:, cols])[None], "qkg": qkg, "lbr": lbr[None],
             "ogain": f32(np.asarray(inp["hgrn_onorm_gain"])[0, c, :].reshape(1, 128, 1)),
             "cache_k": f32(np.asarray(inp["cache_sb_k"])[0, :, :, c, :])[None],
             "cache_v": f32(np.asarray(inp["cache_sb_v"])[0, :, :, c, :])[None],
             "s_in": f32(np.asarray(inp["state_hgrn"])[0, :, c])[None]}
        m.update(consts)
        in_maps.append(m)
    return in_maps


def make_out_maps(inp, a_all, b_all, cores=None):
    f32 = _f32
    xp = np.asarray(inp["x_prompt"]).reshape(TP, D)
    xs = np.asarray(inp["x_sample"]).reshape(TS, D)
    cp = np.asarray(inp["c_prompt"])
    cs = np.asarray(inp["c_sample"])
    w_ada0 = f32(np.asarray(inp["w_ada"])[0])
    b_ada0 = np.asarray(inp["b_ada"])[0]
    b_adaT = f32(b_ada0.reshape(48, 128).T)
    ngT = f32(np.asarray(inp["norm_gain"])[0].reshape(KC, 128).T)
    w_gate = f32(np.asarray(inp["w_in"])[0][:, 8192:])
    w_bsb = f32(np.asarray(inp["w_branch_sb"])[0])
    w_bhg = f32(np.asarray(inp["w_branch_hgrn"])[0])
    w_o = f32(np.asarray(inp["w_out"])[0])
    b_gate_bc = f32(np.broadcast_to(b_ada0[2 * D:][None, :], (128, D)))
    maps = []
    for c in (range(NCORES) if cores is None else cores):
        ptok = np.concatenate([own_chunk(c, s_) * CH + np.arange(CH) for s_ in range(4)])
        tok = np.concatenate([ptok, TP + np.arange(c * OWN_S, (c + 1) * OWN_S)])
        m = {"c_all": f32(np.concatenate([cp, cs[2 * c:2 * c + 2]], axis=0)), "w_ada": w_ada0, "b_adaT": b_adaT, "ngT": ngT,
             "identf": np.eye(128, dtype=np.float32),
             "x_own": f32(np.concatenate([xp[ptok], xs[c * OWN_S:(c + 1) * OWN_S]], axis=0)),
             "w_gate": w_gate, "w_bsb": w_bsb, "w_bhg": w_bhg, "w_o": w_o, "b_gate_bc": b_gate_bc,
             }
        if a_all is not None:
            m["ab_own"] = f32(np.stack([a_all[:, :, tok], b_all[:, :, tok]], axis=0))
        maps.append(m)
    return maps


def make_fused_maps(inp, cores=None, x_rows=TT, x_base=0):
    f32 = _f32
    base = make_out_maps(inp, None, None, cores=cores)
    x_all = f32(np.concatenate([np.asarray(inp["x_prompt"]).reshape(TP, D), np.asarray(inp["x_sample"]).reshape(TS, D)], axis=0))[x_base:x_base + x_rows]
    cp = np.asarray(inp["c_prompt"])
    cs = np.asarray(inp["c_sample"])
    w_in0 = np.asarray(inp["w_in"])[0]
    w_head = f32(np.stack([w_in0[:, np.concatenate([np.arange(j * 1024 + h * 128, j * 1024 + (h + 1) * 128) for j in range(8)])]
                           for h in range(8)], axis=0))
    qkg = f32(np.stack([np.asarray(inp["q_norm_gain"])[0], np.asarray(inp["k_norm_gain"])[0]], axis=1))
    lbr = f32(np.asarray(inp["hgrn_lb_raw"]).reshape(2, 8, 128).transpose(1, 2, 0))
    ogain = f32(np.asarray(inp["hgrn_onorm_gain"])[0].reshape(8, 128, 1))
    ck = np.asarray(inp["cache_sb_k"])[0]
    cv = np.asarray(inp["cache_sb_v"])[0]
    s_in = f32(np.asarray(inp["state_hgrn"])[0].transpose(1, 0, 2, 3))
    consts = _consts()
    maps = []
    for n, c in enumerate(range(NCORES) if cores is None else cores):
        m = dict(base[n])
        m.pop("ab_own", None)
        m["c_all"] = f32(np.concatenate([cp, cs, cs[2 * c:2 * c + 2]], axis=0))
        idx_s = np.stack([own_chunk(c, s_) * 128 + np.arange(128) for s_ in range(4)], axis=1).astype(np.int32)
        vis = np.full((128, 512), -30000.0, np.float32)
        for s_ in range(4):
            vis[:, s_ * 128:s_ * 128 + 4 * own_chunk(c, s_)] = 0.0
        m.update({"x_all": x_all, "w_head": w_head, "qkg": qkg, "lbr": lbr, "ogain": ogain,
                  "cache_k": f32(ck[2 * c:2 * c + 2].transpose(2, 0, 1, 3)), "cache_v": f32(cv[2 * c:2 * c + 2].transpose(2, 0, 1, 3)),
                  "s_in": s_in, "idx_s": idx_s, "vis": vis,
                  "s_in_own": f32(s_in[:, 2 * c:2 * c + 2])})
        m.update(consts)
        maps.append(m)
    return maps


def kernel(**inp):
    nc = build(mode="fused")
    res = run_bass_kernel_spmd(nc, make_fused_maps(inp), core_ids=list(range(NCORES)))
    r = res.results
    kall = r[0]["k_out"].reshape(8, TT, 128).transpose(1, 0, 2)
    vall = r[0]["v_out"].reshape(8, TT, 128).transpose(1, 0, 2)
    sall = r[0]["s_out"].reshape(8, 17, 128, 128).transpose(1, 0, 2, 3)
    y = [r[c]["y_own"] for c in range(NCORES)]
    y_prompt = np.zeros((TP, D), np.float32)
    for c in range(NCORES):
        for s_ in range(4):
            y_prompt[own_chunk(c, s_) * CH:(own_chunk(c, s_) + 1) * CH] = y[c][s_ * CH:(s_ + 1) * CH]
    y_prompt = y_prompt.reshape(1, TP, D)
    y_sample = np.concatenate([yy[OWN_P:] for yy in y], axis=0).reshape(NB, TSQ, D)
    new_k_prompt = kall[:TP].reshape(1, 1, TP, 8, 128)
    new_v_prompt = vall[:TP].reshape(1, 1, TP, 8, 128)
    new_k_sample = kall[TP:].reshape(1, NB, TSQ, 8, 128)
    new_v_sample = vall[TP:].reshape(1, NB, TSQ, 8, 128)
    new_s_prompt = sall[0].reshape(1, 1, 8, 128, 128)
    new_s_sample = sall[1:].reshape(1, NB, 8, 128, 128)
    c32 = lambda a: np.ascontiguousarray(a, dtype=np.float32)
    return (c32(y_prompt), c32(y_sample), c32(new_k_prompt), c32(new_v_prompt), c32(new_s_prompt),
            c32(new_k_sample), c32(new_v_sample), c32(new_s_sample))
```
